# Optimizing a Trainium2 kernel written in Bass

```python
import math
import jax
import jax.numpy as jnp
from jax import lax
import numpy as np

D_MODEL = 1024
BATCH = 8
SEQ = 2048
DEPTH = 2
DEC_BATCH = 128
DEC_SEQ = 8
PAST_LEN = 16384
PAGE_SIZE = 128

A_HEADS = 16
A_KV_HEADS = 2
A_HEAD_DIM = 64
A_GROUP = A_HEADS // A_KV_HEADS
WINDOW = 128
ROT_DIM = A_HEAD_DIM // 4
ROPE_THETA = 500000.0
NEG_INF = -1e30
B_HEADS = 4
B_DK = D_MODEL // 2 // B_HEADS
B_DV = D_MODEL // B_HEADS
B_GATE_RANK = 16
B_TAU = 16.0
GLA_CHUNK = 64
POOL_WINDOWS = (2, 4, 8, 16)
C_GROUPS = len(POOL_WINDOWS)
C_GROUP_W = D_MODEL // C_GROUPS
C_WIDTH = C_GROUPS * C_GROUP_W
POOL_STATE = max(POOL_WINDOWS) - 1
PEER_HEADS = 8
N_KEYS = 128
N_EXPERTS = N_KEYS * N_KEYS
PEER_TOPK = 16
PEER_DKEY = 256
PEER_DHALF = PEER_DKEY // 2
PEER_BLOCK = 128
DN_ALPHA = (2 * DEPTH) ** 0.25
DN_BETA = (8 * DEPTH) ** -0.25
LN_EPS = 1e-5
RMS_EPS = 1e-6

A_Q = A_HEADS * A_HEAD_DIM
A_KV = A_KV_HEADS * A_HEAD_DIM
B_QK = B_HEADS * B_DK
B_V = B_HEADS * B_DV
SPLITS = (A_Q, A_KV, A_KV, B_QK, B_QK, B_V, B_GATE_RANK, B_V, C_WIDTH, 3 * D_MODEL)
IN_WIDTH = sum(SPLITS)

kernel_name = 'hybrid_swa_gla_pool_peer_step'


def _layer_norm(x, g, b):
    xf = x.astype(jnp.float32)
    mu = jnp.mean(xf, -1, keepdims=True)
    var = jnp.mean(jnp.square(xf - mu), -1, keepdims=True)
    y = (xf - mu) * lax.rsqrt(var + LN_EPS) * g.astype(jnp.float32) + b.astype(jnp.float32)
    return y.astype(x.dtype)


def _partial_rope(x, pos):
    half = ROT_DIM // 2
    inv = ROPE_THETA ** (-jnp.arange(half, dtype=jnp.float32) / half)
    ang = pos.astype(jnp.float32)[:, None] * inv[None, :]
    cos = jnp.cos(ang)[:, None, :]
    sin = jnp.sin(ang)[:, None, :]
    xr = x[..., :ROT_DIM].astype(jnp.float32)
    x1, x2 = xr[..., :half], xr[..., half:]
    rot = jnp.concatenate([x1 * cos - x2 * sin, x2 * cos + x1 * sin], -1).astype(x.dtype)
    return jnp.concatenate([rot, x[..., ROT_DIM:]], -1)


def _sink_attention(q, k, v, q_pos, k_pos, sinks):
    s = jnp.einsum('bnqhgd,bnkhd->bnhgqk', q, k, preferred_element_type=jnp.float32) * (A_HEAD_DIM ** -0.5)
    kp = k_pos[:, None, :]
    qp = q_pos[:, :, None]
    ok = (kp <= qp) & (kp > qp - WINDOW) & (kp >= 0)
    s = jnp.where(ok[None, :, None, None], s, NEG_INF)
    sink = jnp.broadcast_to(sinks.astype(jnp.float32).reshape(1, 1, A_KV_HEADS, A_GROUP, 1, 1), s.shape[:-1] + (1,))
    p = jax.nn.softmax(jnp.concatenate([s, sink], -1), axis=-1)[..., :-1]
    return jnp.einsum('bnhgqk,bnkhd->bnqhgd', p.astype(v.dtype), v)


def _attn_prompt(q, k, v, sinks, buf_len):
    Bx, T = q.shape[:2]
    nb = T // WINDOW
    qb = q.reshape(Bx, nb, WINDOW, A_KV_HEADS, A_GROUP, A_HEAD_DIM)

    def banded(t):
        tb = t.reshape(Bx, nb, WINDOW, A_KV_HEADS, A_HEAD_DIM)
        prev = jnp.concatenate([jnp.zeros_like(tb[:, :1]), tb[:, :-1]], 1)
        return jnp.concatenate([prev, tb], 2)

    qpos = jnp.arange(T).reshape(nb, WINDOW)
    kpos = jnp.concatenate([qpos - WINDOW, qpos], 1)
    o = _sink_attention(qb, banded(k), banded(v), qpos, kpos, sinks)
    return o.reshape(Bx, T, A_Q), k[:, T - buf_len:], v[:, T - buf_len:]


def _attn_sample(q, k, v, sinks, k_buf, v_buf, pos0):
    Bx, T = q.shape[:2]
    L = k_buf.shape[1]
    kk = jnp.concatenate([k_buf, k], 1)
    vv = jnp.concatenate([v_buf, v], 1)
    qpos = (pos0 + jnp.arange(T))[None]
    kpos = (pos0 - L + jnp.arange(L + T))[None]
    qb = q.reshape(Bx, 1, T, A_KV_HEADS, A_GROUP, A_HEAD_DIM)
    o = _sink_attention(qb, kk[:, None], vv[:, None], qpos, kpos, sinks)
    return o.reshape(Bx, T, A_Q), kk[:, -L:], vv[:, -L:]


def _gla_chunked(q, k, v, log_a, s0):
    Bx, T, H, _ = q.shape
    c = math.gcd(T, GLA_CHUNK)
    n = T // c

    def to_chunks(t):
        return jnp.moveaxis(t.astype(jnp.float32).reshape(Bx, n, c, H, t.shape[-1]), 1, 0)

    causal = jnp.tril(jnp.ones((c, c), dtype=bool))

    def step(S, inp):
        qi, ki, vi, ai = inp
        b = jnp.cumsum(ai, axis=1)
        qd = qi * jnp.exp(b)
        kd = ki * jnp.exp(-b)
        o = jnp.einsum('bchk,bhkv->bchv', qd, S)
        att = jnp.where(causal, jnp.einsum('bihk,bjhk->bhij', qd, kd), 0.0)
        o = o + jnp.einsum('bhij,bjhv->bihv', att, vi)
        bl = b[:, -1]
        kl = ki * jnp.exp(bl[:, None] - b)
        S = jnp.exp(bl)[..., None] * S + jnp.einsum('bchk,bchv->bhkv', kl, vi)
        return S, o

    S, o = lax.scan(step, s0.astype(jnp.float32), (to_chunks(q), to_chunks(k), to_chunks(v), to_chunks(log_a)))
    o = jnp.moveaxis(o, 0, 1).reshape(Bx, T, H, v.shape[-1])
    return o, S.astype(s0.dtype)


def _pool_mix(u, prev, pos0, w_pool, pool_scale):
    Bx, T, _ = u.shape
    full = jnp.concatenate([prev.astype(u.dtype), u], 1)
    cs = jnp.cumsum(full.astype(jnp.float32), axis=1)
    cs = jnp.concatenate([jnp.zeros((Bx, 1, C_WIDTH), jnp.float32), cs], 1)
    end = POOL_STATE + jnp.arange(T) + 1
    pos = pos0 + jnp.arange(T)
    means = []
    for g, w in enumerate(POOL_WINDOWS):
        sl = slice(g * C_GROUP_W, (g + 1) * C_GROUP_W)
        wsum = cs[:, end, sl] - cs[:, end - w, sl]
        cnt = jnp.minimum(pos + 1, w).astype(jnp.float32)
        means.append(wsum / cnt[None, :, None])
    d = (jnp.concatenate(means, -1) - u.astype(jnp.float32)).astype(u.dtype)
    d = d.reshape(Bx, T, C_GROUPS, C_GROUP_W)
    y = jnp.einsum('btgc,gce->btge', d, w_pool).reshape(Bx, T, C_WIDTH) * pool_scale
    return y, full[:, -POOL_STATE:]


def _peer_ffn(x, w_query, sub_keys, u_tab, v_tab):
    Bx, T, D = x.shape
    n_tok = Bx * T
    pad = (-n_tok) % PEER_BLOCK
    xb = jnp.pad(x.reshape(n_tok, D), ((0, pad), (0, 0))).reshape(-1, PEER_BLOCK, D)

    def block(xi):
        q = jnp.einsum('td,dhk->thk', xi, w_query).reshape(PEER_BLOCK, PEER_HEADS, 2, PEER_DHALF)
        s = jnp.einsum('thpk,hpnk->thpn', q, sub_keys, preferred_element_type=jnp.float32)
        sv, si = lax.top_k(s, PEER_TOPK)
        cand = sv[:, :, 0, :, None] + sv[:, :, 1, None, :]
        cidx = si[:, :, 0, :, None] * N_KEYS + si[:, :, 1, None, :]
        cscore, ci = lax.top_k(cand.reshape(PEER_BLOCK, PEER_HEADS, PEER_TOPK * PEER_TOPK), PEER_TOPK)
        eidx = jnp.take_along_axis(cidx.reshape(PEER_BLOCK, PEER_HEADS, PEER_TOPK * PEER_TOPK), ci, axis=-1)
        g = jax.nn.softmax(cscore, axis=-1)
        u = jnp.take(u_tab, eidx, axis=0)
        h = jax.nn.gelu(jnp.einsum('thkd,td->thk', u, xi, preferred_element_type=jnp.float32), approximate=False)
        v = jnp.take(v_tab, eidx, axis=0)
        return jnp.einsum('thk,thkd->td', (g * h).astype(v.dtype), v)

    y = lax.map(block, xb).reshape(-1, D)[:n_tok]
    return y.reshape(Bx, T, D)


def _trunk_layer(x, win_k, win_v, gla_s, pool_prev, pos0, buf_len,
                 w_in, b_in, attn_sinks, w_alpha, b_alpha, gla_norm_g, w_pool, pool_scale,
                 w_branch_a, w_branch_b, w_branch_c, w_out, ln1_g, ln1_b,
                 peer_query, peer_subkeys, peer_u, peer_v, ln2_g, ln2_b):
    Bx, T, _ = x.shape
    pos = pos0 + jnp.arange(T)
    proj = jnp.einsum('btd,de->bte', x, w_in) + b_in
    cuts = [int(c) for c in np.cumsum(SPLITS)[:-1]]
    qa, ka, va, qb, kb, vb, lr, gb, uc, gates = jnp.split(proj, cuts, axis=-1)

    qa = _partial_rope(qa.reshape(Bx, T, A_HEADS, A_HEAD_DIM), pos)
    ka = _partial_rope(ka.reshape(Bx, T, A_KV_HEADS, A_HEAD_DIM), pos)
    va = va.reshape(Bx, T, A_KV_HEADS, A_HEAD_DIM)
    if win_k is None:
        oa, nk, nv = _attn_prompt(qa, ka, va, attn_sinks, buf_len)
    else:
        oa, nk, nv = _attn_sample(qa, ka, va, attn_sinks, win_k, win_v, pos0)

    log_a = jax.nn.log_sigmoid(jnp.einsum('btr,rk->btk', lr, w_alpha).astype(jnp.float32) + b_alpha.astype(jnp.float32)) / B_TAU
    s0 = jnp.zeros((Bx, B_HEADS, B_DK, B_DV), x.dtype) if gla_s is None else gla_s
    ob, ns = _gla_chunked(qb.reshape(Bx, T, B_HEADS, B_DK) * (B_DK ** -0.5), kb.reshape(Bx, T, B_HEADS, B_DK),
                          vb.reshape(Bx, T, B_HEADS, B_DV), log_a.reshape(Bx, T, B_HEADS, B_DK), s0)
    ob = ob * lax.rsqrt(jnp.mean(jnp.square(ob), -1, keepdims=True) + RMS_EPS) * gla_norm_g.astype(jnp.float32)
    ob = (ob.astype(x.dtype) * jax.nn.silu(gb.reshape(Bx, T, B_HEADS, B_DV))).reshape(Bx, T, B_V)

    prev = jnp.zeros((Bx, POOL_STATE, C_WIDTH), x.dtype) if pool_prev is None else pool_prev
    oc, npv = _pool_mix(uc, prev, pos0, w_pool, pool_scale)

    ga, gbr, gc = jnp.split(jax.nn.sigmoid(gates), 3, axis=-1)
    merged = (ga * jnp.einsum('bti,id->btd', oa, w_branch_a)
              + gbr * jnp.einsum('bti,id->btd', ob, w_branch_b)
              + gc * jnp.einsum('bti,id->btd', oc, w_branch_c))
    mix = jnp.einsum('btd,de->bte', merged, w_out)
    x = _layer_norm(DN_ALPHA * x + mix, ln1_g, ln1_b)

    x = _layer_norm(DN_ALPHA * x + _peer_ffn(x, peer_query, peer_subkeys, peer_u, peer_v), ln2_g, ln2_b)
    return x, nk, nv, ns, npv


def setup_inputs(seed: int = 0) -> dict:
    key = jax.random.key(seed)
    ks = jax.random.split(key, 26)
    f32 = jnp.float32

    def nrm(i, shape, scale):
        return jax.random.normal(ks[i], shape, f32) * scale

    wb = min(WINDOW, PAST_LEN)
    return {
        'x_prompt': nrm(0, (BATCH, SEQ, D_MODEL), 1.0),
        'x_sample': nrm(1, (DEC_BATCH, DEC_SEQ, D_MODEL), 1.0),
        'state_win_k': nrm(2, (DEPTH, DEC_BATCH, wb, A_KV_HEADS, A_HEAD_DIM), 1.0),
        'state_win_v': nrm(3, (DEPTH, DEC_BATCH, wb, A_KV_HEADS, A_HEAD_DIM), 1.0),
        'state_gla': nrm(4, (DEPTH, DEC_BATCH, B_HEADS, B_DK, B_DV), B_DK ** -0.5),
        'state_pool': nrm(5, (DEPTH, DEC_BATCH, POOL_STATE, C_WIDTH), 1.0),
        'w_in': nrm(6, (DEPTH, D_MODEL, IN_WIDTH), D_MODEL ** -0.5),
        'b_in': nrm(7, (DEPTH, IN_WIDTH), 0.02),
        'attn_sinks': nrm(8, (DEPTH, A_HEADS), 0.5),
        'w_alpha': nrm(9, (DEPTH, B_GATE_RANK, B_QK), B_GATE_RANK ** -0.5),
        'b_alpha': nrm(10, (DEPTH, B_QK), 0.1),
        'gla_norm_g': 1.0 + nrm(11, (DEPTH, B_HEADS, B_DV), 0.02),
        'w_pool': nrm(12, (DEPTH, C_GROUPS, C_GROUP_W, C_GROUP_W), C_GROUP_W ** -0.5),
        'pool_scale': 1.0 + nrm(13, (DEPTH, C_WIDTH), 0.05),
        'w_branch_a': nrm(14, (DEPTH, A_Q, D_MODEL), A_Q ** -0.5),
        'w_branch_b': nrm(15, (DEPTH, B_V, D_MODEL), B_V ** -0.5),
        'w_branch_c': nrm(16, (DEPTH, C_WIDTH, D_MODEL), C_WIDTH ** -0.5),
        'w_out': nrm(17, (DEPTH, D_MODEL, D_MODEL), DN_BETA * D_MODEL ** -0.5),
        'ln1_g': 1.0 + nrm(18, (DEPTH, D_MODEL), 0.01),
        'ln1_b': nrm(19, (DEPTH, D_MODEL), 0.01),
        'peer_query': nrm(20, (DEPTH, D_MODEL, PEER_HEADS, PEER_DKEY), D_MODEL ** -0.5),
        'peer_subkeys': nrm(21, (DEPTH, PEER_HEADS, 2, N_KEYS, PEER_DHALF), PEER_DHALF ** -0.5),
        'peer_u': nrm(22, (DEPTH, N_EXPERTS, D_MODEL), D_MODEL ** -0.5),
        'peer_v': nrm(23, (DEPTH, N_EXPERTS, D_MODEL), DN_BETA * PEER_HEADS ** -0.5),
        'ln2_g': 1.0 + nrm(24, (DEPTH, D_MODEL), 0.01),
        'ln2_b': nrm(25, (DEPTH, D_MODEL), 0.01),
    }


def reference(x_prompt, x_sample, state_win_k, state_win_v, state_gla, state_pool,
              w_in, b_in, attn_sinks, w_alpha, b_alpha, gla_norm_g, w_pool, pool_scale,
              w_branch_a, w_branch_b, w_branch_c, w_out, ln1_g, ln1_b,
              peer_query, peer_subkeys, peer_u, peer_v, ln2_g, ln2_b):
    buf_len = state_win_k.shape[2]
    yp, ys = x_prompt, x_sample
    pk, pv, pg, pp = [], [], [], []
    sk, sv, sg, sp = [], [], [], []
    for l in range(DEPTH):
        w = (w_in[l], b_in[l], attn_sinks[l], w_alpha[l], b_alpha[l], gla_norm_g[l], w_pool[l], pool_scale[l],
             w_branch_a[l], w_branch_b[l], w_branch_c[l], w_out[l], ln1_g[l], ln1_b[l],
             peer_query[l], peer_subkeys[l], peer_u[l], peer_v[l], ln2_g[l], ln2_b[l])
        yp, k1, v1, g1, q1 = _trunk_layer(yp, None, None, None, None, 0, buf_len, *w)
        ys, k2, v2, g2, q2 = _trunk_layer(ys, state_win_k[l], state_win_v[l], state_gla[l], state_pool[l],
                                          PAST_LEN, buf_len, *w)
        pk.append(k1); pv.append(v1); pg.append(g1); pp.append(q1)
        sk.append(k2); sv.append(v2); sg.append(g2); sp.append(q2)
    return (yp, ys,
            jnp.stack(pk), jnp.stack(pv), jnp.stack(pg), jnp.stack(pp),
            jnp.stack(sk), jnp.stack(sv), jnp.stack(sg), jnp.stack(sp))
```

```python
import numpy as np
from contextlib import ExitStack
import concourse.bass as bass
import concourse.mybir as mybir
from concourse.bass_utils import run_bass_kernel_spmd

F32 = mybir.dt.float32
BF16 = mybir.dt.bfloat16
I32 = mybir.dt.int32
U32 = mybir.dt.uint32
ACT = mybir.ActivationFunctionType
ALU = mybir.AluOpType
AX = mybir.AxisListType

D = 1024
INW = 8464
NEG = -30000.0
ALPHA = 4.0 ** 0.25
C_QA, C_KA, C_VA, C_QB, C_KB, C_VB, C_LR, C_GB, C_UC, C_GT = 0, 1024, 1152, 1280, 1792, 2304, 3328, 3344, 4368, 5392
VF_BQ, VF_BQB, VF_BKB, VF_BGB, VF_BGT, VF_BLR, VF_GNG, VF_PSC, VF_L1G, VF_L1B, VF_L2G, VF_L2B, VF_N = 0, 18, 22, 26, 34, 58, 59, 67, 75, 83, 91, 99, 107


class Buf:
    def __init__(self, t, name):
        self.t = t
        self.name = name
        self.w = None
        self.r = []

    def __getitem__(self, k):
        return self.t[k]


class KB:
    EPOCH = 3000

    def __init__(self, n_dma_sems=32):
        self.nc = bass.Bass("TRN2", target_bir_lowering=False)
        nc = self.nc
        self.es = None
        self.eng = {"pe": nc.tensor, "act": nc.scalar, "dve": nc.vector, "pool": nc.gpsimd, "sp": nc.sync}
        self.sems = {}
        self.cur = {}
        self.known = {e: {} for e in self.eng}
        self.n_dma_sems = n_dma_sems
        self.dma_rr = 0
        self.dma_tot = {}
        self.nsem = 0
        self.ninstr = 0

    def start(self, es):
        self.es = es
        for e in ("pe", "act", "dve", "pool"):
            self._new_epoch(e)
        for i in range(self.n_dma_sems):
            k = ("dma", i)
            self.sems[k] = es.enter_context(self.nc.semaphore("dsem%d" % i))
            self.dma_tot[k] = 0

    def _new_epoch(self, e):
        self.nsem += 1
        k = (e, self.nsem)
        self.sems[k] = self.es.enter_context(self.nc.semaphore("s_%s_%d" % (e, self.nsem)))
        self.cur[e] = [k, 0]

    def sb(self, name, shape, dtype, es=None):
        self.nalloc = getattr(self, "nalloc", 0) + 1
        name = "sb%d_%s" % (self.nalloc, name)
        t = (es or self.es).enter_context(self.nc.sbuf_tensor(name, list(shape), dtype))
        return Buf(t, name)

    def ps(self, name, shape, dtype=F32, es=None):
        t = (es or self.es).enter_context(self.nc.psum_tensor(name, list(shape), dtype))
        return Buf(t, name)

    def dram(self, name, shape, dtype, kind):
        t = self.nc.dram_tensor(name, list(shape), dtype, kind=kind)
        return Buf(t.ap(), name)

    def _wait(self, e, dep):
        k, v = dep
        kn = self.known[e]
        if kn.get(k, 0) >= v:
            return
        self.eng[e].wait_ge(self.sems[k], v)
        kn[k] = v

    FUSE_WAIT = True

    def _deps(self, e, reads, writes, fuse=False):
        m = {}
        for b in reads:
            if b.w is not None:
                k, v = b.w
                if m.get(k, 0) < v:
                    m[k] = v
        for b in writes:
            if b.w is not None:
                k, v = b.w
                if m.get(k, 0) < v:
                    m[k] = v
            for k, v in b.r:
                if m.get(k, 0) < v:
                    m[k] = v
        pend = []
        kn = self.known[e]
        for k, v in m.items():
            if e == "pe" and k[0] == "pe":
                continue
            if kn.get(k, 0) >= v:
                continue
            pend.append((k, v))
        if fuse and pend:
            for dep in pend[:-1]:
                self._wait(e, dep)
            return pend[-1]
        for dep in pend:
            self._wait(e, dep)
        return None

    def op(self, e, fn, reads=(), writes=()):
        last = self._deps(e, reads, writes, fuse=self.FUSE_WAIT)
        ins = fn()
        if last is not None:
            ins._wait_ge(self.sems[last[0]], last[1])
            self.known[e][last[0]] = last[1]
        k, c = self.cur[e]
        c += 1
        ins.then_inc(self.sems[k], 1)
        self.cur[e][1] = c
        tag = (k, c)
        for b in reads:
            b.r.append(tag)
            if len(b.r) > 64:
                b.r = self._compact(b.r)
        for b in writes:
            b.w = tag
            b.r = []
        self.ninstr += 1
        if c >= self.EPOCH:
            self._new_epoch(e)
        return ins

    @staticmethod
    def _compact(r):
        m = {}
        for k, v in r:
            if m.get(k, 0) < v:
                m[k] = v
        return list(m.items())

    def dma(self, q, fn, reads=(), writes=()):
        self._deps(q, reads, writes)
        k = ("dma", self.dma_rr)
        self.dma_rr = (self.dma_rr + 1) % self.n_dma_sems
        if self.dma_tot[k] > 0:
            self._wait(q, (k, self.dma_tot[k]))
        ins = fn()
        self.dma_tot[k] += 16
        ins.then_inc(self.sems[k], 16)
        tag = (k, self.dma_tot[k])
        for b in reads:
            b.r.append(tag)
        for b in writes:
            b.w = tag
            b.r = []
        self.ninstr += 1
        return tag

    def barrier(self):
        tags = []
        for e in ("pe", "act", "dve", "pool"):
            k, c = self.cur[e]
            if c > 0:
                tags.append((k, c))
        for k, v in self.dma_tot.items():
            if v > 0:
                tags.append((k, v))
        for e in self.eng:
            for t in tags:
                self._wait(e, t)


def make_consts(NTP):
    import ml_dtypes
    c = {}
    j = np.arange(128)[:, None]
    i = np.arange(128)[None, :]
    bj, tj = j // 8, j % 8
    bi, ti = i // 8, i % 8
    same = (bj == bi)

    def neg(ok):
        return np.where(ok, 0.0, NEG).astype(np.float32)

    att = np.zeros((4, 128, 512), np.float32)
    att[0] = np.tile(neg(j <= i), (1, 4))
    att[1] = np.tile(neg(j > i), (1, 4))
    att[2] = np.tile(neg(same & (tj <= ti)), (1, 4))
    t64 = (np.arange(64) % 8)[None, :]
    att[3] = np.tile(neg(j > t64), (1, 8))
    c["c_att"] = att.transpose(1, 0, 2).copy()
    gm = np.zeros((128, 2, 128), np.float32)
    gm[:, 0] = (j <= i)
    gm[:, 1] = same & (tj <= ti)
    c["c_gm"] = gm
    gu = np.zeros((128, 4, 128), np.float32)
    gu[:, 0] = np.where(j <= i, -1.0 / 16, 0.0)
    gu[:, 1] = np.where(j > i, -1.0 / 16, 0.0)
    gu[:, 2] = np.where(same & (tj <= ti), -1.0 / 16, 0.0)
    gu[:, 3] = np.where(same & (tj > ti), -1.0 / 16, 0.0)
    c["c_gu"] = gu
    ind = np.zeros((128, 16), np.float32)
    ind[np.arange(128), np.arange(128) // 8] = 1.0
    c["c_ind"] = ind
    pm = np.zeros((128, 6, 4, 128), np.float32)
    eye = (j == i).astype(np.float32)
    for g, w in enumerate((2, 4, 8, 16)):
        pm[:, 0, g] = np.where((j <= i) & (j > i - w), 1.0 / w, 0.0) - eye
        pm[:, 1, g] = np.where(j >= 129 + i - w, 1.0 / w, 0.0)
        cnt = np.minimum(i + 1, w).astype(np.float32)
        pm[:, 2, g] = np.where((j <= i) & (j > i - w), 1.0 / cnt, 0.0) - eye
        pm[:, 3, g] = np.where(same & (tj <= ti) & (tj > ti - w), 1.0 / w, 0.0) - eye
        rows = np.arange(240)[:, None]
        rb, rr = rows // 15, rows % 15
        mp = np.where((rb == bi) & (rr >= ti + 16 - w), 1.0 / w, 0.0)
        pm[:, 4, g] = mp[:128]
        pm[:112, 5, g] = mp[128:]
    c["c_pm"] = pm
    T = NTP * 128 + 128
    pos = np.concatenate([np.arange(NTP * 128), 16384 + (np.arange(128) % 8)]).astype(np.float32)
    inv = (np.float32(500000.0) ** (-np.arange(8, dtype=np.float32) / np.float32(8))).astype(np.float32)
    ang = (pos[None, :] * inv[:, None]).astype(np.float32)
    rc = np.ones((64, T), np.float32)
    rs = np.zeros((64, T), np.float32)
    rc[0:8] = np.cos(ang)
    rc[8:16] = np.cos(ang)
    rs[0:8] = -np.sin(ang)
    rs[8:16] = np.sin(ang)
    c["c_rope"] = np.stack([rc, rs], 1).copy()
    perm = np.zeros((64, 64), np.float32)
    for m in range(16):
        perm[(m + 8) % 16, m] = 1.0
    misc = np.zeros((128, 4, 128), np.float32)
    misc[:, 0] = np.eye(128)
    misc[:64, 1, :64] = perm
    misc[:, 2, :16] = np.arange(16)[None, :]
    misc[:, 3, :] = np.arange(128)[None, :]
    c["c_misc"] = misc
    blk = np.zeros((4, 512), np.float32)
    for k in range(4):
        blk[k, k * 128:(k + 1) * 128] = 1.0
    c["c_blk"] = blk
    return c


class Prog:
    def __init__(self, NTP=16, DEPTH=2, dbg=()):
        self.NTP = NTP
        self.NT = NTP + 1
        self.T = self.NT * 128
        self.DEPTH = DEPTH
        self.dbg = dbg
        self.kb = KB()
        self.nc = self.kb.nc
        self.dbg_outs = []
        self.phases = "ABCDE"
        self.peer_mode = "dense"

    def mm(self, out, lhsT, rhs, start, stop, reads, bank):
        nc = self.nc
        return self.kb.op("pe", lambda: nc.tensor.matmul(out, lhsT=lhsT, rhs=rhs, start=start, stop=stop), reads=reads, writes=[bank])

    def act(self, out, in_, func, reads, writes, bias=None, scale=1.0):
        nc = self.nc
        if bias is None:
            return self.kb.op("act", lambda: nc.scalar.activation(out=out, in_=in_, func=func, scale=scale), reads=reads, writes=writes)
        return self.kb.op("act", lambda: nc.scalar.activation(out=out, in_=in_, func=func, bias=bias, scale=scale), reads=reads, writes=writes)

    def tt(self, e, out, in0, in1, op, reads, writes):
        eng = self.kb.eng[e]
        return self.kb.op(e, lambda: eng.tensor_tensor(out=out, in0=in0, in1=in1, op=op), reads=reads, writes=writes)

    def ts(self, e, out, in0, s1, s2, op0, op1, reads, writes):
        eng = self.kb.eng[e]
        if op1 is None:
            return self.kb.op(e, lambda: eng.tensor_scalar(out=out, in0=in0, scalar1=s1, scalar2=None, op0=op0), reads=reads, writes=writes)
        return self.kb.op(e, lambda: eng.tensor_scalar(out=out, in0=in0, scalar1=s1, scalar2=s2, op0=op0, op1=op1), reads=reads, writes=writes)

    def stt(self, out, in0, scalar, in1, op0, op1, reads, writes, accum_out=None):
        nc = self.nc
        if accum_out is None:
            return self.kb.op("dve", lambda: nc.vector.scalar_tensor_tensor(out=out, in0=in0, scalar=scalar, in1=in1, op0=op0, op1=op1), reads=reads, writes=writes)
        return self.kb.op("dve", lambda: nc.vector.scalar_tensor_tensor(out=out, in0=in0, scalar=scalar, in1=in1, op0=op0, op1=op1, accum_out=accum_out), reads=reads, writes=writes)

    def cp(self, e, out, in_, reads, writes):
        if e == "act":
            nc = self.nc
            return self.kb.op("act", lambda: nc.scalar.copy(out=out, in_=in_), reads=reads, writes=writes)
        eng = self.kb.eng[e]
        return self.kb.op(e, lambda: eng.tensor_copy(out=out, in_=in_), reads=reads, writes=writes)

    def ld(self, q, out, in_, reads, writes):
        eng = self.kb.eng[q]
        return self.kb.dma(q, lambda: eng.dma_start(out=out, in_=in_), reads=reads, writes=writes)

    def bank(self):
        b = self.banks[self.bank_i]
        self.bank_i = (self.bank_i + 1) % 8
        return b

    def dump(self, name, buf, ap, shape, dtype=F32):
        d = self.kb.dram("dbg_" + name, list(shape), dtype, "ExternalOutput")
        self.ld("sp", d[:], ap, [buf], [d])
        self.dbg_outs.append("dbg_" + name)

    def build(self):
        kb, nc = self.kb, self.nc
        NTP, NT, T, DEPTH = self.NTP, self.NT, self.T, self.DEPTH
        I = lambda n, s, dt=F32: kb.dram(n, s, dt, "ExternalInput")
        O = lambda n, s, dt=F32: kb.dram(n, s, dt, "ExternalOutput")
        self.d = d = {}
        d["xT"] = I("xT", [128, 8, T])
        d["swkT"] = I("swkT", [DEPTH, 64, 16, 2, 128])
        d["swv"] = I("swv", [DEPTH, 128, 16, 2, 64])
        d["sgla"] = I("sgla", [DEPTH, 4, 128, 16, 256])
        d["spool"] = I("spool", [DEPTH, 240, 1024])
        d["w_in"] = I("w_in", [DEPTH, 1024, INW])
        d["vecF"] = I("vecF", [DEPTH, 128, VF_N])
        d["b_in"] = I("b_in", [DEPTH, INW])
        d["sinks"] = I("sinks", [DEPTH, 16])
        d["w_alpha"] = I("w_alpha", [DEPTH, 16, 512])
        d["b_alpha"] = I("b_alpha", [DEPTH, 512])
        d["w_pool"] = I("w_pool", [DEPTH, 4, 256, 256])
        d["w_ba"] = I("w_ba", [DEPTH, 1024, 1024])
        d["w_bb"] = I("w_bb", [DEPTH, 1024, 1024])
        d["w_bc"] = I("w_bc", [DEPTH, 1024, 1024])
        d["w_out"] = I("w_out", [DEPTH, 1024, 1024])
        d["pq"] = I("pq", [DEPTH, 1024, 2048])
        d["skT"] = I("skT", [DEPTH, 128, 16, 128])
        for l_ in range(DEPTH):
            if self.peer_mode == "dense":
                d["puT%d" % l_] = I("puT%d" % l_, [128, 128, 1024])
                d["pvp%d" % l_] = I("pvp%d" % l_, [128, 128, 1024])
                d["ub16_%d" % l_] = kb.dram("ub16_%d" % l_, [128, 128, 1024], BF16, "Internal")
                d["vb16_%d" % l_] = kb.dram("vb16_%d" % l_, [128, 128, 1024], BF16, "Internal")
            else:
                d["pu%d" % l_] = I("pu%d" % l_, [16384, 1024])
                d["pv%d" % l_] = I("pv%d" % l_, [16384, 1024])
        for k, s in (("c_att", [128, 4, 512]), ("c_gm", [128, 2, 128]), ("c_gu", [128, 4, 128]), ("c_ind", [128, 16]),
                     ("c_pm", [128, 6, 4, 128]), ("c_rope", [64, 2, T]), ("c_misc", [128, 4, 128]), ("c_blk", [4, 512])):
            d[k] = I(k, s)
        d["yT"] = O("yT", [128, 8, T])
        d["o_pkT"] = O("o_pkT", [DEPTH, 64, 2, 128])
        d["o_pv"] = O("o_pv", [DEPTH, 128, 128])
        d["o_pg"] = O("o_pg", [DEPTH, 4, 128, 256])
        d["o_pp"] = O("o_pp", [DEPTH, 15, 1024])
        d["o_skT"] = O("o_skT", [DEPTH, 64, 16, 2, 128])
        d["o_sv"] = O("o_sv", [DEPTH, 16, 128, 128])
        d["o_sg"] = O("o_sg", [DEPTH, 4, 128, 16, 256])
        d["o_sp"] = O("o_sp", [DEPTH, 16, 15, 1024])

        with ExitStack() as es:
            kb.start(es)
            self.banks = [kb.ps("bank%d" % i, [128, 512]) for i in range(8)]
            self.bank_i = 0
            self.xT_t = es.enter_context(nc.sbuf_tensor("xTres", [128, 8, T], F32))
            self.xT = [Buf(self.xT_t[:, :, n * 128:(n + 1) * 128], "xT%d" % n) for n in range(NT)]
            self.misc = kb.sb("misc", [128, 4, 128], F32)
            self.ident = self.misc[:, 0, :]
            self.cbf = kb.sb("cbf", [128, 4, 128], BF16)
            self.vecF = kb.sb("vecF", [128, VF_N], F32)
            self.blk = kb.sb("blk", [4, 512], BF16)
            self.gbr = kb.sb("gbr", [4, 6, 128], BF16)
            self.load_w(self.blk, d["c_blk"][:], d["c_blk"])
            self.ld("sp", self.misc[:], d["c_misc"][:], [d["c_misc"]], [self.misc])
            self.cp("dve", self.cbf[:, 0, :], self.misc[:, 0, :], [self.misc], [self.cbf])
            kb.op("dve", lambda: nc.vector.memset(self.cbf[:, 1, :], 1.0), writes=[self.cbf])
            kb.op("dve", lambda: nc.vector.memset(self.cbf[:, 2, :], 1.0 / 1024), writes=[self.cbf])
            kb.op("dve", lambda: nc.vector.memset(self.cbf[:, 3, :], 1.0 / 256), writes=[self.cbf])
            for n in range(NT):
                self.ld("sp", self.xT[n][:], d["xT"][:, :, n * 128:(n + 1) * 128], [d["xT"]], [self.xT[n]])
            for l in range(DEPTH):
                self.layer(l)
            for n in range(NT):
                self.ld("sp", d["yT"][:, :, n * 128:(n + 1) * 128], self.xT[n][:], [self.xT[n]], [d["yT"]])
            kb.barrier()
        return self

    def cast_x(self, n, pool):
        xb = pool[self.xb_i % len(pool)]
        self.xb_i += 1
        self.cp("dve", xb[:], self.xT[n][:], [self.xT[n]], [xb])
        return xb

    def load_w(self, dst, src_ap, src):
        self.ld("pool", dst[:], src_ap, [src], [dst])

    def gate_and_merge(self, l, n, xb, Wg, gcol, Wbr, orows, oT, first, es_sig):
        sig, tmp = es_sig
        for half in range(2):
            bb = self.bank()
            bg = self.bank()
            for c4 in range(4):
                c = half * 4 + c4
                nk = len(orows)
                for ki, k in enumerate(orows):
                    self.mm(bb[:, c4 * 128:(c4 + 1) * 128], Wbr[:, k, c * 128:(c + 1) * 128], oT[:, ki, :], ki == 0, ki == nk - 1, [Wbr, oT], bb)
                for dc in range(8):
                    self.mm(bg[:, c4 * 128:(c4 + 1) * 128], Wg[:, dc, c * 128:(c + 1) * 128], xb[:, dc, :], dc == 0 and c4 == 0, False, [Wg, xb], bg)
            self.mm(bg[:], self.gbr[:, gcol * 2 + half, :], self.blk[:], False, True, [self.gbr, self.blk], bg)
            self.act(sig[:].rearrange("p a b -> p (a b)"), bg[:], ACT.Sigmoid, [bg], [sig])
            mslice = self.merged[n][:, half * 4:(half + 1) * 4, :]
            bv = bb[:].rearrange("p (a b) -> p a b", a=4)
            if first:
                self.tt("dve", mslice, bv, sig[:], ALU.mult, [bb, sig], [self.merged[n]])
            else:
                self.tt("dve", tmp[:], bv, sig[:], ALU.mult, [bb, sig], [tmp])
                self.tt("pool", mslice, mslice, tmp[:], ALU.add, [tmp, self.merged[n]], [self.merged[n]])

    def layer(self, l):
        kb, nc, d = self.kb, self.nc, self.d
        NTP, NT = self.NTP, self.NT
        kb.barrier()
        self.ld("sp", self.vecF[:], d["vecF"][l], [d["vecF"]], [self.vecF])
        kb.dma("pool", lambda: nc.gpsimd.dma_start(out=self.gbr[:], in_=d["b_in"][l, C_GT:INW].rearrange("(g k p) -> k g p", k=4, p=128)),
               reads=[d["b_in"]], writes=[self.gbr])
        with ExitStack() as les:
            mt = les.enter_context(nc.sbuf_tensor("merged%d" % l, [128, 8, self.T], BF16))
            self.merged = [Buf(mt[:, :, n * 128:(n + 1) * 128], "mg%d" % n) for n in range(NT)]
            if "A" in self.phases:
                with nc.named_scope("A%d" % l):
                    self.phase_attn(l)
            if "B" in self.phases:
                for hh in range(4):
                    with nc.named_scope("B%d_%d" % (l, hh)):
                        self.phase_gla(l, hh)
            if "C" in self.phases:
                with nc.named_scope("C%d" % l):
                    self.phase_pool(l)
            if "D" in self.phases:
                with nc.named_scope("D%d" % l):
                    self.phase_out(l)
            kb.barrier()
        if "E" in self.phases:
            with nc.named_scope("E%d" % l):
                if self.peer_mode == "dense":
                    self.phase_peer_dense(l)
                else:
                    self.phase_peer(l)

    def phase_attn(self, l):
        kb, nc, d = self.kb, self.nc, self.d
        NTP, NT = self.NTP, self.NT
        kb.barrier()
        with ExitStack() as es:
            sb = lambda n, s, dt: kb.sb(n, s, dt, es)
            Wqk = sb("Wqk", [128, 8, 1152], BF16)
            Wv = sb("Wv", [128, 8, 128], BF16)
            Wg = sb("WgA", [128, 8, 1024], BF16)
            Wa = sb("Wa", [128, 8, 1024], BF16)
            win = d["w_in"]
            wv = lambda c0, c1: win[l, :, c0:c1].rearrange("(c p) n -> p c n", p=128)
            self.load_w(Wqk, wv(C_QA, C_VA), win)
            self.load_w(Wv, wv(C_VA, C_QB), win)
            self.load_w(Wg, wv(C_GT, C_GT + 1024), win)
            self.load_w(Wa, d["w_ba"][l].rearrange("(c p) n -> p c n", p=128), d["w_ba"])
            amask = sb("amask", [128, 4, 512], BF16)
            self.load_w(amask, d["c_att"][:], d["c_att"])
            bva = sb("bva", [128, 128], F32)
            self.ld("sp", bva[:], d["b_in"][l, C_VA:C_QB].partition_broadcast(128), [d["b_in"]], [bva])
            esk = sb("esk", [128, 16], F32)
            self.ld("sp", esk[:], d["sinks"][l, :].partition_broadcast(128), [d["sinks"]], [esk])
            self.act(esk[:], esk[:], ACT.Exp, [esk], [esk])
            xbp = [sb("xbA%d" % i, [128, 8, 128], BF16) for i in range(1)]
            self.xb_i = 0
            rope = [sb("rope%d" % i, [64, 2, 128], F32) for i in range(2)]
            qf = [sb("qf%d" % i, [64, 4, 128], F32) for i in range(2)]
            t2 = [sb("t2%d" % i, [64, 4, 128], F32) for i in range(1)]
            QTs = sb("QTs", [64, 16, 128], BF16)
            KTs = [sb("KT%d" % i, [64, 2, 128], BF16) for i in range(2)]
            krf = sb("krf", [64, 2, 128], F32)
            Vd = [sb("Vd%d" % i, [128, 2, 2, 64], BF16) for i in range(2)]
            vf = sb("vf", [128, 128], F32)
            Pown = [sb("Pown%d" % i, [128, 512], BF16) for i in range(2)]
            Pprev = [sb("Pprev%d" % i, [128, 512], BF16) for i in range(2)]
            rden = [sb("rden%d" % i, [128, 512], F32) for i in range(1)]
            oaT = [sb("oaT%d" % i, [128, 8, 128], BF16) for i in range(2)]
            sig = [sb("sigA%d" % i, [128, 4, 128], F32) for i in range(1)]
            kvb = sb("kvb", [128, 4096], BF16)
            kbT = kvb[0:64, :].rearrange("p (b g t) -> p b g t", b=16, g=2)
            vbd = kvb[:, :].rearrange("p (b g a e) -> p b g a e", b=16, g=2, a=2)
            Psp = sb("Psp", [128, 32, 64], BF16)
            perm = self.misc[0:64, 1, 0:64]
            pi = 0
            for n in range(NT):
                samp = (n == NTP)
                first = (n == 0)
                xb = self.cast_x(n, xbp)
                rp = rope[n % 2]
                self.ld("sp", rp[:], d["c_rope"][:, :, n * 128:(n + 1) * 128], [d["c_rope"]], [rp])
                Q = QTs
                KT = KTs[n % 2]
                KTp = KTs[(n + 1) % 2]
                for hg in range(5):
                    nh = 4 if hg < 4 else 2
                    bq = self.bank()
                    for hh in range(nh):
                        h = hg * 4 + hh
                        for dc in range(8):
                            self.mm(bq[0:64, hh * 128:(hh + 1) * 128], Wqk[:, dc, h * 64:(h + 1) * 64], xb[:, dc, :], dc == 0, dc == 7, [Wqk, xb], bq)
                    q_ = qf[pi % 2]
                    t_ = t2[0]
                    pi += 1
                    for hh in range(nh):
                        h = hg * 4 + hh
                        self.act(q_[:, hh, :], bq[0:64, hh * 128:(hh + 1) * 128], ACT.Identity, [bq, self.vecF], [q_],
                                 bias=self.vecF[0:64, VF_BQ + h:VF_BQ + h + 1])
                    bp = self.bank()
                    self.mm(bp[0:64, 0:nh * 128], perm, q_[:, 0:nh, :], True, True, [self.misc, q_], bp)
                    cb = rp[:, 0, :].unsqueeze(1).to_broadcast([64, nh, 128])
                    sbb = rp[:, 1, :].unsqueeze(1).to_broadcast([64, nh, 128])
                    self.tt("dve", t_[:, 0:nh, :], bp[0:64, 0:nh * 128].rearrange("p (a b) -> p a b", a=nh), sbb, ALU.mult, [bp, rp], [t_])
                    self.tt("pool", q_[:, 0:nh, :], q_[:, 0:nh, :], cb, ALU.mult, [q_, rp], [q_])
                    if hg < 4:
                        self.tt("dve", Q[:, hg * 4:hg * 4 + nh, :], q_[:, 0:nh, :], t_[:, 0:nh, :], ALU.add, [q_, t_], [Q])
                    else:
                        self.tt("dve", KT[:], q_[:, 0:nh, :], t_[:, 0:nh, :], ALU.add, [q_, t_], [KT])
                    if hg == 4 and (n == NTP - 1 or samp):
                        self.tt("pool", krf[:], q_[:, 0:2, :], t_[:, 0:2, :], ALU.add, [q_, t_], [krf])
                bv = self.bank()
                for dc in range(8):
                    self.mm(bv[:, 0:128], xb[:, dc, :], Wv[:, dc, :], dc == 0, dc == 7, [xb, Wv], bv)
                V = Vd[n % 2]
                Vp = Vd[(n + 1) % 2]
                bvv = bv[:, 0:128].rearrange("p (g e) -> p g e", g=2).unsqueeze(2).to_broadcast([128, 2, 2, 64])
                bia = bva[:].rearrange("p (g e) -> p g e", g=2).unsqueeze(2).to_broadcast([128, 2, 2, 64])
                self.tt("dve", V[:], bvv, bia, ALU.add, [bv, bva], [V])
                if n == NTP - 1 or samp:
                    self.tt("dve", vf[:], bv[:, 0:128], bva[:], ALU.add, [bv, bva], [vf])
                if n == NTP - 1:
                    self.ld("sp", d["o_pkT"][l], krf[:], [krf], [d["o_pkT"]])
                    self.ld("sp", d["o_pv"][l], vf[:], [vf], [d["o_pv"]])
                if samp:
                    self.ld("sp", d["o_skT"][l, :, :, :, 0:120], d["swkT"][l, :, :, :, 8:128], [d["swkT"]], [d["o_skT"]])
                    for g_ in range(2):
                        self.ld("sp", d["o_skT"][l, :, :, g_, 120:128], krf[:, g_, :].rearrange("e (b t) -> e b t", t=8), [krf], [d["o_skT"]])
                    self.ld("sp", d["o_sv"][l, :, 0:120, :], d["swv"][l, 8:128].rearrange("p b g e -> b p (g e)"), [d["swv"]], [d["o_sv"]])
                    for b in range(16):
                        self.ld("sp", d["o_sv"][l, b, 120:128, :], vf[8 * b:8 * b + 8, :], [vf], [d["o_sv"]])
                    kb.dma("pool", lambda: nc.gpsimd.dma_start(out=kbT, in_=d["swkT"][l]), reads=[d["swkT"]], writes=[kvb])
                    for q4 in range(4):
                        bs = self.bank()
                        self.mm(bs[:, :], self.cbf[:, 0, :], amask[:, 3, :], True, False, [self.cbf, amask], bs)
                        for bl in range(8):
                            blk = q4 * 8 + bl
                            b, g = blk // 2, blk % 2
                            rhs = Q[:, 8 * g:8 * g + 8, 8 * b:8 * b + 8]
                            self.mm(bs[:, bl * 64:(bl + 1) * 64], kbT[:, b, g, :], rhs, False, bl == 7, [kvb, Q], bs)
                        self.act(Psp[:, q4 * 8:(q4 + 1) * 8, :], bs[:].rearrange("p (a b) -> p a b", a=8), ACT.Exp, [bs], [Psp], scale=0.125)
                    for a_ in range(2):
                        kb.dma("pool", lambda: nc.gpsimd.dma_start(out=vbd[:, :, :, a_, :], in_=d["swv"][l]), reads=[d["swv"]], writes=[kvb])
                for hg in range(4):
                    g = hg // 2
                    Po = Pown[hg % 2]
                    Pp = Pprev[hg % 2]
                    rd = rden[0]
                    bo = self.bank()
                    rq = Q[:, hg * 4:(hg + 1) * 4, :]
                    self.mm(bo[:], KT[:, g, :], rq, True, False, [KT, Q], bo)
                    self.mm(bo[:], self.cbf[:, 0, :], amask[:, 2 if samp else 0, :], False, True, [self.cbf, amask], bo)
                    self.act(Po[:], bo[:], ACT.Exp, [bo], [Po], scale=0.125)
                    use_prev = (not first) and (not samp)
                    if use_prev:
                        bpv = self.bank()
                        self.mm(bpv[:], KTp[:, g, :], rq, True, False, [KTp, Q], bpv)
                        self.mm(bpv[:], self.cbf[:, 0, :], amask[:, 1, :], False, True, [self.cbf, amask], bpv)
                        self.act(Pp[:], bpv[:], ACT.Exp, [bpv], [Pp], scale=0.125)
                    bO = self.bank()
                    bD = self.bank()
                    self.mm(bO[:], V[:, g].rearrange("p a e -> p (a e)"), Po[:], True, (not use_prev) and (not samp), [V, Po], bO)
                    if use_prev:
                        self.mm(bO[:], Vp[:, g].rearrange("p a e -> p (a e)"), Pp[:], False, True, [Vp, Pp], bO)
                    self.mm(bD[:], self.cbf[:, 1, :], Po[:], True, (not use_prev) and (not samp), [self.cbf, Po], bD)
                    if use_prev:
                        self.mm(bD[:], self.cbf[:, 1, :], Pp[:], False, True, [self.cbf, Pp], bD)
                    if samp:
                        hl = (hg % 2) * 4
                        for b in range(16):
                            rhs = Psp[:, b * 2 + g, hl * 8:(hl + 4) * 8]
                            oO = bO[:].rearrange("p (a t) -> p a t", a=4)[:, :, 8 * b:8 * b + 8]
                            oD = bD[:].rearrange("p (a t) -> p a t", a=4)[:, :, 8 * b:8 * b + 8]
                            self.mm(oO, vbd[:, b, g].rearrange("p a e -> p (a e)"), rhs, False, b == 15, [kvb, Psp], bO)
                            self.mm(oD, self.cbf[:, 1, :], rhs, False, b == 15, [self.cbf, Psp], bD)
                    else:
                        pass
                    for hh_ in range(4):
                        self.ts("dve", rd[:, hh_ * 128:(hh_ + 1) * 128], bD[:, hh_ * 128:(hh_ + 1) * 128], esk[:, hg * 4 + hh_:hg * 4 + hh_ + 1], None, ALU.add, None, [bD, esk], [rd])
                    kb.op("dve", lambda: nc.vector.reciprocal(out=rd[:], in_=rd[:]), reads=[rd], writes=[rd])
                    oa = oaT[n % 2]
                    bO3 = bO[:].rearrange("p (a t) -> p a t", a=4)
                    rd3 = rd[:].rearrange("p (a t) -> p a t", a=4)
                    self.tt("dve", oa[0:64, hg * 2:hg * 2 + 2, :], bO3[0:64, 0:4:2, :], rd3[0:64, 0:4:2, :], ALU.mult, [bO, rd], [oa])
                    self.tt("dve", oa[64:128, hg * 2:hg * 2 + 2, :], bO3[64:128, 1:4:2, :], rd3[64:128, 1:4:2, :], ALU.mult, [bO, rd], [oa])
                if "oa" in self.dbg:
                    self.dump("oa_%d_%d" % (l, n), oaT[n % 2], oaT[n % 2][:], [128, 8, 128], BF16)
                self.gate_and_merge(l, n, xb, Wg, 0, Wa, list(range(8)), oaT[n % 2], True, (sig[0], None))
            kb.barrier()

    def phase_gla(self, l, hh):
        kb, nc, d = self.kb, self.nc, self.d
        NTP, NT = self.NTP, self.NT
        kb.barrier()
        with ExitStack() as es:
            sb = lambda n, s, dt: kb.sb(n, s, dt, es)
            win = d["w_in"]
            wv = lambda c0, c1: win[l, :, c0:c1].rearrange("(c p) n -> p c n", p=128)
            Wq = sb("Wq", [128, 8, 128], BF16)
            Wk = sb("Wk", [128, 8, 128], BF16)
            Wv = sb("WvB", [128, 8, 256], BF16)
            Wgb = sb("Wgb", [128, 8, 256], BF16)
            Wlr = sb("Wlr", [128, 8, 16], BF16)
            Wal = sb("Wal", [16, 128], F32)
            Wg = sb("WgB", [128, 8, 1024], BF16)
            Wb = sb("Wb", [128, 2, 1024], BF16)
            self.load_w(Wq, wv(C_QB + hh * 128, C_QB + (hh + 1) * 128), win)
            self.load_w(Wk, wv(C_KB + hh * 128, C_KB + (hh + 1) * 128), win)
            self.load_w(Wv, wv(C_VB + hh * 256, C_VB + (hh + 1) * 256), win)
            self.load_w(Wgb, wv(C_GB + hh * 256, C_GB + (hh + 1) * 256), win)
            self.load_w(Wlr, wv(C_LR, C_LR + 16), win)
            self.load_w(Wg, wv(C_GT + 1024, C_GT + 2048), win)
            self.load_w(Wb, d["w_bb"][l, hh * 256:(hh + 1) * 256, :].rearrange("(c p) n -> p c n", p=128), d["w_bb"])
            self.ld("sp", Wal[:], d["w_alpha"][l, :, hh * 128:(hh + 1) * 128], [d["w_alpha"]], [Wal])
            bkb = sb("bkb", [128, 128], F32)
            bvb = sb("bvb", [128, 256], F32)
            bal = sb("bal", [128, 128], F32)
            self.ld("sp", bkb[:], d["b_in"][l, C_KB + hh * 128:C_KB + (hh + 1) * 128].partition_broadcast(128), [d["b_in"]], [bkb])
            self.ld("sp", bvb[:], d["b_in"][l, C_VB + hh * 256:C_VB + (hh + 1) * 256].partition_broadcast(128), [d["b_in"]], [bvb])
            self.ld("sp", bal[:], d["b_alpha"][l, hh * 128:(hh + 1) * 128].partition_broadcast(128), [d["b_alpha"]], [bal])
            gm = sb("gm", [128, 2, 128], BF16)
            gu = sb("gu", [128, 4, 128], F32)
            ind = sb("ind", [128, 16], F32)
            self.load_w(gm, d["c_gm"][:], d["c_gm"])
            self.ld("sp", gu[:], d["c_gu"][:], [d["c_gu"]], [gu])
            self.ld("sp", ind[:], d["c_ind"][:], [d["c_ind"]], [ind])
            S = sb("S", [128, 256], F32)
            Sbs = [sb("Sb%d" % i, [128, 256], BF16) for i in range(2)]
            kb.op("dve", lambda: nc.vector.memset(S[:], 0.0), writes=[S])
            kb.op("dve", lambda: nc.vector.memset(Sbs[1][:], 0.0), writes=[Sbs[1]])
            S0 = sb("S0", [128, 16, 256], F32)
            S0b = sb("S0b", [128, 16, 256], BF16)
            QM = sb("QM", [128, 16, 128], BF16)
            kb.op("pool", lambda: nc.gpsimd.memset(QM[:], 0.0), writes=[QM])
            xbp = [sb("xbB%d" % i, [128, 8, 128], BF16) for i in range(2)]
            self.xb_i = 0
            gla_specs = (("lrT", [16, 128], F32), ("zb", [128, 128], F32), ("lsp", [128, 128], F32), ("ebs", [128, 128], F32),
                         ("enb", [128, 128], F32), ("eb", [128, 128], F32), ("erb", [128, 128], F32), ("qd", [128, 128], BF16),
                         ("kd", [128, 128], BF16), ("ktm", [128, 128], F32), ("kl", [128, 128], BF16), ("vB", [128, 256], BF16),
                         ("attm", [128, 128], BF16), ("sq", [128, 256], BF16), ("sd", [128, 128], F32), ("rstd", [128, 128], F32),
                         ("gsl", [128, 2, 128], F32), ("otmp", [128, 128], F32))
            gla_sets = [[sb("%s_%d" % (nm, i), shp, dt) for (nm, shp, dt) in gla_specs] for i in range(2)]
            klm = [sb("klm%d" % i, [128, 128], BF16) for i in range(2)]
            obT = [sb("obT%d" % i, [128, 2, 128], BF16) for i in range(2)]
            mtmp = [sb("mtmpB%d" % i, [128, 4, 128], F32) for i in range(1)] * 2
            lnscale = float(np.log(128.0 ** -0.5))
            lnsc = sb("lnsc", [128, 1], F32)
            eps6 = sb("eps6", [128, 1], F32)
            kb.op("dve", lambda: nc.vector.memset(lnsc[:], lnscale), writes=[lnsc])
            kb.op("dve", lambda: nc.vector.memset(eps6[:], 1e-6), writes=[eps6])
            sig8 = [sb("sig8_%d" % i, [128, 8, 128], F32) for i in range(2)]

            def S1(n):
                samp = (n == NTP)
                (lrT, zb, lsp, ebs, enb, eb, erb, qd, kd, ktm, kl, v, attm, sq, sd, rstd, gsl, otmp) = gla_sets[n % 2]
                xb = self.cast_x(n, xbp)
                if samp:
                    self.ld("sp", S0[:], d["sgla"][l, hh], [d["sgla"]], [S0])
                    self.load_w(S0b, d["sgla"][l, hh], d["sgla"])
                b1 = self.bank()
                for dc in range(8):
                    self.mm(b1[0:16, 0:128], Wlr[:, dc, :], xb[:, dc, :], dc == 0, dc == 7, [Wlr, xb], b1)
                self.act(lrT[:], b1[0:16, 0:128], ACT.Identity, [b1, self.vecF], [lrT], bias=self.vecF[0:16, VF_BLR:VF_BLR + 1])
                b2 = self.bank()
                self.mm(b2[:, 0:128], lrT[:], Wal[:], True, True, [lrT, Wal], b2)
                self.tt("dve", zb[:], b2[:, 0:128], bal[:], ALU.add, [b2, bal], [zb])
                self.act(zb[:], zb[:], ACT.Exp, [zb], [zb], scale=-1.0)
                self.act(lsp[:], zb[:], ACT.Ln, [zb], [lsp], bias=1.0)
                kU = 2 if samp else 0
                b3 = self.bank()
                self.mm(b3[:, 0:128], lsp[:], gu[:, kU, :], True, True, [lsp, gu], b3)
                self.mm(b3[:, 128:256], gu[:, kU + 1, :], lsp[:], True, True, [lsp, gu], b3)
                self.act(ebs[:], b3[:, 0:128], ACT.Exp, [b3, lnsc], [ebs], bias=lnsc[:, 0:1])
                self.act(enb[:], b3[:, 0:128], ACT.Exp, [b3], [enb], scale=-1.0)
                self.act(eb[:], b3[:, 0:128], ACT.Exp, [b3], [eb])
                self.act(erb[:], b3[:, 128:256], ACT.Exp, [b3], [erb])
                b4 = self.bank()
                for dc in range(8):
                    self.mm(b4[:, 0:128], Wq[:, dc, :], xb[:, dc, :], dc == 0, dc == 7, [Wq, xb], b4)
                for dc in range(8):
                    self.mm(b4[:, 128:256], Wk[:, dc, :], xb[:, dc, :], dc == 0, dc == 7, [Wk, xb], b4)
                self.stt(qd[:], b4[:, 0:128], self.vecF[:, VF_BQB + hh:VF_BQB + hh + 1], ebs[:], ALU.add, ALU.mult, [b4, self.vecF, ebs], [qd])
                self.stt(kd[:], b4[:, 128:256], self.vecF[:, VF_BKB + hh:VF_BKB + hh + 1], enb[:], ALU.add, ALU.mult, [b4, self.vecF, enb], [kd])
                b5 = self.bank()
                for dc in range(8):
                    self.mm(b5[:, 0:128], xb[:, dc, :], Wk[:, dc, :], dc == 0, dc == 7, [Wk, xb], b5)
                for dc in range(8):
                    self.mm(b5[:, 128:384], xb[:, dc, :], Wv[:, dc, :], dc == 0, dc == 7, [Wv, xb], b5)
                self.tt("dve", ktm[:], b5[:, 0:128], bkb[:], ALU.add, [b5, bkb], [ktm])
                self.tt("pool", kl[:], ktm[:], erb[:], ALU.mult, [ktm, erb], [kl])
                self.tt("dve", v[:], b5[:, 128:384], bvb[:], ALU.add, [b5, bvb], [v])
                if samp:
                    qm_diag = bass.AP(tensor=QM.t, offset=0, ap=[[16 * 128, 128], [128 + 8, 16], [1, 8]])
                    self.cp("dve", qm_diag, qd[:].rearrange("p (b t) -> p b t", t=8), [qd], [QM])
                sg = sig8[n % 2]
                for half in range(2):
                    bg = self.bank()
                    for c4 in range(4):
                        c = half * 4 + c4
                        for dc in range(8):
                            self.mm(bg[:, c4 * 128:(c4 + 1) * 128], Wg[:, dc, c * 128:(c + 1) * 128], xb[:, dc, :], dc == 0 and c4 == 0, False, [Wg, xb], bg)
                    self.mm(bg[:], self.gbr[:, 2 + half, :], self.blk[:], False, True, [self.gbr, self.blk], bg)
                    self.act(sg[:, half * 4:(half + 1) * 4, :].rearrange("p a b -> p (a b)"), bg[:], ACT.Sigmoid, [bg], [sg])
                b9 = self.bank()
                for dvc in range(2):
                    for dc in range(8):
                        self.mm(b9[:, dvc * 128:(dvc + 1) * 128], Wgb[:, dc, dvc * 128:(dvc + 1) * 128], xb[:, dc, :], dc == 0, dc == 7, [Wgb, xb], b9)
                for dvc in range(2):
                    cidx = VF_BGB + hh * 2 + dvc
                    self.act(gsl[:, dvc, :], b9[:, dvc * 128:(dvc + 1) * 128], ACT.Sigmoid, [b9, self.vecF], [gsl], bias=self.vecF[:, cidx:cidx + 1])
                    self.stt(gsl[:, dvc, :], b9[:, dvc * 128:(dvc + 1) * 128], self.vecF[:, cidx:cidx + 1], gsl[:, dvc, :], ALU.add, ALU.mult, [b9, self.vecF, gsl], [gsl])

            def S2(n):
                samp = (n == NTP)
                (lrT, zb, lsp, ebs, enb, eb, erb, qd, kd, ktm, kl, v, attm, sq, sd, rstd, gsl, otmp) = gla_sets[n % 2]
                Sb_prev = Sbs[(n + 1) % 2]
                if not samp:
                    b10 = self.bank()
                    self.mm(b10[:, 0:256], kl[:], v[:], True, True, [kl, v], b10)
                    self.stt(S[:], S[:], eb[:, 127:128], b10[:, 0:256], ALU.mult, ALU.add, [S, eb, b10], [S])
                    self.cp("pool", Sbs[n % 2][:], S[:], [S], [Sbs[n % 2]])
                    if n == NTP - 1:
                        self.ld("sp", d["o_pg"][l, hh], S[:], [S], [d["o_pg"]])
                b6 = self.bank()
                self.mm(b6[:, 0:128], kd[:], qd[:], True, True, [kd, qd], b6)
                self.tt("dve", attm[:], b6[:, 0:128], gm[:, 1 if samp else 0, :], ALU.mult, [b6, gm], [attm])
                b7 = self.bank()
                for dvc in range(2):
                    o7 = b7[:, dvc * 128:(dvc + 1) * 128]
                    self.mm(o7, v[:, dvc * 128:(dvc + 1) * 128], attm[:], True, False, [v, attm], b7)
                    if not samp:
                        self.mm(o7, Sb_prev[:, dvc * 128:(dvc + 1) * 128], qd[:], False, True, [Sb_prev, qd], b7)
                    else:
                        for b in range(16):
                            self.mm(o7, S0b[:, b, dvc * 128:(dvc + 1) * 128], QM[:, b, :], False, b == 15, [S0b, QM], b7)
                self.act(sq[:], b7[:, 0:256], ACT.Square, [b7], [sq])
                b8 = self.bank()
                self.mm(b8[:, 0:128], self.cbf[:, 3, :], sq[:, 0:128], True, False, [self.cbf, sq], b8)
                self.mm(b8[:, 0:128], self.cbf[:, 3, :], sq[:, 128:256], False, True, [self.cbf, sq], b8)
                self.act(sd[:], b8[:, 0:128], ACT.Ln, [b8, eps6], [sd], bias=eps6[:, 0:1])
                self.act(rstd[:], sd[:], ACT.Exp, [sd], [rstd], scale=-0.5)
                ob = obT[n % 2]
                for dvc in range(2):
                    cidx = VF_GNG + hh * 2 + dvc
                    self.tt("dve", otmp[:], b7[:, dvc * 128:(dvc + 1) * 128], rstd[:], ALU.mult, [b7, rstd], [otmp])
                    self.stt(ob[:, dvc, :], otmp[:], self.vecF[:, cidx:cidx + 1], gsl[:, dvc, :], ALU.mult, ALU.mult, [otmp, self.vecF, gsl], [ob])
                if "ob" in self.dbg:
                    self.dump("ob_%d_%d_%d" % (l, hh, n), ob, ob[:], [128, 2, 128], BF16)
                sg = sig8[n % 2]
                tmp = mtmp[n % 2]
                for half in range(2):
                    bb = self.bank()
                    for c4 in range(4):
                        c = half * 4 + c4
                        for ki in range(2):
                            self.mm(bb[:, c4 * 128:(c4 + 1) * 128], Wb[:, ki, c * 128:(c + 1) * 128], ob[:, ki, :], ki == 0, ki == 1, [Wb, ob], bb)
                    mslice = self.merged[n][:, half * 4:(half + 1) * 4, :]
                    self.tt("dve", tmp[:], bb[:].rearrange("p (a b) -> p a b", a=4), sg[:, half * 4:(half + 1) * 4, :], ALU.mult, [bb, sg], [tmp])
                    self.tt("pool", mslice, mslice, tmp[:], ALU.add, [tmp, self.merged[n]], [self.merged[n]])
                if samp:
                    for b in range(16):
                        km = klm[b % 2]
                        self.ts("pool", km[:], kl[:], ind[:, b:b + 1], None, ALU.mult, None, [kl, ind], [km])
                        bb = self.bank()
                        self.mm(bb[:, 0:256], km[:], v[:], True, True, [km, v], bb)
                        self.stt(S0[:, b, :], S0[:, b, :], eb[:, 8 * b + 7:8 * b + 8], bb[:, 0:256], ALU.mult, ALU.add, [S0, eb, bb], [S0])
                    self.ld("sp", d["o_sg"][l, hh], S0[:], [S0], [d["o_sg"]])

            S1(0)
            for n in range(NT):
                if n + 1 < NT:
                    S1(n + 1)
                S2(n)
            kb.barrier()

    def phase_pool(self, l):
        kb, nc, d = self.kb, self.nc, self.d
        NTP, NT = self.NTP, self.NT
        kb.barrier()
        with ExitStack() as es:
            sb = lambda n, s, dt: kb.sb(n, s, dt, es)
            win = d["w_in"]
            wv = lambda c0, c1: win[l, :, c0:c1].rearrange("(c p) n -> p c n", p=128)
            Wu = sb("Wu", [128, 8, 1024], BF16)
            Wp = sb("Wp", [128, 8, 256], BF16)
            Wc = sb("Wc", [128, 8, 1024], BF16)
            Wg = sb("WgC", [128, 8, 1024], BF16)
            self.load_w(Wu, wv(C_UC, C_UC + 1024), win)
            self.load_w(Wp, d["w_pool"][l].rearrange("g (c p) e -> p (g c) e", p=128), d["w_pool"])
            self.load_w(Wc, d["w_bc"][l].rearrange("(c p) n -> p c n", p=128), d["w_bc"])
            self.load_w(Wg, wv(C_GT + 2048, C_GT + 3072), win)
            buc = sb("buc", [128, 1024], F32)
            self.ld("sp", buc[:], d["b_in"][l, C_UC:C_UC + 1024].partition_broadcast(128), [d["b_in"]], [buc])
            pm = sb("pm", [128, 6, 4, 128], BF16)
            self.load_w(pm, d["c_pm"][:], d["c_pm"])
            ub = [sb("ub%d" % i, [128, 1024], BF16) for i in range(2)]
            uf = sb("uf", [128, 1024], F32)
            sprev = sb("sprev", [128, 2, 1024], BF16)
            kb.dma("pool", lambda: nc.gpsimd.dma_start(out=sprev[:, 0, :], in_=d["spool"][l, 0:128, :]), reads=[d["spool"]], writes=[sprev])
            kb.dma("pool", lambda: nc.gpsimd.dma_start(out=sprev[0:112, 1, :], in_=d["spool"][l, 128:240, :]), reads=[d["spool"]], writes=[sprev])
            xbp = [sb("xbC%d" % i, [128, 8, 128], BF16) for i in range(2)]
            self.xb_i = 0
            dbf = sb("dbf", [128, 8, 128], BF16)
            ocT = [sb("ocT%d" % i, [128, 8, 128], BF16) for i in range(2)]
            sig = [sb("sigC%d" % i, [128, 4, 128], F32) for i in range(2)]
            mtmp = [sb("mtmpC%d" % i, [128, 4, 128], F32) for i in range(2)]
            for n in range(NT):
                samp = (n == NTP)
                xb = self.cast_x(n, xbp)
                u = ub[n % 2]
                up = ub[(n + 1) % 2]
                outt = (n == NTP - 1) or samp
                for half in range(2):
                    bu = self.bank()
                    for dc in range(8):
                        self.mm(bu[:], xb[:, dc, :], Wu[:, dc, half * 512:(half + 1) * 512], dc == 0, dc == 7, [xb, Wu], bu)
                    self.tt("dve", u[:, half * 512:(half + 1) * 512], bu[:], buc[:, half * 512:(half + 1) * 512], ALU.add, [bu, buc], [u])
                    if outt:
                        self.tt("dve", uf[:, half * 512:(half + 1) * 512], bu[:], buc[:, half * 512:(half + 1) * 512], ALU.add, [bu, buc], [uf])
                for half in range(2):
                    bd = self.bank()
                    for c4 in range(4):
                        c = half * 4 + c4
                        g = c // 2
                        o = bd[:, c4 * 128:(c4 + 1) * 128]
                        lhs = u[:, c * 128:(c + 1) * 128]
                        if samp:
                            self.mm(o, lhs, pm[:, 3, g, :], True, False, [u, pm], bd)
                            self.mm(o, sprev[:, 0, c * 128:(c + 1) * 128], pm[:, 4, g, :], False, False, [sprev, pm], bd)
                            self.mm(o, sprev[0:112, 1, c * 128:(c + 1) * 128], pm[0:112, 5, g, :], False, True, [sprev, pm], bd)
                        elif n == 0:
                            self.mm(o, lhs, pm[:, 2, g, :], True, True, [u, pm], bd)
                        else:
                            self.mm(o, lhs, pm[:, 0, g, :], True, False, [u, pm], bd)
                            self.mm(o, up[:, c * 128:(c + 1) * 128], pm[:, 1, g, :], False, True, [up, pm], bd)
                    self.cp("act", dbf[:, half * 4:(half + 1) * 4, :], bd[:].rearrange("p (a b) -> p a b", a=4), [bd], [dbf])
                oc = ocT[n % 2]
                for half in range(2):
                    by = self.bank()
                    for c4 in range(4):
                        e = half * 4 + c4
                        g, ec = e // 2, e % 2
                        for cc in range(2):
                            self.mm(by[:, c4 * 128:(c4 + 1) * 128], Wp[:, g * 2 + cc, ec * 128:(ec + 1) * 128], dbf[:, g * 2 + cc, :], cc == 0, cc == 1, [Wp, dbf], by)
                    for c4 in range(4):
                        e = half * 4 + c4
                        self.ts("dve", oc[:, e, :], by[:, c4 * 128:(c4 + 1) * 128], self.vecF[:, VF_PSC + e:VF_PSC + e + 1], None, ALU.mult, None, [by, self.vecF], [oc])
                if "oc" in self.dbg:
                    self.dump("oc_%d_%d" % (l, n), oc, oc[:], [128, 8, 128], BF16)
                self.gate_and_merge(l, n, xb, Wg, 2, Wc, list(range(8)), oc, False, (sig[n % 2], mtmp[n % 2]))
                if n == NTP - 1:
                    self.ld("sp", d["o_pp"][l], uf[113:128, :], [uf], [d["o_pp"]])
                if samp:
                    self.ld("sp", d["o_sp"][l, :, 0:7, :], d["spool"][l].rearrange("(b r) f -> b r f", r=15)[:, 8:15, :], [d["spool"]], [d["o_sp"]])
                    for b in range(16):
                        self.ld("sp", d["o_sp"][l, b, 7:15, :], uf[8 * b:8 * b + 8, :], [uf], [d["o_sp"]])
            kb.barrier()

    def ln_fm(self, y, n, gcol, bcol, bufs, bank=None):
        kb, nc = self.kb, self.nc
        ybf, ysq, mean, m2, var, eps5 = bufs
        self.cp("act", ybf[:], y[:], [y], [ybf])
        self.act(ysq[:], y[:], ACT.Square, [y], [ysq])
        bm = bank if bank is not None else self.bank()
        for c in range(8):
            self.mm(bm[:, 0:128], self.cbf[:, 2, :], ybf[:, c, :], c == 0, c == 7, [self.cbf, ybf], bm)
        for c in range(8):
            self.mm(bm[:, 128:256], self.cbf[:, 2, :], ysq[:, c, :], c == 0, c == 7, [self.cbf, ysq], bm)
        self.cp("act", mean[:], bm[:, 0:128], [bm], [mean])
        self.tt("pool", m2[:], mean[:], mean[:], ALU.mult, [mean], [m2])
        self.tt("dve", var[:], bm[:, 128:256], m2[:], ALU.subtract, [bm, m2], [var])
        self.act(var[:], var[:], ACT.Sqrt, [var, eps5], [var], bias=eps5[:, 0:1])
        kb.op("dve", lambda: nc.vector.reciprocal(out=var[:], in_=var[:]), reads=[var], writes=[var])
        mb = mean[:].unsqueeze(1).to_broadcast([128, 8, 128])
        rb = var[:].unsqueeze(1).to_broadcast([128, 8, 128])
        self.tt("dve", y[:], y[:], mb, ALU.subtract, [y, mean], [y])
        self.tt("pool", y[:], y[:], rb, ALU.mult, [y, var], [y])
        for c in range(8):
            self.ts("dve", self.xT[n][:, c, :], y[:, c, :], self.vecF[:, gcol + c:gcol + c + 1], self.vecF[:, bcol + c:bcol + c + 1],
                    ALU.mult, ALU.add, [y, self.vecF], [self.xT[n]])

    def phase_out(self, l):
        kb, nc, d = self.kb, self.nc, self.d
        NTP, NT = self.NTP, self.NT
        kb.barrier()
        with ExitStack() as es:
            sb = lambda n, s, dt: kb.sb(n, s, dt, es)
            Wo = sb("Wo", [128, 8, 1024], BF16)
            self.load_w(Wo, d["w_out"][l].rearrange("(c p) n -> p c n", p=128), d["w_out"])
            ys = [sb("yD%d" % i, [128, 8, 128], F32) for i in range(2)]
            eps5 = sb("eps5", [128, 1], F32)
            kb.op("dve", lambda: nc.vector.memset(eps5[:], 1e-5), writes=[eps5])
            lnb = [(sb("ybf%d" % i, [128, 8, 128], BF16), sb("ysq%d" % i, [128, 8, 128], BF16), sb("mean%d" % i, [128, 128], F32),
                    sb("m2%d" % i, [128, 128], F32), sb("var%d" % i, [128, 128], F32), eps5) for i in range(2)]
            for n in range(NT):
                y = ys[n % 2]
                for half in range(2):
                    bo = self.bank()
                    for c4 in range(4):
                        c = half * 4 + c4
                        for k in range(8):
                            self.mm(bo[:, c4 * 128:(c4 + 1) * 128], Wo[:, k, c * 128:(c + 1) * 128], self.merged[n][:, k, :], k == 0, k == 7, [Wo, self.merged[n]], bo)
                    self.stt(y[:, half * 4:(half + 1) * 4, :], self.xT[n][:, half * 4:(half + 1) * 4, :], ALPHA,
                             bo[:].rearrange("p (a b) -> p a b", a=4), ALU.mult, ALU.add, [self.xT[n], bo], [y])
                self.ln_fm(y, n, VF_L1G, VF_L1B, lnb[n % 2])
                if "x1" in self.dbg:
                    self.dump("x1_%d_%d" % (l, n), self.xT[n], self.xT[n][:], [128, 8, 128], F32)
            kb.barrier()

    def phase_peer(self, l):
        kb, nc, d = self.kb, self.nc, self.d
        NTP, NT = self.NTP, self.NT
        kb.barrier()
        with ExitStack() as es:
            sb = lambda n, s, dt: kb.sb(n, s, dt, es)
            Wpq = sb("Wpq", [128, 8, 2048], BF16)
            skT = sb("skT", [128, 16, 128], BF16)
            self.load_w(Wpq, d["pq"][l].rearrange("(c p) n -> p c n", p=128), d["pq"])
            self.load_w(skT, d["skT"][l], d["skT"])
            pu, pv = d["pu%d" % l], d["pv%d" % l]
            xbp = [sb("xbE%d" % i, [128, 8, 128], BF16) for i in range(2)]
            self.xb_i = 0
            xtok = sb("xtok", [128, 1024], F32)
            qT = sb("qTE", [128, 16, 128], BF16)
            ssb = sb("ssb", [128, 16, 128], F32)
            s2 = sb("s2", [128, 128], F32)
            sv = sb("sv", [128, 16, 16], F32)
            si = sb("si", [128, 16, 16], U32)
            sif = sb("sif", [128, 16, 16], F32)
            cand = sb("cand", [128, 8, 256], F32)
            cand2 = sb("cand2", [128, 256], F32)
            cs = sb("cs", [128, 8, 16], F32)
            ci = sb("ci", [128, 8, 16], U32)
            ciu = sb("ciu", [128, 8, 16], U32)
            af = sb("af", [128, 8, 16], F32)
            bf = sb("bf", [128, 8, 16], F32)
            eq = sb("eq", [128, 8, 16, 16], F32)
            i0 = sb("i0", [128, 8, 16], F32)
            i1 = sb("i1", [128, 8, 16], F32)
            eidf = sb("eidf", [128, 128], F32)
            eidx = sb("eidx", [128, 128], I32)
            gex = sb("gex", [128, 8, 16], F32)
            gsum = sb("gsum", [128, 8], F32)
            gg = sb("gg", [128, 128], F32)
            hraw = sb("hraw", [128, 128], F32)
            hw = sb("hw", [128, 128], F32)
            NG = 10
            gb = [sb("gbuf%d" % i, [128, 1024], BF16) for i in range(NG)]
            NPB = 4
            prods = [sb("prod%d" % i, [128, 1024], BF16) for i in range(NPB)]
            dgs = [sb("dg%d" % i, [128, 128], BF16) for i in range(4)]
            xtokb = sb("xtokb", [128, 1024], BF16)
            junkb = sb("junkb", [128, 1024], BF16)
            junk = sb("junk", [128, 1024], F32)
            acc = sb("acc", [128, 1024], F32)
            st = sb("st", [128, 8], F32)
            iota16 = self.misc[:, 2, 0:16]
            gi = 0
            for n in range(NT):
                xb = self.cast_x(n, xbp)
                for half in range(2):
                    bt = self.bank()
                    for c4 in range(4):
                        c = half * 4 + c4
                        kb.op("pe", lambda: nc.tensor.transpose(bt[:, c4 * 128:(c4 + 1) * 128], self.xT[n][:, c, :], self.ident), reads=[self.xT[n], self.misc], writes=[bt])
                    self.cp("act", xtok[:, half * 512:(half + 1) * 512], bt[:], [bt], [xtok])
                for q4 in range(4):
                    bq = self.bank()
                    for hh in range(4):
                        hp = q4 * 4 + hh
                        for dc in range(8):
                            self.mm(bq[:, hh * 128:(hh + 1) * 128], Wpq[:, dc, hp * 128:(hp + 1) * 128], xb[:, dc, :], dc == 0, dc == 7, [Wpq, xb], bq)
                    self.cp("act", qT[:, q4 * 4:(q4 + 1) * 4, :], bq[:].rearrange("p (a b) -> p a b", a=4), [bq], [qT])
                for q4 in range(4):
                    bs = self.bank()
                    for hh in range(4):
                        hp = q4 * 4 + hh
                        self.mm(bs[:, hh * 128:(hh + 1) * 128], qT[:, hp, :], skT[:, hp, :], True, True, [qT, skT], bs)
                    self.cp("act", ssb[:, q4 * 4:(q4 + 1) * 4, :], bs[:].rearrange("p (a b) -> p a b", a=4), [bs], [ssb])
                for hp in range(16):
                    kb.op("dve", lambda: nc.vector.max(out=sv[:, hp, 0:8], in_=ssb[:, hp, :]), reads=[ssb], writes=[sv])
                    kb.op("dve", lambda: nc.vector.max_index(out=si[:, hp, 0:8], in_max=sv[:, hp, 0:8], in_values=ssb[:, hp, :]), reads=[ssb, sv], writes=[si])
                    kb.op("dve", lambda: nc.vector.match_replace(out=s2[:], in_to_replace=sv[:, hp, 0:8], in_values=ssb[:, hp, :], imm_value=-1e30), reads=[ssb, sv], writes=[s2])
                    kb.op("dve", lambda: nc.vector.max(out=sv[:, hp, 8:16], in_=s2[:]), reads=[s2], writes=[sv])
                    kb.op("dve", lambda: nc.vector.max_index(out=si[:, hp, 8:16], in_max=sv[:, hp, 8:16], in_values=s2[:]), reads=[s2, sv], writes=[si])
                self.cp("dve", sif[:], si[:], [si], [sif])
                sv4 = sv[:].rearrange("p (h q) k -> p h q k", q=2)
                sif4 = sif[:].rearrange("p (h q) k -> p h q k", q=2)
                cand4 = cand[:].rearrange("p h (a b) -> p h a b", a=16)
                self.tt("dve", cand4, sv4[:, :, 0, :].unsqueeze(3).to_broadcast([128, 8, 16, 16]),
                        sv4[:, :, 1, :].unsqueeze(2).to_broadcast([128, 8, 16, 16]), ALU.add, [sv], [cand])
                for h in range(8):
                    kb.op("dve", lambda: nc.vector.max(out=cs[:, h, 0:8], in_=cand[:, h, :]), reads=[cand], writes=[cs])
                    kb.op("dve", lambda: nc.vector.max_index(out=ci[:, h, 0:8], in_max=cs[:, h, 0:8], in_values=cand[:, h, :]), reads=[cand, cs], writes=[ci])
                    kb.op("dve", lambda: nc.vector.match_replace(out=cand2[:], in_to_replace=cs[:, h, 0:8], in_values=cand[:, h, :], imm_value=-1e30), reads=[cand, cs], writes=[cand2])
                    kb.op("dve", lambda: nc.vector.max(out=cs[:, h, 8:16], in_=cand2[:]), reads=[cand2], writes=[cs])
                    kb.op("dve", lambda: nc.vector.max_index(out=ci[:, h, 8:16], in_max=cs[:, h, 8:16], in_values=cand2[:]), reads=[cand2, cs], writes=[ci])
                kb.op("dve", lambda: nc.vector.tensor_single_scalar(out=ciu[:], in_=ci[:], scalar=4, op=ALU.logical_shift_right), reads=[ci], writes=[ciu])
                self.cp("dve", af[:], ciu[:], [ciu], [af])
                kb.op("dve", lambda: nc.vector.tensor_single_scalar(out=ciu[:], in_=ci[:], scalar=15, op=ALU.bitwise_and), reads=[ci], writes=[ciu])
                self.cp("dve", bf[:], ciu[:], [ciu], [bf])
                io4 = iota16.unsqueeze(1).unsqueeze(1).to_broadcast([128, 8, 16, 16])
                for (sel, q, dst) in ((af, 0, i0), (bf, 1, i1)):
                    self.tt("dve", eq[:], sel[:].unsqueeze(3).to_broadcast([128, 8, 16, 16]), io4, ALU.is_equal, [sel, self.misc], [eq])
                    self.tt("dve", eq[:], eq[:], sif4[:, :, q, :].unsqueeze(2).to_broadcast([128, 8, 16, 16]), ALU.mult, [eq, sif], [eq])
                    kb.op("dve", lambda: nc.vector.tensor_reduce(out=dst[:], in_=eq[:], axis=AX.X, op=ALU.add), reads=[eq], writes=[dst])
                self.stt(eidf[:].rearrange("p (h k) -> p h k", h=8), i0[:], 128.0, i1[:], ALU.mult, ALU.add, [i0, i1], [eidf])
                self.ts("dve", eidf[:], eidf[:], 0.0, 16383.0, ALU.max, ALU.min, [eidf], [eidf])
                self.cp("dve", eidx[:], eidf[:], [eidf], [eidx])
                self.tt("dve", gex[:], cs[:], cs[:, :, 0:1].to_broadcast([128, 8, 16]), ALU.subtract, [cs], [gex])
                self.act(gex[:], gex[:], ACT.Exp, [gex], [gex])
                kb.op("dve", lambda: nc.vector.tensor_reduce(out=gsum[:], in_=gex[:], axis=AX.X, op=ALU.add), reads=[gex], writes=[gsum])
                kb.op("dve", lambda: nc.vector.reciprocal(out=gsum[:], in_=gsum[:]), reads=[gsum], writes=[gsum])
                self.tt("dve", gg[:].rearrange("p (h k) -> p h k", h=8), gex[:], gsum[:].unsqueeze(2).to_broadcast([128, 8, 16]), ALU.mult, [gex, gsum], [gg])
                if "eidx" in self.dbg:
                    self.dump("eidx_%d_%d" % (l, n), eidx, eidx[:], [128, 128], I32)
                    self.dump("gg_%d_%d" % (l, n), gg, gg[:], [128, 128], F32)
                self.cp("act", xtokb[:], xtok[:], [xtok], [xtokb])
                for s in range(128):
                    g_ = gb[gi % NG]
                    pr = prods[gi % NPB]
                    gi += 1
                    kb.dma("pool", lambda: nc.gpsimd.indirect_dma_start(out=g_[:], out_offset=None, in_=pu[:], in_offset=bass.IndirectOffsetOnAxis(ap=eidx[:, s:s + 1], axis=0)),
                           reads=[eidx, pu], writes=[g_])
                    self.tt("dve", pr[:], g_[:], xtokb[:], ALU.mult, [g_, xtokb], [pr])
                    kb.op("act", lambda: nc.scalar.activation(out=junkb[:], in_=pr[:], func=ACT.Copy, accum_out=hraw[:, s:s + 1]), reads=[pr], writes=[junkb, hraw])
                self.act(hw[:], hraw[:], ACT.Gelu, [hraw], [hw])
                self.tt("dve", hw[:], hw[:], gg[:], ALU.mult, [hw, gg], [hw])
                bacc = [self.bank(), self.bank()]
                for s in range(128):
                    g_ = gb[gi % NG]
                    dg = dgs[gi % 4]
                    gi += 1
                    kb.dma("pool", lambda: nc.gpsimd.indirect_dma_start(out=g_[:], out_offset=None, in_=pv[:], in_offset=bass.IndirectOffsetOnAxis(ap=eidx[:, s:s + 1], axis=0)),
                           reads=[eidx, pv], writes=[g_])
                    kb.op("act", lambda: nc.scalar.activation(out=dg[:], in_=self.cbf[:, 0, :], func=ACT.Copy, scale=hw[:, s:s + 1]), reads=[self.cbf, hw], writes=[dg])
                    for half in range(2):
                        self.mm(bacc[half][:], dg[:], g_[:, half * 512:(half + 1) * 512], s == 0, s == 127, [dg, g_], bacc[half])
                for half in range(2):
                    self.stt(acc[:, half * 512:(half + 1) * 512], xtok[:, half * 512:(half + 1) * 512], ALPHA, bacc[half][:], ALU.mult, ALU.add, [xtok, bacc[half]], [acc])
                kb.op("dve", lambda: nc.vector.tensor_reduce(out=st[:, 0:1], in_=acc[:], axis=AX.X, op=ALU.add), reads=[acc], writes=[st])
                self.stt(junk[:], acc[:], 1.0, acc[:], ALU.mult, ALU.mult, [acc], [junk, st], accum_out=st[:, 1:2])
                self.ts("dve", st[:, 2:3], st[:, 0:1], 1.0 / 1024, None, ALU.mult, None, [st], [st])
                self.tt("dve", st[:, 3:4], st[:, 2:3], st[:, 2:3], ALU.mult, [st], [st])
                self.stt(st[:, 4:5], st[:, 1:2], 1.0 / 1024, st[:, 3:4], ALU.mult, ALU.subtract, [st], [st])
                self.ts("dve", st[:, 4:5], st[:, 4:5], 1e-5, None, ALU.add, None, [st], [st])
                self.act(st[:, 5:6], st[:, 4:5], ACT.Sqrt, [st], [st])
                kb.op("dve", lambda: nc.vector.reciprocal(out=st[:, 6:7], in_=st[:, 5:6]), reads=[st], writes=[st])
                self.ts("dve", acc[:], acc[:], st[:, 2:3], st[:, 6:7], ALU.subtract, ALU.mult, [acc, st], [acc])
                for half in range(2):
                    bt = self.bank()
                    for c4 in range(4):
                        c = half * 4 + c4
                        kb.op("pe", lambda: nc.tensor.transpose(bt[:, c4 * 128:(c4 + 1) * 128], acc[:, c * 128:(c + 1) * 128], self.ident), reads=[acc, self.misc], writes=[bt])
                    for c4 in range(4):
                        c = half * 4 + c4
                        self.ts("dve", self.xT[n][:, c, :], bt[:, c4 * 128:(c4 + 1) * 128], self.vecF[:, VF_L2G + c:VF_L2G + c + 1],
                                self.vecF[:, VF_L2B + c:VF_L2B + c + 1], ALU.mult, ALU.add, [bt, self.vecF], [self.xT[n]])
            kb.barrier()


    def phase_peer_dense(self, l):
        kb, nc, d = self.kb, self.nc, self.d
        NTP, NT, T = self.NTP, self.NT, self.T
        kb.barrier()
        puT, pvp, ub16, vb16 = d["puT%d" % l], d["pvp%d" % l], d["ub16_%d" % l], d["vb16_%d" % l]
        for c in range(16):
            kb.dma("pool", lambda: nc.gpsimd.dma_start(out=ub16[c * 8:(c + 1) * 8], in_=puT[c * 8:(c + 1) * 8]), reads=[puT], writes=[ub16])
            kb.dma("pool", lambda: nc.gpsimd.dma_start(out=vb16[c * 8:(c + 1) * 8], in_=pvp[c * 8:(c + 1) * 8]), reads=[pvp], writes=[vb16])
        with ExitStack() as oes:
            idxT = kb.sb("idxT", [128, T, 2], F32, oes)
            gT = kb.sb("gT", [128, T], F32, oes)
            with ExitStack() as es:
                sb = lambda n, s, dt: kb.sb(n, s, dt, es)
                Wpq = sb("Wpq", [128, 8, 2048], BF16)
                skT = sb("skT", [128, 16, 128], BF16)
                self.load_w(Wpq, d["pq"][l].rearrange("(c p) n -> p c n", p=128), d["pq"])
                self.load_w(skT, d["skT"][l], d["skT"])
                xbp = [sb("xbE%d" % i, [128, 8, 128], BF16) for i in range(2)]
                self.xb_i = 0
                qT = sb("qTE", [128, 16, 128], BF16)
                ssb = sb("ssb", [128, 16, 128], F32)
                s2 = sb("s2", [128, 128], F32)
                sv = sb("sv", [128, 16, 16], F32)
                si = sb("si", [128, 16, 16], U32)
                sif = sb("sif", [128, 16, 16], F32)
                cand = sb("cand", [128, 8, 256], F32)
                cand2 = sb("cand2", [128, 256], F32)
                cs = sb("cs", [128, 8, 16], F32)
                ci = sb("ci", [128, 8, 16], U32)
                ciu = sb("ciu", [128, 8, 16], U32)
                af = sb("af", [128, 8, 16], F32)
                bf = sb("bf", [128, 8, 16], F32)
                eq = sb("eq", [128, 8, 16, 16], F32)
                i0 = sb("i0", [128, 8, 16], F32)
                i1 = sb("i1", [128, 8, 16], F32)
                gex = sb("gex", [128, 8, 16], F32)
                gsum = sb("gsum", [128, 8], F32)
                gg = sb("gg", [128, 128], F32)
                iota16 = self.misc[:, 2, 0:16]

                def split(name, ngrp, width, dt):
                    t = es.enter_context(nc.sbuf_tensor("%s_%d" % (name, l), [128, ngrp, width], dt))
                    return t, [Buf(t[:, g_, :], "%s%d" % (name, g_)) for g_ in range(ngrp)]
                svAt, svA = split("svA", 16, 8, F32)
                svBt, svB = split("svB", 16, 8, F32)
                siAt, siA = split("siA", 16, 8, U32)
                siBt, siB = split("siB", 16, 8, U32)
                s2t, s2s = split("s2s", 16, 128, F32)
                csAt, csA = split("csA", 8, 8, F32)
                csBt, csB = split("csB", 8, 8, F32)
                ciAt, ciA = split("ciA", 8, 8, U32)
                ciBt, ciB = split("ciB", 8, 8, U32)
                c2t, c2s = split("c2s", 8, 256, F32)
                ssbs = [ssb, sb("ssb2", [128, 16, 128], F32)]

                def E1a(n):
                    ssb = ssbs[n % 2]
                    xb = self.cast_x(n, xbp)
                    for q4 in range(4):
                        bq = self.bank()
                        for hh in range(4):
                            hp = q4 * 4 + hh
                            for dc in range(8):
                                self.mm(bq[:, hh * 128:(hh + 1) * 128], Wpq[:, dc, hp * 128:(hp + 1) * 128], xb[:, dc, :], dc == 0, dc == 7, [Wpq, xb], bq)
                        self.cp("act", qT[:, q4 * 4:(q4 + 1) * 4, :], bq[:].rearrange("p (a b) -> p a b", a=4), [bq], [qT])
                    for q4 in range(4):
                        bs = self.bank()
                        for hh in range(4):
                            hp = q4 * 4 + hh
                            self.mm(bs[:, hh * 128:(hh + 1) * 128], qT[:, hp, :], skT[:, hp, :], True, True, [qT, skT], bs)
                        self.cp("act", ssb[:, q4 * 4:(q4 + 1) * 4, :], bs[:].rearrange("p (a b) -> p a b", a=4), [bs], [ssb])

                E1a(0)
                for n in range(NT):
                    if n + 1 < NT:
                        E1a(n + 1)
                    ssb = ssbs[n % 2]
                    for hp in range(16):
                        kb.op("dve", lambda: nc.vector.max(out=svA[hp][:], in_=ssb[:, hp, :]), reads=[ssb], writes=[svA[hp]])
                    for hp in range(16):
                        kb.op("dve", lambda: nc.vector.max_index(out=siA[hp][:], in_max=svA[hp][:], in_values=ssb[:, hp, :]), reads=[ssb, svA[hp]], writes=[siA[hp]])
                    for hp in range(16):
                        kb.op("dve", lambda: nc.vector.match_replace(out=s2s[hp][:], in_to_replace=svA[hp][:], in_values=ssb[:, hp, :], imm_value=-1e30), reads=[ssb, svA[hp]], writes=[s2s[hp]])
                    for hp in range(16):
                        kb.op("dve", lambda: nc.vector.max(out=svB[hp][:], in_=s2s[hp][:]), reads=[s2s[hp]], writes=[svB[hp]])
                    for hp in range(16):
                        kb.op("dve", lambda: nc.vector.max_index(out=siB[hp][:], in_max=svB[hp][:], in_values=s2s[hp][:]), reads=[s2s[hp], svB[hp]], writes=[siB[hp]])
                    svv = sv[:].rearrange("p a (q k) -> p a q k", q=2)
                    siv = si[:].rearrange("p a (q k) -> p a q k", q=2)
                    self.cp("dve", svv[:, :, 0, :], svAt[:], svA, [sv])
                    self.cp("dve", svv[:, :, 1, :], svBt[:], svB, [sv])
                    self.cp("dve", siv[:, :, 0, :], siAt[:], siA, [si])
                    self.cp("dve", siv[:, :, 1, :], siBt[:], siB, [si])
                    self.cp("dve", sif[:], si[:], [si], [sif])
                    sv4 = sv[:].rearrange("p (h q) k -> p h q k", q=2)
                    sif4 = sif[:].rearrange("p (h q) k -> p h q k", q=2)
                    cand4 = cand[:].rearrange("p h (a b) -> p h a b", a=16)
                    self.tt("dve", cand4, sv4[:, :, 0, :].unsqueeze(3).to_broadcast([128, 8, 16, 16]),
                            sv4[:, :, 1, :].unsqueeze(2).to_broadcast([128, 8, 16, 16]), ALU.add, [sv], [cand])
                    for h in range(8):
                        kb.op("dve", lambda: nc.vector.max(out=csA[h][:], in_=cand[:, h, :]), reads=[cand], writes=[csA[h]])
                    for h in range(8):
                        kb.op("dve", lambda: nc.vector.max_index(out=ciA[h][:], in_max=csA[h][:], in_values=cand[:, h, :]), reads=[cand, csA[h]], writes=[ciA[h]])
                    for h in range(8):
                        kb.op("dve", lambda: nc.vector.match_replace(out=c2s[h][:], in_to_replace=csA[h][:], in_values=cand[:, h, :], imm_value=-1e30), reads=[cand, csA[h]], writes=[c2s[h]])
                    for h in range(8):
                        kb.op("dve", lambda: nc.vector.max(out=csB[h][:], in_=c2s[h][:]), reads=[c2s[h]], writes=[csB[h]])
                    for h in range(8):
                        kb.op("dve", lambda: nc.vector.max_index(out=ciB[h][:], in_max=csB[h][:], in_values=c2s[h][:]), reads=[c2s[h], csB[h]], writes=[ciB[h]])
                    csv = cs[:].rearrange("p a (q k) -> p a q k", q=2)
                    civ = ci[:].rearrange("p a (q k) -> p a q k", q=2)
                    self.cp("dve", csv[:, :, 0, :], csAt[:], csA, [cs])
                    self.cp("dve", csv[:, :, 1, :], csBt[:], csB, [cs])
                    self.cp("dve", civ[:, :, 0, :], ciAt[:], ciA, [ci])
                    self.cp("dve", civ[:, :, 1, :], ciBt[:], ciB, [ci])
                    kb.op("dve", lambda: nc.vector.tensor_single_scalar(out=ciu[:], in_=ci[:], scalar=4, op=ALU.logical_shift_right), reads=[ci], writes=[ciu])
                    self.cp("dve", af[:], ciu[:], [ciu], [af])
                    kb.op("dve", lambda: nc.vector.tensor_single_scalar(out=ciu[:], in_=ci[:], scalar=15, op=ALU.bitwise_and), reads=[ci], writes=[ciu])
                    self.cp("dve", bf[:], ciu[:], [ciu], [bf])
                    io4 = iota16.unsqueeze(1).unsqueeze(1).to_broadcast([128, 8, 16, 16])
                    for (sel, q, dst) in ((af, 0, i0), (bf, 1, i1)):
                        self.tt("dve", eq[:], sel[:].unsqueeze(3).to_broadcast([128, 8, 16, 16]), io4, ALU.is_equal, [sel, self.misc], [eq])
                        self.tt("dve", eq[:], eq[:], sif4[:, :, q, :].unsqueeze(2).to_broadcast([128, 8, 16, 16]), ALU.mult, [eq, sif], [eq])
                        kb.op("dve", lambda: nc.vector.tensor_reduce(out=dst[:], in_=eq[:], axis=AX.X, op=ALU.add), reads=[eq], writes=[dst])
                    self.tt("dve", gex[:], cs[:], cs[:, :, 0:1].to_broadcast([128, 8, 16]), ALU.subtract, [cs], [gex])
                    self.act(gex[:], gex[:], ACT.Exp, [gex], [gex])
                    kb.op("dve", lambda: nc.vector.tensor_reduce(out=gsum[:], in_=gex[:], axis=AX.X, op=ALU.add), reads=[gex], writes=[gsum])
                    kb.op("dve", lambda: nc.vector.reciprocal(out=gsum[:], in_=gsum[:]), reads=[gsum], writes=[gsum])
                    self.tt("dve", gg[:].rearrange("p (h k) -> p h k", h=8), gex[:], gsum[:].unsqueeze(2).to_broadcast([128, 8, 16]), ALU.mult, [gex, gsum], [gg])
                    bt = self.bank()
                    for k_, (src, sbuf_) in enumerate(((i0, i0), (i1, i1), (gg, gg))):
                        src_ap = src[:].rearrange("p h k -> p (h k)") if k_ < 2 else src[:]
                        kb.op("pe", lambda: nc.tensor.transpose(bt[:, k_ * 128:(k_ + 1) * 128], src_ap, self.ident), reads=[sbuf_, self.misc], writes=[bt])
                    self.cp("act", idxT[:, n * 128:(n + 1) * 128, 0], bt[:, 0:128], [bt], [idxT])
                    self.cp("act", idxT[:, n * 128:(n + 1) * 128, 1], bt[:, 128:256], [bt], [idxT])
                    self.cp("act", gT[:, n * 128:(n + 1) * 128], bt[:, 256:384], [bt], [gT])
                kb.barrier()
            with ExitStack() as es:
                sb = lambda n, s, dt: kb.sb(n, s, dt, es)
                G = sb("Gsb", [128, 256, 128], BF16)
                xbb = sb("xbb", [128, 8, 256], BF16)
                NR = 4
                ubl = [sb("ubl%d" % i, [128, 8, 128], BF16) for i in range(NR)]
                vbl = [sb("vbl%d" % i, [128, 1024], BF16) for i in range(NR)]
                hgs = [sb("hg%d" % i, [128, 256], BF16) for i in range(NR)]
                Ws = [sb("Wd%d" % i, [128, 256], BF16) for i in range(NR)]
                Ag = [sb("Ag%d" % i, [128, 128], BF16) for i in range(4)]
                Bg = [sb("Bg%d" % i, [128, 128], BF16) for i in range(4)]
                ys = [sb("yE%d" % i, [128, 8, 128], F32) for i in range(1)] * 2
                eps5 = sb("eps5E", [128, 1], F32)
                kb.op("dve", lambda: nc.vector.memset(eps5[:], 1e-5), writes=[eps5])
                lnb = [(sb("ybfE%d" % i, [128, 8, 128], BF16), sb("ysqE%d" % i, [128, 8, 128], BF16), sb("meanE%d" % i, [128, 128], F32),
                        sb("m2E%d" % i, [128, 128], F32), sb("varE%d" % i, [128, 128], F32), eps5) for i in range(2)]
                iota128 = self.misc[:, 3, :]
                accb = self.banks[0:4]
                bhs = self.banks[4:6]
                bgs = self.banks[6:8]
                blocks = [(2 * k, 2) for k in range(NTP // 2)] + [(NTP, 1)]
                ri = 0
                yi = 0
                for (n0, ntl) in blocks:
                    nt = ntl * 128
                    t0 = n0 * 128
                    for k in range(ntl):
                        self.cp("dve", xbb[:, :, k * 128:(k + 1) * 128], self.xT[n0 + k][:], [self.xT[n0 + k]], [xbb])
                    sid_g, _ = nc.enter_named_scope("E2g%d" % l, False)
                    for tq in range(nt // 4):
                        bg = bgs[tq % 2]
                        for k in range(4):
                            tl = tq * 4 + k
                            tok = t0 + tl
                            A_ = Ag[tl % 4]
                            B_ = Bg[tl % 4]
                            self.ts("dve", A_[:], iota128, idxT[:, tok, 0:1], gT[:, tok:tok + 1], ALU.is_equal, ALU.mult, [self.misc, idxT, gT], [A_])
                            self.ts("dve", B_[:], iota128, idxT[:, tok, 1:2], None, ALU.is_equal, None, [self.misc, idxT], [B_])
                            self.mm(bg[:, k * 128:(k + 1) * 128], A_[:], B_[:], True, True, [A_, B_], bg)
                        self.cp("act", G[:, tq * 4:(tq + 1) * 4, :], bg[:].rearrange("p (a b) -> p a b", a=4), [bg], [G])
                    nc.leave_named_scope("E2g%d" % l, sid_g, False)
                    sid_m, _ = nc.enter_named_scope("E2m%d" % l, False)
                    def stage1(i1_):
                        r = ri + i1_
                        ub, vb, hg, W, bh = ubl[r % NR], vbl[r % NR], hgs[r % NR], Ws[r % NR], bhs[r % 2]
                        self.ld("sp", ub[:].rearrange("p c n -> p (c n)"), ub16[i1_], [ub16], [ub])
                        self.ld("sp", vb[:], vb16[i1_], [vb16], [vb])
                        for dc in range(8):
                            self.mm(bh[:, 0:nt], ub[:, dc, :], xbb[:, dc, 0:nt], dc == 0, dc == 7, [ub, xbb], bh)
                        self.act(hg[:, 0:nt], bh[:, 0:nt], ACT.Gelu, [bh], [hg])
                        self.tt("dve", W[:, 0:nt], hg[:, 0:nt], G[:, 0:nt, i1_], ALU.mult, [hg, G], [W])

                    def stage2(i1_):
                        r = ri + i1_
                        vb, W = vbl[r % NR], Ws[r % NR]
                        for c in range(8):
                            ab = accb[c // 2]
                            o = ab[:, (c % 2) * 256:(c % 2) * 256 + nt]
                            self.mm(o, vb[:, c * 128:(c + 1) * 128], W[:, 0:nt], (i1_ == 0 and c % 2 == 0), i1_ == 127, [vb, W], ab)

                    for i1_ in range(128):
                        stage1(i1_)
                        if i1_ >= 1:
                            stage2(i1_ - 1)
                    stage2(127)
                    ri += 128
                    nc.leave_named_scope("E2m%d" % l, sid_m, False)
                    for k in range(ntl):
                        n = n0 + k
                        y = ys[yi % 2]
                        lb = lnb[yi % 2]
                        yi += 1
                        for c in range(8):
                            ab = accb[c // 2]
                            o = ab[:, (c % 2) * 256 + k * 128:(c % 2) * 256 + (k + 1) * 128]
                            self.stt(y[:, c, :], self.xT[n][:, c, :], ALPHA, o, ALU.mult, ALU.add, [self.xT[n], ab], [y])
                        self.ln_fm(y, n, VF_L2G, VF_L2B, lb, bank=bhs[yi % 2])
                kb.barrier()


_CACHE = {}


def host_weights(inp, DEPTH, peer_mode="dense"):
    w = {}
    f = lambda a: np.ascontiguousarray(np.asarray(a, dtype=np.float32))
    w["w_in"] = f(inp["w_in"][:DEPTH])
    b_in = f(inp["b_in"][:DEPTH])
    w["b_in"] = b_in
    vec = np.zeros((DEPTH, 128, VF_N), np.float32)
    for l in range(DEPTH):
        b = b_in[l]
        vec[l, :64, VF_BQ:VF_BQ + 16] = b[C_QA:C_KA].reshape(16, 64).T
        vec[l, :64, VF_BQ + 16:VF_BQ + 18] = b[C_KA:C_VA].reshape(2, 64).T
        vec[l, :, VF_BQB:VF_BQB + 4] = b[C_QB:C_KB].reshape(4, 128).T
        vec[l, :, VF_BKB:VF_BKB + 4] = b[C_KB:C_VB].reshape(4, 128).T
        vec[l, :, VF_BGB:VF_BGB + 8] = b[C_GB:C_UC].reshape(8, 128).T
        vec[l, :, VF_BGT:VF_BGT + 24] = b[C_GT:INW].reshape(24, 128).T
        vec[l, :16, VF_BLR] = b[C_LR:C_GB]
        vec[l, :, VF_GNG:VF_GNG + 8] = np.asarray(inp["gla_norm_g"][l], np.float32).reshape(8, 128).T
        vec[l, :, VF_PSC:VF_PSC + 8] = np.asarray(inp["pool_scale"][l], np.float32).reshape(8, 128).T
        vec[l, :, VF_L1G:VF_L1G + 8] = np.asarray(inp["ln1_g"][l], np.float32).reshape(8, 128).T
        vec[l, :, VF_L1B:VF_L1B + 8] = np.asarray(inp["ln1_b"][l], np.float32).reshape(8, 128).T
        vec[l, :, VF_L2G:VF_L2G + 8] = np.asarray(inp["ln2_g"][l], np.float32).reshape(8, 128).T
        vec[l, :, VF_L2B:VF_L2B + 8] = np.asarray(inp["ln2_b"][l], np.float32).reshape(8, 128).T
    w["vecF"] = vec
    w["sinks"] = f(inp["attn_sinks"][:DEPTH])
    w["w_alpha"] = f(inp["w_alpha"][:DEPTH])
    w["b_alpha"] = f(inp["b_alpha"][:DEPTH])
    w["w_pool"] = f(inp["w_pool"][:DEPTH])
    w["w_ba"] = f(inp["w_branch_a"][:DEPTH])
    w["w_bb"] = f(inp["w_branch_b"][:DEPTH])
    w["w_bc"] = f(inp["w_branch_c"][:DEPTH])
    w["w_out"] = f(inp["w_out"][:DEPTH])
    w["pq"] = f(np.asarray(inp["peer_query"][:DEPTH]).reshape(DEPTH, 1024, 2048))
    sk = np.asarray(inp["peer_subkeys"][:DEPTH], np.float32).reshape(DEPTH, 16, 128, 128)
    w["skT"] = f(sk.transpose(0, 3, 1, 2))
    for l in range(DEPTH):
        if peer_mode == "dense":
            u = np.asarray(inp["peer_u"][l], np.float32).reshape(128, 128, 8, 128)
            w["puT%d" % l] = f(u.transpose(1, 3, 2, 0).reshape(128, 128, 1024))
            v = np.asarray(inp["peer_v"][l], np.float32).reshape(128, 128, 1024)
            w["pvp%d" % l] = f(v.transpose(1, 0, 2))
        else:
            w["pu%d" % l] = f(inp["peer_u"][l])
            w["pv%d" % l] = f(inp["peer_v"][l])
    return w


def core_inputs(inp, i, NTP, DEPTH):
    f = lambda a: np.ascontiguousarray(np.asarray(a, dtype=np.float32))
    T = NTP * 128 + 128
    xp = np.asarray(inp["x_prompt"][i, :NTP * 128], np.float32)
    xs = np.asarray(inp["x_sample"][16 * i:16 * i + 16], np.float32).reshape(128, D)
    x = np.concatenate([xp, xs], 0)
    m = {}
    m["xT"] = f(x.reshape(T, 8, 128).transpose(2, 1, 0))
    sl = slice(16 * i, 16 * i + 16)
    swk = np.asarray(inp["state_win_k"][:DEPTH, sl], np.float32)
    m["swkT"] = f(swk.transpose(0, 4, 1, 3, 2))
    swv = np.asarray(inp["state_win_v"][:DEPTH, sl], np.float32)
    m["swv"] = f(swv.transpose(0, 2, 1, 3, 4))
    sg = np.asarray(inp["state_gla"][:DEPTH, sl], np.float32)
    m["sgla"] = f(sg.transpose(0, 2, 3, 1, 4))
    sp = np.asarray(inp["state_pool"][:DEPTH, sl], np.float32)
    m["spool"] = f(sp.reshape(DEPTH, 240, 1024))
    return m


PEER_MODE = "dense"


def get_prog(NTP, DEPTH, dbg=(), phases="ABCDE"):
    key = (NTP, DEPTH, tuple(dbg), phases, PEER_MODE)
    if key not in _CACHE:
        p = Prog(NTP, DEPTH, dbg)
        p.phases = phases
        p.peer_mode = PEER_MODE
        p.build()
        _CACHE[key] = p
    return _CACHE[key]


def run(inp, NTP=16, DEPTH=2, n_cores=8, dbg=(), phases="ABCDE"):
    p = get_prog(NTP, DEPTH, dbg, phases)
    consts = make_consts(NTP)
    w = host_weights(inp, DEPTH, PEER_MODE)
    in_maps = []
    for i in range(n_cores):
        m = dict(w)
        m.update(consts)
        m.update(core_inputs(inp, i, NTP, DEPTH))
        in_maps.append(m)
    res = run_bass_kernel_spmd(p.nc, in_maps, core_ids=list(range(n_cores)))
    return p, res.results


def assemble(results, NTP, DEPTH, n_cores):
    Tp = NTP * 128
    yp = np.zeros((n_cores, Tp, D), np.float32)
    ys = np.zeros((n_cores * 16, 8, D), np.float32)
    pk = np.zeros((DEPTH, n_cores, 128, 2, 64), np.float32)
    pv = np.zeros((DEPTH, n_cores, 128, 2, 64), np.float32)
    pg = np.zeros((DEPTH, n_cores, 4, 128, 256), np.float32)
    pp = np.zeros((DEPTH, n_cores, 15, 1024), np.float32)
    sk = np.zeros((DEPTH, n_cores * 16, 128, 2, 64), np.float32)
    sv = np.zeros((DEPTH, n_cores * 16, 128, 2, 64), np.float32)
    sg = np.zeros((DEPTH, n_cores * 16, 4, 128, 256), np.float32)
    sp = np.zeros((DEPTH, n_cores * 16, 15, 1024), np.float32)
    for i, r in enumerate(results):
        y = np.asarray(r["yT"]).transpose(2, 1, 0).reshape(-1, D)
        yp[i] = y[:Tp]
        ys[16 * i:16 * i + 16] = y[Tp:].reshape(16, 8, D)
        sl = slice(16 * i, 16 * i + 16)
        pk[:, i] = np.asarray(r["o_pkT"]).transpose(0, 3, 2, 1)
        pv[:, i] = np.asarray(r["o_pv"]).reshape(DEPTH, 128, 2, 64)
        pg[:, i] = np.asarray(r["o_pg"])
        pp[:, i] = np.asarray(r["o_pp"])
        sk[:, sl] = np.asarray(r["o_skT"]).transpose(0, 2, 4, 3, 1)
        sv[:, sl] = np.asarray(r["o_sv"]).reshape(DEPTH, 16, 128, 2, 64)
        sg[:, sl] = np.asarray(r["o_sg"]).transpose(0, 3, 1, 2, 4)
        sp[:, sl] = np.asarray(r["o_sp"])
    return (yp, ys, pk, pv, pg, pp, sk, sv, sg, sp)


def kernel(**inputs):
    NTP, DEPTH, NC = 16, 2, 8
    _, results = run(inputs, NTP, DEPTH, NC)
    return assemble(results, NTP, DEPTH, NC)
```

```python
import numpy as np
from contextlib import ExitStack
import concourse.bass as bass
import concourse.mybir as mybir
from concourse.bass_utils import run_bass_kernel_spmd

F32 = mybir.dt.float32
BF16 = mybir.dt.bfloat16
I32 = mybir.dt.int32
U32 = mybir.dt.uint32
ACT = mybir.ActivationFunctionType
ALU = mybir.AluOpType
AX = mybir.AxisListType

D = 1024
INW = 8464
NEG = -30000.0
ALPHA = 4.0 ** 0.25
C_QA, C_KA, C_VA, C_QB, C_KB, C_VB, C_LR, C_GB, C_UC, C_GT = 0, 1024, 1152, 1280, 1792, 2304, 3328, 3344, 4368, 5392
VF_BQ, VF_BQB, VF_BKB, VF_BGB, VF_BGT, VF_BLR, VF_GNG, VF_PSC, VF_L1G, VF_L1B, VF_L2G, VF_L2B, VF_N = 0, 18, 22, 26, 34, 58, 59, 67, 75, 83, 91, 99, 107


class Buf:
    def __init__(self, t, name):
        self.t = t
        self.name = name
        self.w = None
        self.r = []

    def __getitem__(self, k):
        return self.t[k]


class KB:
    EPOCH = 3000

    def __init__(self, n_dma_sems=32):
        self.nc = bass.Bass("TRN2", target_bir_lowering=False)
        nc = self.nc
        self.es = None
        self.eng = {"pe": nc.tensor, "act": nc.scalar, "dve": nc.vector, "pool": nc.gpsimd, "sp": nc.sync}
        self.sems = {}
        self.cur = {}
        self.known = {e: {} for e in self.eng}
        self.n_dma_sems = n_dma_sems
        self.dma_rr = 0
        self.dma_tot = {}
        self.nsem = 0
        self.ninstr = 0

    def start(self, es):
        self.es = es
        for e in ("pe", "act", "dve", "pool"):
            self._new_epoch(e)
        for i in range(self.n_dma_sems):
            k = ("dma", i)
            self.sems[k] = es.enter_context(self.nc.semaphore("dsem%d" % i))
            self.dma_tot[k] = 0

    def _new_epoch(self, e):
        self.nsem += 1
        k = (e, self.nsem)
        self.sems[k] = self.es.enter_context(self.nc.semaphore("s_%s_%d" % (e, self.nsem)))
        self.cur[e] = [k, 0]

    def sb(self, name, shape, dtype, es=None):
        self.nalloc = getattr(self, "nalloc", 0) + 1
        name = "sb%d_%s" % (self.nalloc, name)
        t = (es or self.es).enter_context(self.nc.sbuf_tensor(name, list(shape), dtype))
        return Buf(t, name)

    def ps(self, name, shape, dtype=F32, es=None):
        t = (es or self.es).enter_context(self.nc.psum_tensor(name, list(shape), dtype))
        return Buf(t, name)

    def dram(self, name, shape, dtype, kind):
        t = self.nc.dram_tensor(name, list(shape), dtype, kind=kind)
        return Buf(t.ap(), name)

    def _wait(self, e, dep):
        k, v = dep
        kn = self.known[e]
        if kn.get(k, 0) >= v:
            return
        self.eng[e].wait_ge(self.sems[k], v)
        kn[k] = v

    FUSE_WAIT = True

    def _deps(self, e, reads, writes, fuse=False):
        m = {}
        for b in reads:
            if b.w is not None:
                k, v = b.w
                if m.get(k, 0) < v:
                    m[k] = v
        for b in writes:
            if b.w is not None:
                k, v = b.w
                if m.get(k, 0) < v:
                    m[k] = v
            for k, v in b.r:
                if m.get(k, 0) < v:
                    m[k] = v
        pend = []
        kn = self.known[e]
        for k, v in m.items():
            if e == "pe" and k[0] == "pe":
                continue
            if kn.get(k, 0) >= v:
                continue
            pend.append((k, v))
        if fuse and pend:
            for dep in pend[:-1]:
                self._wait(e, dep)
            return pend[-1]
        for dep in pend:
            self._wait(e, dep)
        return None

    def op(self, e, fn, reads=(), writes=()):
        last = self._deps(e, reads, writes, fuse=self.FUSE_WAIT)
        ins = fn()
        if last is not None:
            ins._wait_ge(self.sems[last[0]], last[1])
            self.known[e][last[0]] = last[1]
        k, c = self.cur[e]
        c += 1
        ins.then_inc(self.sems[k], 1)
        self.cur[e][1] = c
        tag = (k, c)
        for b in reads:
            b.r.append(tag)
            if len(b.r) > 64:
                b.r = self._compact(b.r)
        for b in writes:
            b.w = tag
            b.r = []
        self.ninstr += 1
        if c >= self.EPOCH:
            self._new_epoch(e)
        return ins

    @staticmethod
    def _compact(r):
        m = {}
        for k, v in r:
            if m.get(k, 0) < v:
                m[k] = v
        return list(m.items())

    def dma(self, q, fn, reads=(), writes=()):
        self._deps(q, reads, writes)
        k = ("dma", self.dma_rr)
        self.dma_rr = (self.dma_rr + 1) % self.n_dma_sems
        if self.dma_tot[k] > 0:
            self._wait(q, (k, self.dma_tot[k]))
        ins = fn()
        self.dma_tot[k] += 16
        ins.then_inc(self.sems[k], 16)
        tag = (k, self.dma_tot[k])
        for b in reads:
            b.r.append(tag)
        for b in writes:
            b.w = tag
            b.r = []
        self.ninstr += 1
        return tag

    def barrier(self):
        tags = []
        for e in ("pe", "act", "dve", "pool"):
            k, c = self.cur[e]
            if c > 0:
                tags.append((k, c))
        for k, v in self.dma_tot.items():
            if v > 0:
                tags.append((k, v))
        for e in self.eng:
            for t in tags:
                self._wait(e, t)


def make_consts(NTP):
    import ml_dtypes
    c = {}
    j = np.arange(128)[:, None]
    i = np.arange(128)[None, :]
    bj, tj = j // 8, j % 8
    bi, ti = i // 8, i % 8
    same = (bj == bi)

    def neg(ok):
        return np.where(ok, 0.0, NEG).astype(np.float32)

    att = np.zeros((4, 128, 512), np.float32)
    att[0] = np.tile(neg(j <= i), (1, 4))
    att[1] = np.tile(neg(j > i), (1, 4))
    att[2] = np.tile(neg(same & (tj <= ti)), (1, 4))
    t64 = (np.arange(64) % 8)[None, :]
    att[3] = np.tile(neg(j > t64), (1, 8))
    c["c_att"] = att.transpose(1, 0, 2).copy()
    gm = np.zeros((128, 2, 128), np.float32)
    gm[:, 0] = (j <= i)
    gm[:, 1] = same & (tj <= ti)
    c["c_gm"] = gm
    gu = np.zeros((128, 4, 128), np.float32)
    gu[:, 0] = np.where(j <= i, -1.0 / 16, 0.0)
    gu[:, 1] = np.where(j > i, -1.0 / 16, 0.0)
    gu[:, 2] = np.where(same & (tj <= ti), -1.0 / 16, 0.0)
    gu[:, 3] = np.where(same & (tj > ti), -1.0 / 16, 0.0)
    c["c_gu"] = gu
    ind = np.zeros((128, 16), np.float32)
    ind[np.arange(128), np.arange(128) // 8] = 1.0
    c["c_ind"] = ind
    pm = np.zeros((128, 6, 4, 128), np.float32)
    eye = (j == i).astype(np.float32)
    for g, w in enumerate((2, 4, 8, 16)):
        pm[:, 0, g] = np.where((j <= i) & (j > i - w), 1.0 / w, 0.0) - eye
        pm[:, 1, g] = np.where(j >= 129 + i - w, 1.0 / w, 0.0)
        cnt = np.minimum(i + 1, w).astype(np.float32)
        pm[:, 2, g] = np.where((j <= i) & (j > i - w), 1.0 / cnt, 0.0) - eye
        pm[:, 3, g] = np.where(same & (tj <= ti) & (tj > ti - w), 1.0 / w, 0.0) - eye
        rows = np.arange(240)[:, None]
        rb, rr = rows // 15, rows % 15
        mp = np.where((rb == bi) & (rr >= ti + 16 - w), 1.0 / w, 0.0)
        pm[:, 4, g] = mp[:128]
        pm[:112, 5, g] = mp[128:]
    c["c_pm"] = pm
    T = NTP * 128 + 128
    pos = np.concatenate([np.arange(NTP * 128), 16384 + (np.arange(128) % 8)]).astype(np.float32)
    inv = (np.float32(500000.0) ** (-np.arange(8, dtype=np.float32) / np.float32(8))).astype(np.float32)
    ang = (pos[None, :] * inv[:, None]).astype(np.float32)
    rc = np.ones((64, T), np.float32)
    rs = np.zeros((64, T), np.float32)
    rc[0:8] = np.cos(ang)
    rc[8:16] = np.cos(ang)
    rs[0:8] = -np.sin(ang)
    rs[8:16] = np.sin(ang)
    c["c_rope"] = np.stack([rc, rs], 1).copy()
    perm = np.zeros((64, 64), np.float32)
    for m in range(16):
        perm[(m + 8) % 16, m] = 1.0
    misc = np.zeros((128, 4, 128), np.float32)
    misc[:, 0] = np.eye(128)
    misc[:64, 1, :64] = perm
    misc[:, 2, :16] = np.arange(16)[None, :]
    misc[:, 3, :] = np.arange(128)[None, :]
    c["c_misc"] = misc
    return c


class Prog:
    def __init__(self, NTP=16, DEPTH=2, dbg=()):
        self.NTP = NTP
        self.NT = NTP + 1
        self.T = self.NT * 128
        self.DEPTH = DEPTH
        self.dbg = dbg
        self.kb = KB()
        self.nc = self.kb.nc
        self.dbg_outs = []
        self.phases = "ABCDE"
        self.peer_mode = "dense"

    def mm(self, out, lhsT, rhs, start, stop, reads, bank):
        nc = self.nc
        return self.kb.op("pe", lambda: nc.tensor.matmul(out, lhsT=lhsT, rhs=rhs, start=start, stop=stop), reads=reads, writes=[bank])

    def act(self, out, in_, func, reads, writes, bias=None, scale=1.0):
        nc = self.nc
        if bias is None:
            return self.kb.op("act", lambda: nc.scalar.activation(out=out, in_=in_, func=func, scale=scale), reads=reads, writes=writes)
        return self.kb.op("act", lambda: nc.scalar.activation(out=out, in_=in_, func=func, bias=bias, scale=scale), reads=reads, writes=writes)

    def tt(self, e, out, in0, in1, op, reads, writes):
        eng = self.kb.eng[e]
        return self.kb.op(e, lambda: eng.tensor_tensor(out=out, in0=in0, in1=in1, op=op), reads=reads, writes=writes)

    def ts(self, e, out, in0, s1, s2, op0, op1, reads, writes):
        eng = self.kb.eng[e]
        if op1 is None:
            return self.kb.op(e, lambda: eng.tensor_scalar(out=out, in0=in0, scalar1=s1, scalar2=None, op0=op0), reads=reads, writes=writes)
        return self.kb.op(e, lambda: eng.tensor_scalar(out=out, in0=in0, scalar1=s1, scalar2=s2, op0=op0, op1=op1), reads=reads, writes=writes)

    def stt(self, out, in0, scalar, in1, op0, op1, reads, writes, accum_out=None):
        nc = self.nc
        if accum_out is None:
            return self.kb.op("dve", lambda: nc.vector.scalar_tensor_tensor(out=out, in0=in0, scalar=scalar, in1=in1, op0=op0, op1=op1), reads=reads, writes=writes)
        return self.kb.op("dve", lambda: nc.vector.scalar_tensor_tensor(out=out, in0=in0, scalar=scalar, in1=in1, op0=op0, op1=op1, accum_out=accum_out), reads=reads, writes=writes)

    def cp(self, e, out, in_, reads, writes):
        if e == "act":
            nc = self.nc
            return self.kb.op("act", lambda: nc.scalar.copy(out=out, in_=in_), reads=reads, writes=writes)
        eng = self.kb.eng[e]
        return self.kb.op(e, lambda: eng.tensor_copy(out=out, in_=in_), reads=reads, writes=writes)

    def ld(self, q, out, in_, reads, writes):
        eng = self.kb.eng[q]
        return self.kb.dma(q, lambda: eng.dma_start(out=out, in_=in_), reads=reads, writes=writes)

    def bank(self):
        b = self.banks[self.bank_i]
        self.bank_i = (self.bank_i + 1) % 8
        return b

    def dump(self, name, buf, ap, shape, dtype=F32):
        d = self.kb.dram("dbg_" + name, list(shape), dtype, "ExternalOutput")
        self.ld("sp", d[:], ap, [buf], [d])
        self.dbg_outs.append("dbg_" + name)

    def build(self):
        kb, nc = self.kb, self.nc
        NTP, NT, T, DEPTH = self.NTP, self.NT, self.T, self.DEPTH
        I = lambda n, s, dt=F32: kb.dram(n, s, dt, "ExternalInput")
        O = lambda n, s, dt=F32: kb.dram(n, s, dt, "ExternalOutput")
        self.d = d = {}
        d["xT"] = I("xT", [128, 8, T])
        d["swkT"] = I("swkT", [DEPTH, 64, 16, 2, 128])
        d["swv"] = I("swv", [DEPTH, 128, 16, 2, 64])
        d["sgla"] = I("sgla", [DEPTH, 4, 128, 16, 256])
        d["spool"] = I("spool", [DEPTH, 240, 1024])
        d["w_in"] = I("w_in", [DEPTH, 1024, INW])
        d["vecF"] = I("vecF", [DEPTH, 128, VF_N])
        d["b_in"] = I("b_in", [DEPTH, INW])
        d["sinks"] = I("sinks", [DEPTH, 16])
        d["w_alpha"] = I("w_alpha", [DEPTH, 16, 512])
        d["b_alpha"] = I("b_alpha", [DEPTH, 512])
        d["w_pool"] = I("w_pool", [DEPTH, 4, 256, 256])
        d["w_ba"] = I("w_ba", [DEPTH, 1024, 1024])
        d["w_bb"] = I("w_bb", [DEPTH, 1024, 1024])
        d["w_bc"] = I("w_bc", [DEPTH, 1024, 1024])
        d["w_out"] = I("w_out", [DEPTH, 1024, 1024])
        d["pq"] = I("pq", [DEPTH, 1024, 2048])
        d["skT"] = I("skT", [DEPTH, 128, 16, 128])
        for l_ in range(DEPTH):
            if self.peer_mode == "dense":
                d["puT%d" % l_] = I("puT%d" % l_, [128, 128, 1024])
                d["pvp%d" % l_] = I("pvp%d" % l_, [128, 128, 1024])
                d["ub16_%d" % l_] = kb.dram("ub16_%d" % l_, [128, 128, 1024], BF16, "Internal")
                d["vb16_%d" % l_] = kb.dram("vb16_%d" % l_, [128, 128, 1024], BF16, "Internal")
            else:
                d["pu%d" % l_] = I("pu%d" % l_, [16384, 1024])
                d["pv%d" % l_] = I("pv%d" % l_, [16384, 1024])
        for k, s in (("c_att", [128, 4, 512]), ("c_gm", [128, 2, 128]), ("c_gu", [128, 4, 128]), ("c_ind", [128, 16]),
                     ("c_pm", [128, 6, 4, 128]), ("c_rope", [64, 2, T]), ("c_misc", [128, 4, 128])):
            d[k] = I(k, s)
        d["yT"] = O("yT", [128, 8, T])
        d["o_pkT"] = O("o_pkT", [DEPTH, 64, 2, 128])
        d["o_pv"] = O("o_pv", [DEPTH, 128, 128])
        d["o_pg"] = O("o_pg", [DEPTH, 4, 128, 256])
        d["o_pp"] = O("o_pp", [DEPTH, 15, 1024])
        d["o_skT"] = O("o_skT", [DEPTH, 64, 16, 2, 128])
        d["o_sv"] = O("o_sv", [DEPTH, 16, 128, 128])
        d["o_sg"] = O("o_sg", [DEPTH, 4, 128, 16, 256])
        d["o_sp"] = O("o_sp", [DEPTH, 16, 15, 1024])

        with ExitStack() as es:
            kb.start(es)
            self.banks = [kb.ps("bank%d" % i, [128, 512]) for i in range(8)]
            self.bank_i = 0
            self.xT_t = es.enter_context(nc.sbuf_tensor("xTres", [128, 8, T], F32))
            self.xT = [Buf(self.xT_t[:, :, n * 128:(n + 1) * 128], "xT%d" % n) for n in range(NT)]
            self.misc = kb.sb("misc", [128, 4, 128], F32)
            self.ident = self.misc[:, 0, :]
            self.cbf = kb.sb("cbf", [128, 4, 128], BF16)
            self.vecF = kb.sb("vecF", [128, VF_N], F32)
            self.ld("sp", self.misc[:], d["c_misc"][:], [d["c_misc"]], [self.misc])
            self.cp("dve", self.cbf[:, 0, :], self.misc[:, 0, :], [self.misc], [self.cbf])
            kb.op("dve", lambda: nc.vector.memset(self.cbf[:, 1, :], 1.0), writes=[self.cbf])
            kb.op("dve", lambda: nc.vector.memset(self.cbf[:, 2, :], 1.0 / 1024), writes=[self.cbf])
            kb.op("dve", lambda: nc.vector.memset(self.cbf[:, 3, :], 1.0 / 256), writes=[self.cbf])
            for n in range(NT):
                self.ld("sp", self.xT[n][:], d["xT"][:, :, n * 128:(n + 1) * 128], [d["xT"]], [self.xT[n]])
            for l in range(DEPTH):
                self.layer(l)
            for n in range(NT):
                self.ld("sp", d["yT"][:, :, n * 128:(n + 1) * 128], self.xT[n][:], [self.xT[n]], [d["yT"]])
            kb.barrier()
        return self

    def cast_x(self, n, pool):
        xb = pool[self.xb_i % len(pool)]
        self.xb_i += 1
        self.cp("dve", xb[:], self.xT[n][:], [self.xT[n]], [xb])
        return xb

    def load_w(self, dst, src_ap, src):
        self.ld("pool", dst[:], src_ap, [src], [dst])

    def gate_and_merge(self, l, n, xb, Wg, gcol, Wbr, orows, oT, first, es_sig):
        sig, tmp = es_sig
        for half in range(2):
            bb = self.bank()
            bg = self.bank()
            for c4 in range(4):
                c = half * 4 + c4
                nk = len(orows)
                for ki, k in enumerate(orows):
                    self.mm(bb[:, c4 * 128:(c4 + 1) * 128], Wbr[:, k, c * 128:(c + 1) * 128], oT[:, ki, :], ki == 0, ki == nk - 1, [Wbr, oT], bb)
                for dc in range(8):
                    self.mm(bg[:, c4 * 128:(c4 + 1) * 128], Wg[:, dc, c * 128:(c + 1) * 128], xb[:, dc, :], dc == 0, dc == 7, [Wg, xb], bg)
            for c4 in range(4):
                c = half * 4 + c4
                self.act(sig[:, c4, :], bg[:, c4 * 128:(c4 + 1) * 128], ACT.Sigmoid, [bg, self.vecF], [sig],
                         bias=self.vecF[:, VF_BGT + gcol * 8 + c:VF_BGT + gcol * 8 + c + 1])
            mslice = self.merged[n][:, half * 4:(half + 1) * 4, :]
            bv = bb[:].rearrange("p (a b) -> p a b", a=4)
            if first:
                self.tt("dve", mslice, bv, sig[:], ALU.mult, [bb, sig], [self.merged[n]])
            else:
                self.tt("dve", tmp[:], bv, sig[:], ALU.mult, [bb, sig], [tmp])
                self.tt("pool", mslice, mslice, tmp[:], ALU.add, [tmp, self.merged[n]], [self.merged[n]])

    def layer(self, l):
        kb, nc, d = self.kb, self.nc, self.d
        NTP, NT = self.NTP, self.NT
        kb.barrier()
        self.ld("sp", self.vecF[:], d["vecF"][l], [d["vecF"]], [self.vecF])
        with ExitStack() as les:
            mt = les.enter_context(nc.sbuf_tensor("merged%d" % l, [128, 8, self.T], BF16))
            self.merged = [Buf(mt[:, :, n * 128:(n + 1) * 128], "mg%d" % n) for n in range(NT)]
            if "A" in self.phases:
                with nc.named_scope("A%d" % l):
                    self.phase_attn(l)
            if "B" in self.phases:
                for hh in range(4):
                    with nc.named_scope("B%d_%d" % (l, hh)):
                        self.phase_gla(l, hh)
            if "C" in self.phases:
                with nc.named_scope("C%d" % l):
                    self.phase_pool(l)
            if "D" in self.phases:
                with nc.named_scope("D%d" % l):
                    self.phase_out(l)
            kb.barrier()
        if "E" in self.phases:
            with nc.named_scope("E%d" % l):
                if self.peer_mode == "dense":
                    self.phase_peer_dense(l)
                else:
                    self.phase_peer(l)

    def phase_attn(self, l):
        kb, nc, d = self.kb, self.nc, self.d
        NTP, NT = self.NTP, self.NT
        kb.barrier()
        with ExitStack() as es:
            sb = lambda n, s, dt: kb.sb(n, s, dt, es)
            Wqk = sb("Wqk", [128, 8, 1152], BF16)
            Wv = sb("Wv", [128, 8, 128], BF16)
            Wg = sb("WgA", [128, 8, 1024], BF16)
            Wa = sb("Wa", [128, 8, 1024], BF16)
            win = d["w_in"]
            wv = lambda c0, c1: win[l, :, c0:c1].rearrange("(c p) n -> p c n", p=128)
            self.load_w(Wqk, wv(C_QA, C_VA), win)
            self.load_w(Wv, wv(C_VA, C_QB), win)
            self.load_w(Wg, wv(C_GT, C_GT + 1024), win)
            self.load_w(Wa, d["w_ba"][l].rearrange("(c p) n -> p c n", p=128), d["w_ba"])
            amask = sb("amask", [128, 4, 512], BF16)
            self.load_w(amask, d["c_att"][:], d["c_att"])
            bva = sb("bva", [128, 128], F32)
            self.ld("sp", bva[:], d["b_in"][l, C_VA:C_QB].partition_broadcast(128), [d["b_in"]], [bva])
            esk = sb("esk", [128, 16], F32)
            self.ld("sp", esk[:], d["sinks"][l, :].partition_broadcast(128), [d["sinks"]], [esk])
            self.act(esk[:], esk[:], ACT.Exp, [esk], [esk])
            xbp = [sb("xbA%d" % i, [128, 8, 128], BF16) for i in range(2)]
            self.xb_i = 0
            rope = [sb("rope%d" % i, [64, 2, 128], F32) for i in range(2)]
            qf = [sb("qf%d" % i, [64, 4, 128], F32) for i in range(2)]
            t2 = [sb("t2%d" % i, [64, 4, 128], F32) for i in range(2)]
            QTs = sb("QTs", [64, 16, 128], BF16)
            KTs = [sb("KT%d" % i, [64, 2, 128], BF16) for i in range(2)]
            krf = sb("krf", [64, 2, 128], F32)
            Vd = [sb("Vd%d" % i, [128, 2, 2, 64], BF16) for i in range(2)]
            vf = sb("vf", [128, 128], F32)
            Pown = [sb("Pown%d" % i, [128, 512], BF16) for i in range(2)]
            Pprev = [sb("Pprev%d" % i, [128, 512], BF16) for i in range(2)]
            rden = [sb("rden%d" % i, [128, 512], F32) for i in range(1)]
            oaT = [sb("oaT%d" % i, [128, 8, 128], BF16) for i in range(2)]
            sig = [sb("sigA%d" % i, [128, 4, 128], F32) for i in range(1)]
            kvb = sb("kvb", [128, 4096], BF16)
            kbT = kvb[0:64, :].rearrange("p (b g t) -> p b g t", b=16, g=2)
            vbd = kvb[:, :].rearrange("p (b g a e) -> p b g a e", b=16, g=2, a=2)
            Psp = sb("Psp", [128, 32, 64], BF16)
            perm = self.misc[0:64, 1, 0:64]
            pi = 0
            for n in range(NT):
                samp = (n == NTP)
                first = (n == 0)
                xb = self.cast_x(n, xbp)
                rp = rope[n % 2]
                self.ld("sp", rp[:], d["c_rope"][:, :, n * 128:(n + 1) * 128], [d["c_rope"]], [rp])
                Q = QTs
                KT = KTs[n % 2]
                KTp = KTs[(n + 1) % 2]
                for hg in range(5):
                    nh = 4 if hg < 4 else 2
                    bq = self.bank()
                    for hh in range(nh):
                        h = hg * 4 + hh
                        for dc in range(8):
                            self.mm(bq[0:64, hh * 128:(hh + 1) * 128], Wqk[:, dc, h * 64:(h + 1) * 64], xb[:, dc, :], dc == 0, dc == 7, [Wqk, xb], bq)
                    q_ = qf[pi % 2]
                    t_ = t2[pi % 2]
                    pi += 1
                    for hh in range(nh):
                        h = hg * 4 + hh
                        self.act(q_[:, hh, :], bq[0:64, hh * 128:(hh + 1) * 128], ACT.Identity, [bq, self.vecF], [q_],
                                 bias=self.vecF[0:64, VF_BQ + h:VF_BQ + h + 1])
                    bp = self.bank()
                    self.mm(bp[0:64, 0:nh * 128], perm, q_[:, 0:nh, :], True, True, [self.misc, q_], bp)
                    cb = rp[:, 0, :].unsqueeze(1).to_broadcast([64, nh, 128])
                    sbb = rp[:, 1, :].unsqueeze(1).to_broadcast([64, nh, 128])
                    self.tt("dve", t_[:, 0:nh, :], bp[0:64, 0:nh * 128].rearrange("p (a b) -> p a b", a=nh), sbb, ALU.mult, [bp, rp], [t_])
                    self.tt("pool", q_[:, 0:nh, :], q_[:, 0:nh, :], cb, ALU.mult, [q_, rp], [q_])
                    if hg < 4:
                        self.tt("dve", Q[:, hg * 4:hg * 4 + nh, :], q_[:, 0:nh, :], t_[:, 0:nh, :], ALU.add, [q_, t_], [Q])
                    else:
                        self.tt("dve", KT[:], q_[:, 0:nh, :], t_[:, 0:nh, :], ALU.add, [q_, t_], [KT])
                    if hg == 4 and (n == NTP - 1 or samp):
                        self.tt("pool", krf[:], q_[:, 0:2, :], t_[:, 0:2, :], ALU.add, [q_, t_], [krf])
                bv = self.bank()
                for dc in range(8):
                    self.mm(bv[:, 0:128], xb[:, dc, :], Wv[:, dc, :], dc == 0, dc == 7, [xb, Wv], bv)
                V = Vd[n % 2]
                Vp = Vd[(n + 1) % 2]
                bvv = bv[:, 0:128].rearrange("p (g e) -> p g e", g=2).unsqueeze(2).to_broadcast([128, 2, 2, 64])
                bia = bva[:].rearrange("p (g e) -> p g e", g=2).unsqueeze(2).to_broadcast([128, 2, 2, 64])
                self.tt("dve", V[:], bvv, bia, ALU.add, [bv, bva], [V])
                if n == NTP - 1 or samp:
                    self.tt("dve", vf[:], bv[:, 0:128], bva[:], ALU.add, [bv, bva], [vf])
                if n == NTP - 1:
                    self.ld("sp", d["o_pkT"][l], krf[:], [krf], [d["o_pkT"]])
                    self.ld("sp", d["o_pv"][l], vf[:], [vf], [d["o_pv"]])
                if samp:
                    self.ld("sp", d["o_skT"][l, :, :, :, 0:120], d["swkT"][l, :, :, :, 8:128], [d["swkT"]], [d["o_skT"]])
                    for g_ in range(2):
                        self.ld("sp", d["o_skT"][l, :, :, g_, 120:128], krf[:, g_, :].rearrange("e (b t) -> e b t", t=8), [krf], [d["o_skT"]])
                    self.ld("sp", d["o_sv"][l, :, 0:120, :], d["swv"][l, 8:128].rearrange("p b g e -> b p (g e)"), [d["swv"]], [d["o_sv"]])
                    for b in range(16):
                        self.ld("sp", d["o_sv"][l, b, 120:128, :], vf[8 * b:8 * b + 8, :], [vf], [d["o_sv"]])
                    kb.dma("pool", lambda: nc.gpsimd.dma_start(out=kbT, in_=d["swkT"][l]), reads=[d["swkT"]], writes=[kvb])
                    for q4 in range(4):
                        bs = self.bank()
                        self.mm(bs[:, :], self.cbf[:, 0, :], amask[:, 3, :], True, False, [self.cbf, amask], bs)
                        for bl in range(8):
                            blk = q4 * 8 + bl
                            b, g = blk // 2, blk % 2
                            rhs = Q[:, 8 * g:8 * g + 8, 8 * b:8 * b + 8]
                            self.mm(bs[:, bl * 64:(bl + 1) * 64], kbT[:, b, g, :], rhs, False, bl == 7, [kvb, Q], bs)
                        self.act(Psp[:, q4 * 8:(q4 + 1) * 8, :], bs[:].rearrange("p (a b) -> p a b", a=8), ACT.Exp, [bs], [Psp], scale=0.125)
                    for a_ in range(2):
                        kb.dma("pool", lambda: nc.gpsimd.dma_start(out=vbd[:, :, :, a_, :], in_=d["swv"][l]), reads=[d["swv"]], writes=[kvb])
                for hg in range(4):
                    g = hg // 2
                    Po = Pown[hg % 2]
                    Pp = Pprev[hg % 2]
                    rd = rden[0]
                    bo = self.bank()
                    rq = Q[:, hg * 4:(hg + 1) * 4, :]
                    self.mm(bo[:], KT[:, g, :], rq, True, False, [KT, Q], bo)
                    self.mm(bo[:], self.cbf[:, 0, :], amask[:, 2 if samp else 0, :], False, True, [self.cbf, amask], bo)
                    self.act(Po[:], bo[:], ACT.Exp, [bo], [Po], scale=0.125)
                    use_prev = (not first) and (not samp)
                    if use_prev:
                        bpv = self.bank()
                        self.mm(bpv[:], KTp[:, g, :], rq, True, False, [KTp, Q], bpv)
                        self.mm(bpv[:], self.cbf[:, 0, :], amask[:, 1, :], False, True, [self.cbf, amask], bpv)
                        self.act(Pp[:], bpv[:], ACT.Exp, [bpv], [Pp], scale=0.125)
                    bO = self.bank()
                    bD = self.bank()
                    self.mm(bO[:], V[:, g].rearrange("p a e -> p (a e)"), Po[:], True, (not use_prev) and (not samp), [V, Po], bO)
                    if use_prev:
                        self.mm(bO[:], Vp[:, g].rearrange("p a e -> p (a e)"), Pp[:], False, True, [Vp, Pp], bO)
                    self.mm(bD[:], self.cbf[:, 1, :], Po[:], True, (not use_prev) and (not samp), [self.cbf, Po], bD)
                    if use_prev:
                        self.mm(bD[:], self.cbf[:, 1, :], Pp[:], False, True, [self.cbf, Pp], bD)
                    if samp:
                        hl = (hg % 2) * 4
                        for b in range(16):
                            rhs = Psp[:, b * 2 + g, hl * 8:(hl + 4) * 8]
                            oO = bO[:].rearrange("p (a t) -> p a t", a=4)[:, :, 8 * b:8 * b + 8]
                            oD = bD[:].rearrange("p (a t) -> p a t", a=4)[:, :, 8 * b:8 * b + 8]
                            self.mm(oO, vbd[:, b, g].rearrange("p a e -> p (a e)"), rhs, False, b == 15, [kvb, Psp], bO)
                            self.mm(oD, self.cbf[:, 1, :], rhs, False, b == 15, [self.cbf, Psp], bD)
                    else:
                        pass
                    for hh_ in range(4):
                        self.ts("dve", rd[:, hh_ * 128:(hh_ + 1) * 128], bD[:, hh_ * 128:(hh_ + 1) * 128], esk[:, hg * 4 + hh_:hg * 4 + hh_ + 1], None, ALU.add, None, [bD, esk], [rd])
                    kb.op("dve", lambda: nc.vector.reciprocal(out=rd[:], in_=rd[:]), reads=[rd], writes=[rd])
                    oa = oaT[n % 2]
                    bO3 = bO[:].rearrange("p (a t) -> p a t", a=4)
                    rd3 = rd[:].rearrange("p (a t) -> p a t", a=4)
                    self.tt("dve", oa[0:64, hg * 2:hg * 2 + 2, :], bO3[0:64, 0:4:2, :], rd3[0:64, 0:4:2, :], ALU.mult, [bO, rd], [oa])
                    self.tt("dve", oa[64:128, hg * 2:hg * 2 + 2, :], bO3[64:128, 1:4:2, :], rd3[64:128, 1:4:2, :], ALU.mult, [bO, rd], [oa])
                if "oa" in self.dbg:
                    self.dump("oa_%d_%d" % (l, n), oaT[n % 2], oaT[n % 2][:], [128, 8, 128], BF16)
                self.gate_and_merge(l, n, xb, Wg, 0, Wa, list(range(8)), oaT[n % 2], True, (sig[0], None))
            kb.barrier()

    def phase_gla(self, l, hh):
        kb, nc, d = self.kb, self.nc, self.d
        NTP, NT = self.NTP, self.NT
        kb.barrier()
        with ExitStack() as es:
            sb = lambda n, s, dt: kb.sb(n, s, dt, es)
            win = d["w_in"]
            wv = lambda c0, c1: win[l, :, c0:c1].rearrange("(c p) n -> p c n", p=128)
            Wq = sb("Wq", [128, 8, 128], BF16)
            Wk = sb("Wk", [128, 8, 128], BF16)
            Wv = sb("WvB", [128, 8, 256], BF16)
            Wgb = sb("Wgb", [128, 8, 256], BF16)
            Wlr = sb("Wlr", [128, 8, 16], BF16)
            Wal = sb("Wal", [16, 128], F32)
            Wg = sb("WgB", [128, 8, 1024], BF16)
            Wb = sb("Wb", [128, 2, 1024], BF16)
            self.load_w(Wq, wv(C_QB + hh * 128, C_QB + (hh + 1) * 128), win)
            self.load_w(Wk, wv(C_KB + hh * 128, C_KB + (hh + 1) * 128), win)
            self.load_w(Wv, wv(C_VB + hh * 256, C_VB + (hh + 1) * 256), win)
            self.load_w(Wgb, wv(C_GB + hh * 256, C_GB + (hh + 1) * 256), win)
            self.load_w(Wlr, wv(C_LR, C_LR + 16), win)
            self.load_w(Wg, wv(C_GT + 1024, C_GT + 2048), win)
            self.load_w(Wb, d["w_bb"][l, hh * 256:(hh + 1) * 256, :].rearrange("(c p) n -> p c n", p=128), d["w_bb"])
            self.ld("sp", Wal[:], d["w_alpha"][l, :, hh * 128:(hh + 1) * 128], [d["w_alpha"]], [Wal])
            bkb = sb("bkb", [128, 128], F32)
            bvb = sb("bvb", [128, 256], F32)
            bal = sb("bal", [128, 128], F32)
            self.ld("sp", bkb[:], d["b_in"][l, C_KB + hh * 128:C_KB + (hh + 1) * 128].partition_broadcast(128), [d["b_in"]], [bkb])
            self.ld("sp", bvb[:], d["b_in"][l, C_VB + hh * 256:C_VB + (hh + 1) * 256].partition_broadcast(128), [d["b_in"]], [bvb])
            self.ld("sp", bal[:], d["b_alpha"][l, hh * 128:(hh + 1) * 128].partition_broadcast(128), [d["b_alpha"]], [bal])
            gm = sb("gm", [128, 2, 128], BF16)
            gu = sb("gu", [128, 4, 128], F32)
            ind = sb("ind", [128, 16], F32)
            self.load_w(gm, d["c_gm"][:], d["c_gm"])
            self.ld("sp", gu[:], d["c_gu"][:], [d["c_gu"]], [gu])
            self.ld("sp", ind[:], d["c_ind"][:], [d["c_ind"]], [ind])
            S = sb("S", [128, 256], F32)
            Sbs = [sb("Sb%d" % i, [128, 256], BF16) for i in range(2)]
            kb.op("dve", lambda: nc.vector.memset(S[:], 0.0), writes=[S])
            kb.op("dve", lambda: nc.vector.memset(Sbs[1][:], 0.0), writes=[Sbs[1]])
            S0 = sb("S0", [128, 16, 256], F32)
            S0b = sb("S0b", [128, 16, 256], BF16)
            QM = sb("QM", [128, 16, 128], BF16)
            kb.op("pool", lambda: nc.gpsimd.memset(QM[:], 0.0), writes=[QM])
            xbp = [sb("xbB%d" % i, [128, 8, 128], BF16) for i in range(2)]
            self.xb_i = 0
            gla_specs = (("lrT", [16, 128], F32), ("zb", [128, 128], F32), ("lsp", [128, 128], F32), ("ebs", [128, 128], F32),
                         ("enb", [128, 128], F32), ("eb", [128, 128], F32), ("erb", [128, 128], F32), ("qd", [128, 128], BF16),
                         ("kd", [128, 128], BF16), ("ktm", [128, 128], F32), ("kl", [128, 128], BF16), ("vB", [128, 256], BF16),
                         ("attm", [128, 128], BF16), ("sq", [128, 256], BF16), ("sd", [128, 128], F32), ("rstd", [128, 128], F32),
                         ("gsl", [128, 2, 128], F32), ("otmp", [128, 128], F32))
            gla_sets = [[sb("%s_%d" % (nm, i), shp, dt) for (nm, shp, dt) in gla_specs] for i in range(2)]
            klm = [sb("klm%d" % i, [128, 128], BF16) for i in range(2)]
            obT = [sb("obT%d" % i, [128, 2, 128], BF16) for i in range(2)]
            mtmp = [sb("mtmpB%d" % i, [128, 4, 128], F32) for i in range(2)]
            lnscale = float(np.log(128.0 ** -0.5))
            lnsc = sb("lnsc", [128, 1], F32)
            eps6 = sb("eps6", [128, 1], F32)
            kb.op("dve", lambda: nc.vector.memset(lnsc[:], lnscale), writes=[lnsc])
            kb.op("dve", lambda: nc.vector.memset(eps6[:], 1e-6), writes=[eps6])
            sig8 = [sb("sig8_%d" % i, [128, 8, 128], F32) for i in range(2)]

            def S1(n):
                samp = (n == NTP)
                (lrT, zb, lsp, ebs, enb, eb, erb, qd, kd, ktm, kl, v, attm, sq, sd, rstd, gsl, otmp) = gla_sets[n % 2]
                xb = self.cast_x(n, xbp)
                if samp:
                    self.ld("sp", S0[:], d["sgla"][l, hh], [d["sgla"]], [S0])
                    self.load_w(S0b, d["sgla"][l, hh], d["sgla"])
                b1 = self.bank()
                for dc in range(8):
                    self.mm(b1[0:16, 0:128], Wlr[:, dc, :], xb[:, dc, :], dc == 0, dc == 7, [Wlr, xb], b1)
                self.act(lrT[:], b1[0:16, 0:128], ACT.Identity, [b1, self.vecF], [lrT], bias=self.vecF[0:16, VF_BLR:VF_BLR + 1])
                b2 = self.bank()
                self.mm(b2[:, 0:128], lrT[:], Wal[:], True, True, [lrT, Wal], b2)
                self.tt("dve", zb[:], b2[:, 0:128], bal[:], ALU.add, [b2, bal], [zb])
                self.act(zb[:], zb[:], ACT.Exp, [zb], [zb], scale=-1.0)
                self.act(lsp[:], zb[:], ACT.Ln, [zb], [lsp], bias=1.0)
                kU = 2 if samp else 0
                b3 = self.bank()
                self.mm(b3[:, 0:128], lsp[:], gu[:, kU, :], True, True, [lsp, gu], b3)
                self.mm(b3[:, 128:256], gu[:, kU + 1, :], lsp[:], True, True, [lsp, gu], b3)
                self.act(ebs[:], b3[:, 0:128], ACT.Exp, [b3, lnsc], [ebs], bias=lnsc[:, 0:1])
                self.act(enb[:], b3[:, 0:128], ACT.Exp, [b3], [enb], scale=-1.0)
                self.act(eb[:], b3[:, 0:128], ACT.Exp, [b3], [eb])
                self.act(erb[:], b3[:, 128:256], ACT.Exp, [b3], [erb])
                b4 = self.bank()
                for dc in range(8):
                    self.mm(b4[:, 0:128], Wq[:, dc, :], xb[:, dc, :], dc == 0, dc == 7, [Wq, xb], b4)
                for dc in range(8):
                    self.mm(b4[:, 128:256], Wk[:, dc, :], xb[:, dc, :], dc == 0, dc == 7, [Wk, xb], b4)
                self.stt(qd[:], b4[:, 0:128], self.vecF[:, VF_BQB + hh:VF_BQB + hh + 1], ebs[:], ALU.add, ALU.mult, [b4, self.vecF, ebs], [qd])
                self.stt(kd[:], b4[:, 128:256], self.vecF[:, VF_BKB + hh:VF_BKB + hh + 1], enb[:], ALU.add, ALU.mult, [b4, self.vecF, enb], [kd])
                b5 = self.bank()
                for dc in range(8):
                    self.mm(b5[:, 0:128], xb[:, dc, :], Wk[:, dc, :], dc == 0, dc == 7, [Wk, xb], b5)
                for dc in range(8):
                    self.mm(b5[:, 128:384], xb[:, dc, :], Wv[:, dc, :], dc == 0, dc == 7, [Wv, xb], b5)
                self.tt("dve", ktm[:], b5[:, 0:128], bkb[:], ALU.add, [b5, bkb], [ktm])
                self.tt("pool", kl[:], ktm[:], erb[:], ALU.mult, [ktm, erb], [kl])
                self.tt("dve", v[:], b5[:, 128:384], bvb[:], ALU.add, [b5, bvb], [v])
                if samp:
                    qm_diag = bass.AP(tensor=QM.t, offset=0, ap=[[16 * 128, 128], [128 + 8, 16], [1, 8]])
                    self.cp("dve", qm_diag, qd[:].rearrange("p (b t) -> p b t", t=8), [qd], [QM])
                b9 = self.bank()
                for dvc in range(2):
                    for dc in range(8):
                        self.mm(b9[:, dvc * 128:(dvc + 1) * 128], Wgb[:, dc, dvc * 128:(dvc + 1) * 128], xb[:, dc, :], dc == 0, dc == 7, [Wgb, xb], b9)
                for dvc in range(2):
                    cidx = VF_BGB + hh * 2 + dvc
                    self.act(gsl[:, dvc, :], b9[:, dvc * 128:(dvc + 1) * 128], ACT.Silu, [b9, self.vecF], [gsl], bias=self.vecF[:, cidx:cidx + 1])
                sg = sig8[n % 2]
                for half in range(2):
                    bg = self.bank()
                    for c4 in range(4):
                        c = half * 4 + c4
                        for dc in range(8):
                            self.mm(bg[:, c4 * 128:(c4 + 1) * 128], Wg[:, dc, c * 128:(c + 1) * 128], xb[:, dc, :], dc == 0, dc == 7, [Wg, xb], bg)
                    for c4 in range(4):
                        c = half * 4 + c4
                        self.act(sg[:, c, :], bg[:, c4 * 128:(c4 + 1) * 128], ACT.Sigmoid, [bg, self.vecF], [sg],
                                 bias=self.vecF[:, VF_BGT + 8 + c:VF_BGT + 8 + c + 1])

            def S2(n):
                samp = (n == NTP)
                (lrT, zb, lsp, ebs, enb, eb, erb, qd, kd, ktm, kl, v, attm, sq, sd, rstd, gsl, otmp) = gla_sets[n % 2]
                Sb_prev = Sbs[(n + 1) % 2]
                if not samp:
                    b10 = self.bank()
                    self.mm(b10[:, 0:256], kl[:], v[:], True, True, [kl, v], b10)
                    self.stt(S[:], S[:], eb[:, 127:128], b10[:, 0:256], ALU.mult, ALU.add, [S, eb, b10], [S])
                    self.cp("pool", Sbs[n % 2][:], S[:], [S], [Sbs[n % 2]])
                    if n == NTP - 1:
                        self.ld("sp", d["o_pg"][l, hh], S[:], [S], [d["o_pg"]])
                b6 = self.bank()
                self.mm(b6[:, 0:128], kd[:], qd[:], True, True, [kd, qd], b6)
                self.tt("dve", attm[:], b6[:, 0:128], gm[:, 1 if samp else 0, :], ALU.mult, [b6, gm], [attm])
                b7 = self.bank()
                for dvc in range(2):
                    o7 = b7[:, dvc * 128:(dvc + 1) * 128]
                    self.mm(o7, v[:, dvc * 128:(dvc + 1) * 128], attm[:], True, False, [v, attm], b7)
                    if not samp:
                        self.mm(o7, Sb_prev[:, dvc * 128:(dvc + 1) * 128], qd[:], False, True, [Sb_prev, qd], b7)
                    else:
                        for b in range(16):
                            self.mm(o7, S0b[:, b, dvc * 128:(dvc + 1) * 128], QM[:, b, :], False, b == 15, [S0b, QM], b7)
                self.act(sq[:], b7[:, 0:256], ACT.Square, [b7], [sq])
                b8 = self.bank()
                self.mm(b8[:, 0:128], self.cbf[:, 3, :], sq[:, 0:128], True, False, [self.cbf, sq], b8)
                self.mm(b8[:, 0:128], self.cbf[:, 3, :], sq[:, 128:256], False, True, [self.cbf, sq], b8)
                self.act(sd[:], b8[:, 0:128], ACT.Sqrt, [b8, eps6], [sd], bias=eps6[:, 0:1])
                kb.op("dve", lambda: nc.vector.reciprocal(out=rstd[:], in_=sd[:]), reads=[sd], writes=[rstd])
                ob = obT[n % 2]
                for dvc in range(2):
                    cidx = VF_GNG + hh * 2 + dvc
                    self.tt("dve", otmp[:], b7[:, dvc * 128:(dvc + 1) * 128], rstd[:], ALU.mult, [b7, rstd], [otmp])
                    self.stt(ob[:, dvc, :], otmp[:], self.vecF[:, cidx:cidx + 1], gsl[:, dvc, :], ALU.mult, ALU.mult, [otmp, self.vecF, gsl], [ob])
                if "ob" in self.dbg:
                    self.dump("ob_%d_%d_%d" % (l, hh, n), ob, ob[:], [128, 2, 128], BF16)
                sg = sig8[n % 2]
                tmp = mtmp[n % 2]
                for half in range(2):
                    bb = self.bank()
                    for c4 in range(4):
                        c = half * 4 + c4
                        for ki in range(2):
                            self.mm(bb[:, c4 * 128:(c4 + 1) * 128], Wb[:, ki, c * 128:(c + 1) * 128], ob[:, ki, :], ki == 0, ki == 1, [Wb, ob], bb)
                    mslice = self.merged[n][:, half * 4:(half + 1) * 4, :]
                    self.tt("dve", tmp[:], bb[:].rearrange("p (a b) -> p a b", a=4), sg[:, half * 4:(half + 1) * 4, :], ALU.mult, [bb, sg], [tmp])
                    self.tt("pool", mslice, mslice, tmp[:], ALU.add, [tmp, self.merged[n]], [self.merged[n]])
                if samp:
                    for b in range(16):
                        km = klm[b % 2]
                        self.ts("pool", km[:], kl[:], ind[:, b:b + 1], None, ALU.mult, None, [kl, ind], [km])
                        bb = self.bank()
                        self.mm(bb[:, 0:256], km[:], v[:], True, True, [km, v], bb)
                        self.stt(S0[:, b, :], S0[:, b, :], eb[:, 8 * b + 7:8 * b + 8], bb[:, 0:256], ALU.mult, ALU.add, [S0, eb, bb], [S0])
                    self.ld("sp", d["o_sg"][l, hh], S0[:], [S0], [d["o_sg"]])

            S1(0)
            for n in range(NT):
                if n + 1 < NT:
                    S1(n + 1)
                S2(n)
            kb.barrier()

    def phase_pool(self, l):
        kb, nc, d = self.kb, self.nc, self.d
        NTP, NT = self.NTP, self.NT
        kb.barrier()
        with ExitStack() as es:
            sb = lambda n, s, dt: kb.sb(n, s, dt, es)
            win = d["w_in"]
            wv = lambda c0, c1: win[l, :, c0:c1].rearrange("(c p) n -> p c n", p=128)
            Wu = sb("Wu", [128, 8, 1024], BF16)
            Wp = sb("Wp", [128, 8, 256], BF16)
            Wc = sb("Wc", [128, 8, 1024], BF16)
            Wg = sb("WgC", [128, 8, 1024], BF16)
            self.load_w(Wu, wv(C_UC, C_UC + 1024), win)
            self.load_w(Wp, d["w_pool"][l].rearrange("g (c p) e -> p (g c) e", p=128), d["w_pool"])
            self.load_w(Wc, d["w_bc"][l].rearrange("(c p) n -> p c n", p=128), d["w_bc"])
            self.load_w(Wg, wv(C_GT + 2048, C_GT + 3072), win)
            buc = sb("buc", [128, 1024], F32)
            self.ld("sp", buc[:], d["b_in"][l, C_UC:C_UC + 1024].partition_broadcast(128), [d["b_in"]], [buc])
            pm = sb("pm", [128, 6, 4, 128], BF16)
            self.load_w(pm, d["c_pm"][:], d["c_pm"])
            ub = [sb("ub%d" % i, [128, 1024], BF16) for i in range(2)]
            uf = sb("uf", [128, 1024], F32)
            sprev = sb("sprev", [128, 2, 1024], BF16)
            kb.dma("pool", lambda: nc.gpsimd.dma_start(out=sprev[:, 0, :], in_=d["spool"][l, 0:128, :]), reads=[d["spool"]], writes=[sprev])
            kb.dma("pool", lambda: nc.gpsimd.dma_start(out=sprev[0:112, 1, :], in_=d["spool"][l, 128:240, :]), reads=[d["spool"]], writes=[sprev])
            xbp = [sb("xbC%d" % i, [128, 8, 128], BF16) for i in range(2)]
            self.xb_i = 0
            dbf = sb("dbf", [128, 8, 128], BF16)
            ocT = [sb("ocT%d" % i, [128, 8, 128], BF16) for i in range(2)]
            sig = [sb("sigC%d" % i, [128, 4, 128], F32) for i in range(2)]
            mtmp = [sb("mtmpC%d" % i, [128, 4, 128], F32) for i in range(2)]
            for n in range(NT):
                samp = (n == NTP)
                xb = self.cast_x(n, xbp)
                u = ub[n % 2]
                up = ub[(n + 1) % 2]
                outt = (n == NTP - 1) or samp
                for half in range(2):
                    bu = self.bank()
                    for dc in range(8):
                        self.mm(bu[:], xb[:, dc, :], Wu[:, dc, half * 512:(half + 1) * 512], dc == 0, dc == 7, [xb, Wu], bu)
                    self.tt("dve", u[:, half * 512:(half + 1) * 512], bu[:], buc[:, half * 512:(half + 1) * 512], ALU.add, [bu, buc], [u])
                    if outt:
                        self.tt("dve", uf[:, half * 512:(half + 1) * 512], bu[:], buc[:, half * 512:(half + 1) * 512], ALU.add, [bu, buc], [uf])
                for half in range(2):
                    bd = self.bank()
                    for c4 in range(4):
                        c = half * 4 + c4
                        g = c // 2
                        o = bd[:, c4 * 128:(c4 + 1) * 128]
                        lhs = u[:, c * 128:(c + 1) * 128]
                        if samp:
                            self.mm(o, lhs, pm[:, 3, g, :], True, False, [u, pm], bd)
                            self.mm(o, sprev[:, 0, c * 128:(c + 1) * 128], pm[:, 4, g, :], False, False, [sprev, pm], bd)
                            self.mm(o, sprev[0:112, 1, c * 128:(c + 1) * 128], pm[0:112, 5, g, :], False, True, [sprev, pm], bd)
                        elif n == 0:
                            self.mm(o, lhs, pm[:, 2, g, :], True, True, [u, pm], bd)
                        else:
                            self.mm(o, lhs, pm[:, 0, g, :], True, False, [u, pm], bd)
                            self.mm(o, up[:, c * 128:(c + 1) * 128], pm[:, 1, g, :], False, True, [up, pm], bd)
                    self.cp("act", dbf[:, half * 4:(half + 1) * 4, :], bd[:].rearrange("p (a b) -> p a b", a=4), [bd], [dbf])
                oc = ocT[n % 2]
                for half in range(2):
                    by = self.bank()
                    for c4 in range(4):
                        e = half * 4 + c4
                        g, ec = e // 2, e % 2
                        for cc in range(2):
                            self.mm(by[:, c4 * 128:(c4 + 1) * 128], Wp[:, g * 2 + cc, ec * 128:(ec + 1) * 128], dbf[:, g * 2 + cc, :], cc == 0, cc == 1, [Wp, dbf], by)
                    for c4 in range(4):
                        e = half * 4 + c4
                        self.ts("dve", oc[:, e, :], by[:, c4 * 128:(c4 + 1) * 128], self.vecF[:, VF_PSC + e:VF_PSC + e + 1], None, ALU.mult, None, [by, self.vecF], [oc])
                if "oc" in self.dbg:
                    self.dump("oc_%d_%d" % (l, n), oc, oc[:], [128, 8, 128], BF16)
                self.gate_and_merge(l, n, xb, Wg, 2, Wc, list(range(8)), oc, False, (sig[n % 2], mtmp[n % 2]))
                if n == NTP - 1:
                    self.ld("sp", d["o_pp"][l], uf[113:128, :], [uf], [d["o_pp"]])
                if samp:
                    self.ld("sp", d["o_sp"][l, :, 0:7, :], d["spool"][l].rearrange("(b r) f -> b r f", r=15)[:, 8:15, :], [d["spool"]], [d["o_sp"]])
                    for b in range(16):
                        self.ld("sp", d["o_sp"][l, b, 7:15, :], uf[8 * b:8 * b + 8, :], [uf], [d["o_sp"]])
            kb.barrier()

    def ln_fm(self, y, n, gcol, bcol, bufs, bank=None):
        kb, nc = self.kb, self.nc
        ybf, ysq, mean, m2, var, eps5 = bufs
        self.cp("act", ybf[:], y[:], [y], [ybf])
        self.act(ysq[:], y[:], ACT.Square, [y], [ysq])
        bm = bank if bank is not None else self.bank()
        for c in range(8):
            self.mm(bm[:, 0:128], self.cbf[:, 2, :], ybf[:, c, :], c == 0, c == 7, [self.cbf, ybf], bm)
        for c in range(8):
            self.mm(bm[:, 128:256], self.cbf[:, 2, :], ysq[:, c, :], c == 0, c == 7, [self.cbf, ysq], bm)
        self.cp("act", mean[:], bm[:, 0:128], [bm], [mean])
        self.tt("pool", m2[:], mean[:], mean[:], ALU.mult, [mean], [m2])
        self.tt("dve", var[:], bm[:, 128:256], m2[:], ALU.subtract, [bm, m2], [var])
        self.act(var[:], var[:], ACT.Sqrt, [var, eps5], [var], bias=eps5[:, 0:1])
        kb.op("dve", lambda: nc.vector.reciprocal(out=var[:], in_=var[:]), reads=[var], writes=[var])
        mb = mean[:].unsqueeze(1).to_broadcast([128, 8, 128])
        rb = var[:].unsqueeze(1).to_broadcast([128, 8, 128])
        self.tt("dve", y[:], y[:], mb, ALU.subtract, [y, mean], [y])
        self.tt("pool", y[:], y[:], rb, ALU.mult, [y, var], [y])
        for c in range(8):
            self.ts("dve", self.xT[n][:, c, :], y[:, c, :], self.vecF[:, gcol + c:gcol + c + 1], self.vecF[:, bcol + c:bcol + c + 1],
                    ALU.mult, ALU.add, [y, self.vecF], [self.xT[n]])

    def phase_out(self, l):
        kb, nc, d = self.kb, self.nc, self.d
        NTP, NT = self.NTP, self.NT
        kb.barrier()
        with ExitStack() as es:
            sb = lambda n, s, dt: kb.sb(n, s, dt, es)
            Wo = sb("Wo", [128, 8, 1024], BF16)
            self.load_w(Wo, d["w_out"][l].rearrange("(c p) n -> p c n", p=128), d["w_out"])
            ys = [sb("yD%d" % i, [128, 8, 128], F32) for i in range(2)]
            eps5 = sb("eps5", [128, 1], F32)
            kb.op("dve", lambda: nc.vector.memset(eps5[:], 1e-5), writes=[eps5])
            lnb = [(sb("ybf%d" % i, [128, 8, 128], BF16), sb("ysq%d" % i, [128, 8, 128], BF16), sb("mean%d" % i, [128, 128], F32),
                    sb("m2%d" % i, [128, 128], F32), sb("var%d" % i, [128, 128], F32), eps5) for i in range(2)]
            for n in range(NT):
                y = ys[n % 2]
                for half in range(2):
                    bo = self.bank()
                    for c4 in range(4):
                        c = half * 4 + c4
                        for k in range(8):
                            self.mm(bo[:, c4 * 128:(c4 + 1) * 128], Wo[:, k, c * 128:(c + 1) * 128], self.merged[n][:, k, :], k == 0, k == 7, [Wo, self.merged[n]], bo)
                    self.stt(y[:, half * 4:(half + 1) * 4, :], self.xT[n][:, half * 4:(half + 1) * 4, :], ALPHA,
                             bo[:].rearrange("p (a b) -> p a b", a=4), ALU.mult, ALU.add, [self.xT[n], bo], [y])
                self.ln_fm(y, n, VF_L1G, VF_L1B, lnb[n % 2])
                if "x1" in self.dbg:
                    self.dump("x1_%d_%d" % (l, n), self.xT[n], self.xT[n][:], [128, 8, 128], F32)
            kb.barrier()

    def phase_peer(self, l):
        kb, nc, d = self.kb, self.nc, self.d
        NTP, NT = self.NTP, self.NT
        kb.barrier()
        with ExitStack() as es:
            sb = lambda n, s, dt: kb.sb(n, s, dt, es)
            Wpq = sb("Wpq", [128, 8, 2048], BF16)
            skT = sb("skT", [128, 16, 128], BF16)
            self.load_w(Wpq, d["pq"][l].rearrange("(c p) n -> p c n", p=128), d["pq"])
            self.load_w(skT, d["skT"][l], d["skT"])
            pu, pv = d["pu%d" % l], d["pv%d" % l]
            xbp = [sb("xbE%d" % i, [128, 8, 128], BF16) for i in range(2)]
            self.xb_i = 0
            xtok = sb("xtok", [128, 1024], F32)
            qT = sb("qTE", [128, 16, 128], BF16)
            ssb = sb("ssb", [128, 16, 128], F32)
            s2 = sb("s2", [128, 128], F32)
            sv = sb("sv", [128, 16, 16], F32)
            si = sb("si", [128, 16, 16], U32)
            sif = sb("sif", [128, 16, 16], F32)
            cand = sb("cand", [128, 8, 256], F32)
            cand2 = sb("cand2", [128, 256], F32)
            cs = sb("cs", [128, 8, 16], F32)
            ci = sb("ci", [128, 8, 16], U32)
            ciu = sb("ciu", [128, 8, 16], U32)
            af = sb("af", [128, 8, 16], F32)
            bf = sb("bf", [128, 8, 16], F32)
            eq = sb("eq", [128, 8, 16, 16], F32)
            i0 = sb("i0", [128, 8, 16], F32)
            i1 = sb("i1", [128, 8, 16], F32)
            eidf = sb("eidf", [128, 128], F32)
            eidx = sb("eidx", [128, 128], I32)
            gex = sb("gex", [128, 8, 16], F32)
            gsum = sb("gsum", [128, 8], F32)
            gg = sb("gg", [128, 128], F32)
            hraw = sb("hraw", [128, 128], F32)
            hw = sb("hw", [128, 128], F32)
            NG = 10
            gb = [sb("gbuf%d" % i, [128, 1024], BF16) for i in range(NG)]
            NPB = 4
            prods = [sb("prod%d" % i, [128, 1024], BF16) for i in range(NPB)]
            dgs = [sb("dg%d" % i, [128, 128], BF16) for i in range(4)]
            xtokb = sb("xtokb", [128, 1024], BF16)
            junkb = sb("junkb", [128, 1024], BF16)
            junk = sb("junk", [128, 1024], F32)
            acc = sb("acc", [128, 1024], F32)
            st = sb("st", [128, 8], F32)
            iota16 = self.misc[:, 2, 0:16]
            gi = 0
            for n in range(NT):
                xb = self.cast_x(n, xbp)
                for half in range(2):
                    bt = self.bank()
                    for c4 in range(4):
                        c = half * 4 + c4
                        kb.op("pe", lambda: nc.tensor.transpose(bt[:, c4 * 128:(c4 + 1) * 128], self.xT[n][:, c, :], self.ident), reads=[self.xT[n], self.misc], writes=[bt])
                    self.cp("act", xtok[:, half * 512:(half + 1) * 512], bt[:], [bt], [xtok])
                for q4 in range(4):
                    bq = self.bank()
                    for hh in range(4):
                        hp = q4 * 4 + hh
                        for dc in range(8):
                            self.mm(bq[:, hh * 128:(hh + 1) * 128], Wpq[:, dc, hp * 128:(hp + 1) * 128], xb[:, dc, :], dc == 0, dc == 7, [Wpq, xb], bq)
                    self.cp("act", qT[:, q4 * 4:(q4 + 1) * 4, :], bq[:].rearrange("p (a b) -> p a b", a=4), [bq], [qT])
                for q4 in range(4):
                    bs = self.bank()
                    for hh in range(4):
                        hp = q4 * 4 + hh
                        self.mm(bs[:, hh * 128:(hh + 1) * 128], qT[:, hp, :], skT[:, hp, :], True, True, [qT, skT], bs)
                    self.cp("act", ssb[:, q4 * 4:(q4 + 1) * 4, :], bs[:].rearrange("p (a b) -> p a b", a=4), [bs], [ssb])
                for hp in range(16):
                    kb.op("dve", lambda: nc.vector.max(out=sv[:, hp, 0:8], in_=ssb[:, hp, :]), reads=[ssb], writes=[sv])
                    kb.op("dve", lambda: nc.vector.max_index(out=si[:, hp, 0:8], in_max=sv[:, hp, 0:8], in_values=ssb[:, hp, :]), reads=[ssb, sv], writes=[si])
                    kb.op("dve", lambda: nc.vector.match_replace(out=s2[:], in_to_replace=sv[:, hp, 0:8], in_values=ssb[:, hp, :], imm_value=-1e30), reads=[ssb, sv], writes=[s2])
                    kb.op("dve", lambda: nc.vector.max(out=sv[:, hp, 8:16], in_=s2[:]), reads=[s2], writes=[sv])
                    kb.op("dve", lambda: nc.vector.max_index(out=si[:, hp, 8:16], in_max=sv[:, hp, 8:16], in_values=s2[:]), reads=[s2, sv], writes=[si])
                self.cp("dve", sif[:], si[:], [si], [sif])
                sv4 = sv[:].rearrange("p (h q) k -> p h q k", q=2)
                sif4 = sif[:].rearrange("p (h q) k -> p h q k", q=2)
                cand4 = cand[:].rearrange("p h (a b) -> p h a b", a=16)
                self.tt("dve", cand4, sv4[:, :, 0, :].unsqueeze(3).to_broadcast([128, 8, 16, 16]),
                        sv4[:, :, 1, :].unsqueeze(2).to_broadcast([128, 8, 16, 16]), ALU.add, [sv], [cand])
                for h in range(8):
                    kb.op("dve", lambda: nc.vector.max(out=cs[:, h, 0:8], in_=cand[:, h, :]), reads=[cand], writes=[cs])
                    kb.op("dve", lambda: nc.vector.max_index(out=ci[:, h, 0:8], in_max=cs[:, h, 0:8], in_values=cand[:, h, :]), reads=[cand, cs], writes=[ci])
                    kb.op("dve", lambda: nc.vector.match_replace(out=cand2[:], in_to_replace=cs[:, h, 0:8], in_values=cand[:, h, :], imm_value=-1e30), reads=[cand, cs], writes=[cand2])
                    kb.op("dve", lambda: nc.vector.max(out=cs[:, h, 8:16], in_=cand2[:]), reads=[cand2], writes=[cs])
                    kb.op("dve", lambda: nc.vector.max_index(out=ci[:, h, 8:16], in_max=cs[:, h, 8:16], in_values=cand2[:]), reads=[cand2, cs], writes=[ci])
                kb.op("dve", lambda: nc.vector.tensor_single_scalar(out=ciu[:], in_=ci[:], scalar=4, op=ALU.logical_shift_right), reads=[ci], writes=[ciu])
                self.cp("dve", af[:], ciu[:], [ciu], [af])
                kb.op("dve", lambda: nc.vector.tensor_single_scalar(out=ciu[:], in_=ci[:], scalar=15, op=ALU.bitwise_and), reads=[ci], writes=[ciu])
                self.cp("dve", bf[:], ciu[:], [ciu], [bf])
                io4 = iota16.unsqueeze(1).unsqueeze(1).to_broadcast([128, 8, 16, 16])
                for (sel, q, dst) in ((af, 0, i0), (bf, 1, i1)):
                    self.tt("dve", eq[:], sel[:].unsqueeze(3).to_broadcast([128, 8, 16, 16]), io4, ALU.is_equal, [sel, self.misc], [eq])
                    self.tt("dve", eq[:], eq[:], sif4[:, :, q, :].unsqueeze(2).to_broadcast([128, 8, 16, 16]), ALU.mult, [eq, sif], [eq])
                    kb.op("dve", lambda: nc.vector.tensor_reduce(out=dst[:], in_=eq[:], axis=AX.X, op=ALU.add), reads=[eq], writes=[dst])
                self.stt(eidf[:].rearrange("p (h k) -> p h k", h=8), i0[:], 128.0, i1[:], ALU.mult, ALU.add, [i0, i1], [eidf])
                self.ts("dve", eidf[:], eidf[:], 0.0, 16383.0, ALU.max, ALU.min, [eidf], [eidf])
                self.cp("dve", eidx[:], eidf[:], [eidf], [eidx])
                self.tt("dve", gex[:], cs[:], cs[:, :, 0:1].to_broadcast([128, 8, 16]), ALU.subtract, [cs], [gex])
                self.act(gex[:], gex[:], ACT.Exp, [gex], [gex])
                kb.op("dve", lambda: nc.vector.tensor_reduce(out=gsum[:], in_=gex[:], axis=AX.X, op=ALU.add), reads=[gex], writes=[gsum])
                kb.op("dve", lambda: nc.vector.reciprocal(out=gsum[:], in_=gsum[:]), reads=[gsum], writes=[gsum])
                self.tt("dve", gg[:].rearrange("p (h k) -> p h k", h=8), gex[:], gsum[:].unsqueeze(2).to_broadcast([128, 8, 16]), ALU.mult, [gex, gsum], [gg])
                if "eidx" in self.dbg:
                    self.dump("eidx_%d_%d" % (l, n), eidx, eidx[:], [128, 128], I32)
                    self.dump("gg_%d_%d" % (l, n), gg, gg[:], [128, 128], F32)
                self.cp("act", xtokb[:], xtok[:], [xtok], [xtokb])
                for s in range(128):
                    g_ = gb[gi % NG]
                    pr = prods[gi % NPB]
                    gi += 1
                    kb.dma("pool", lambda: nc.gpsimd.indirect_dma_start(out=g_[:], out_offset=None, in_=pu[:], in_offset=bass.IndirectOffsetOnAxis(ap=eidx[:, s:s + 1], axis=0)),
                           reads=[eidx, pu], writes=[g_])
                    self.tt("dve", pr[:], g_[:], xtokb[:], ALU.mult, [g_, xtokb], [pr])
                    kb.op("act", lambda: nc.scalar.activation(out=junkb[:], in_=pr[:], func=ACT.Copy, accum_out=hraw[:, s:s + 1]), reads=[pr], writes=[junkb, hraw])
                self.act(hw[:], hraw[:], ACT.Gelu, [hraw], [hw])
                self.tt("dve", hw[:], hw[:], gg[:], ALU.mult, [hw, gg], [hw])
                bacc = [self.bank(), self.bank()]
                for s in range(128):
                    g_ = gb[gi % NG]
                    dg = dgs[gi % 4]
                    gi += 1
                    kb.dma("pool", lambda: nc.gpsimd.indirect_dma_start(out=g_[:], out_offset=None, in_=pv[:], in_offset=bass.IndirectOffsetOnAxis(ap=eidx[:, s:s + 1], axis=0)),
                           reads=[eidx, pv], writes=[g_])
                    kb.op("act", lambda: nc.scalar.activation(out=dg[:], in_=self.cbf[:, 0, :], func=ACT.Copy, scale=hw[:, s:s + 1]), reads=[self.cbf, hw], writes=[dg])
                    for half in range(2):
                        self.mm(bacc[half][:], dg[:], g_[:, half * 512:(half + 1) * 512], s == 0, s == 127, [dg, g_], bacc[half])
                for half in range(2):
                    self.stt(acc[:, half * 512:(half + 1) * 512], xtok[:, half * 512:(half + 1) * 512], ALPHA, bacc[half][:], ALU.mult, ALU.add, [xtok, bacc[half]], [acc])
                kb.op("dve", lambda: nc.vector.tensor_reduce(out=st[:, 0:1], in_=acc[:], axis=AX.X, op=ALU.add), reads=[acc], writes=[st])
                self.stt(junk[:], acc[:], 1.0, acc[:], ALU.mult, ALU.mult, [acc], [junk, st], accum_out=st[:, 1:2])
                self.ts("dve", st[:, 2:3], st[:, 0:1], 1.0 / 1024, None, ALU.mult, None, [st], [st])
                self.tt("dve", st[:, 3:4], st[:, 2:3], st[:, 2:3], ALU.mult, [st], [st])
                self.stt(st[:, 4:5], st[:, 1:2], 1.0 / 1024, st[:, 3:4], ALU.mult, ALU.subtract, [st], [st])
                self.ts("dve", st[:, 4:5], st[:, 4:5], 1e-5, None, ALU.add, None, [st], [st])
                self.act(st[:, 5:6], st[:, 4:5], ACT.Sqrt, [st], [st])
                kb.op("dve", lambda: nc.vector.reciprocal(out=st[:, 6:7], in_=st[:, 5:6]), reads=[st], writes=[st])
                self.ts("dve", acc[:], acc[:], st[:, 2:3], st[:, 6:7], ALU.subtract, ALU.mult, [acc, st], [acc])
                for half in range(2):
                    bt = self.bank()
                    for c4 in range(4):
                        c = half * 4 + c4
                        kb.op("pe", lambda: nc.tensor.transpose(bt[:, c4 * 128:(c4 + 1) * 128], acc[:, c * 128:(c + 1) * 128], self.ident), reads=[acc, self.misc], writes=[bt])
                    for c4 in range(4):
                        c = half * 4 + c4
                        self.ts("dve", self.xT[n][:, c, :], bt[:, c4 * 128:(c4 + 1) * 128], self.vecF[:, VF_L2G + c:VF_L2G + c + 1],
                                self.vecF[:, VF_L2B + c:VF_L2B + c + 1], ALU.mult, ALU.add, [bt, self.vecF], [self.xT[n]])
            kb.barrier()


    def phase_peer_dense(self, l):
        kb, nc, d = self.kb, self.nc, self.d
        NTP, NT, T = self.NTP, self.NT, self.T
        kb.barrier()
        puT, pvp, ub16, vb16 = d["puT%d" % l], d["pvp%d" % l], d["ub16_%d" % l], d["vb16_%d" % l]
        for c in range(16):
            kb.dma("pool", lambda: nc.gpsimd.dma_start(out=ub16[c * 8:(c + 1) * 8], in_=puT[c * 8:(c + 1) * 8]), reads=[puT], writes=[ub16])
            kb.dma("pool", lambda: nc.gpsimd.dma_start(out=vb16[c * 8:(c + 1) * 8], in_=pvp[c * 8:(c + 1) * 8]), reads=[pvp], writes=[vb16])
        with ExitStack() as oes:
            idxT = kb.sb("idxT", [128, T, 2], F32, oes)
            gT = kb.sb("gT", [128, T], F32, oes)
            with ExitStack() as es:
                sb = lambda n, s, dt: kb.sb(n, s, dt, es)
                Wpq = sb("Wpq", [128, 8, 2048], BF16)
                skT = sb("skT", [128, 16, 128], BF16)
                self.load_w(Wpq, d["pq"][l].rearrange("(c p) n -> p c n", p=128), d["pq"])
                self.load_w(skT, d["skT"][l], d["skT"])
                xbp = [sb("xbE%d" % i, [128, 8, 128], BF16) for i in range(2)]
                self.xb_i = 0
                qT = sb("qTE", [128, 16, 128], BF16)
                ssb = sb("ssb", [128, 16, 128], F32)
                s2 = sb("s2", [128, 128], F32)
                sv = sb("sv", [128, 16, 16], F32)
                si = sb("si", [128, 16, 16], U32)
                sif = sb("sif", [128, 16, 16], F32)
                cand = sb("cand", [128, 8, 256], F32)
                cand2 = sb("cand2", [128, 256], F32)
                cs = sb("cs", [128, 8, 16], F32)
                ci = sb("ci", [128, 8, 16], U32)
                ciu = sb("ciu", [128, 8, 16], U32)
                af = sb("af", [128, 8, 16], F32)
                bf = sb("bf", [128, 8, 16], F32)
                eq = sb("eq", [128, 8, 16, 16], F32)
                i0 = sb("i0", [128, 8, 16], F32)
                i1 = sb("i1", [128, 8, 16], F32)
                gex = sb("gex", [128, 8, 16], F32)
                gsum = sb("gsum", [128, 8], F32)
                gg = sb("gg", [128, 128], F32)
                iota16 = self.misc[:, 2, 0:16]

                def split(name, ngrp, width, dt):
                    t = es.enter_context(nc.sbuf_tensor("%s_%d" % (name, l), [128, ngrp, width], dt))
                    return t, [Buf(t[:, g_, :], "%s%d" % (name, g_)) for g_ in range(ngrp)]
                svAt, svA = split("svA", 16, 8, F32)
                svBt, svB = split("svB", 16, 8, F32)
                siAt, siA = split("siA", 16, 8, U32)
                siBt, siB = split("siB", 16, 8, U32)
                s2t, s2s = split("s2s", 16, 128, F32)
                csAt, csA = split("csA", 8, 8, F32)
                csBt, csB = split("csB", 8, 8, F32)
                ciAt, ciA = split("ciA", 8, 8, U32)
                ciBt, ciB = split("ciB", 8, 8, U32)
                c2t, c2s = split("c2s", 8, 256, F32)
                ssbs = [ssb, sb("ssb2", [128, 16, 128], F32)]

                def E1a(n):
                    ssb = ssbs[n % 2]
                    xb = self.cast_x(n, xbp)
                    for q4 in range(4):
                        bq = self.bank()
                        for hh in range(4):
                            hp = q4 * 4 + hh
                            for dc in range(8):
                                self.mm(bq[:, hh * 128:(hh + 1) * 128], Wpq[:, dc, hp * 128:(hp + 1) * 128], xb[:, dc, :], dc == 0, dc == 7, [Wpq, xb], bq)
                        self.cp("act", qT[:, q4 * 4:(q4 + 1) * 4, :], bq[:].rearrange("p (a b) -> p a b", a=4), [bq], [qT])
                    for q4 in range(4):
                        bs = self.bank()
                        for hh in range(4):
                            hp = q4 * 4 + hh
                            self.mm(bs[:, hh * 128:(hh + 1) * 128], qT[:, hp, :], skT[:, hp, :], True, True, [qT, skT], bs)
                        self.cp("act", ssb[:, q4 * 4:(q4 + 1) * 4, :], bs[:].rearrange("p (a b) -> p a b", a=4), [bs], [ssb])

                E1a(0)
                for n in range(NT):
                    if n + 1 < NT:
                        E1a(n + 1)
                    ssb = ssbs[n % 2]
                    for hp in range(16):
                        kb.op("dve", lambda: nc.vector.max(out=svA[hp][:], in_=ssb[:, hp, :]), reads=[ssb], writes=[svA[hp]])
                    for hp in range(16):
                        kb.op("dve", lambda: nc.vector.max_index(out=siA[hp][:], in_max=svA[hp][:], in_values=ssb[:, hp, :]), reads=[ssb, svA[hp]], writes=[siA[hp]])
                    for hp in range(16):
                        kb.op("dve", lambda: nc.vector.match_replace(out=s2s[hp][:], in_to_replace=svA[hp][:], in_values=ssb[:, hp, :], imm_value=-1e30), reads=[ssb, svA[hp]], writes=[s2s[hp]])
                    for hp in range(16):
                        kb.op("dve", lambda: nc.vector.max(out=svB[hp][:], in_=s2s[hp][:]), reads=[s2s[hp]], writes=[svB[hp]])
                    for hp in range(16):
                        kb.op("dve", lambda: nc.vector.max_index(out=siB[hp][:], in_max=svB[hp][:], in_values=s2s[hp][:]), reads=[s2s[hp], svB[hp]], writes=[siB[hp]])
                    svv = sv[:].rearrange("p a (q k) -> p a q k", q=2)
                    siv = si[:].rearrange("p a (q k) -> p a q k", q=2)
                    self.cp("dve", svv[:, :, 0, :], svAt[:], svA, [sv])
                    self.cp("dve", svv[:, :, 1, :], svBt[:], svB, [sv])
                    self.cp("dve", siv[:, :, 0, :], siAt[:], siA, [si])
                    self.cp("dve", siv[:, :, 1, :], siBt[:], siB, [si])
                    self.cp("dve", sif[:], si[:], [si], [sif])
                    sv4 = sv[:].rearrange("p (h q) k -> p h q k", q=2)
                    sif4 = sif[:].rearrange("p (h q) k -> p h q k", q=2)
                    cand4 = cand[:].rearrange("p h (a b) -> p h a b", a=16)
                    self.tt("dve", cand4, sv4[:, :, 0, :].unsqueeze(3).to_broadcast([128, 8, 16, 16]),
                            sv4[:, :, 1, :].unsqueeze(2).to_broadcast([128, 8, 16, 16]), ALU.add, [sv], [cand])
                    for h in range(8):
                        kb.op("dve", lambda: nc.vector.max(out=csA[h][:], in_=cand[:, h, :]), reads=[cand], writes=[csA[h]])
                    for h in range(8):
                        kb.op("dve", lambda: nc.vector.max_index(out=ciA[h][:], in_max=csA[h][:], in_values=cand[:, h, :]), reads=[cand, csA[h]], writes=[ciA[h]])
                    for h in range(8):
                        kb.op("dve", lambda: nc.vector.match_replace(out=c2s[h][:], in_to_replace=csA[h][:], in_values=cand[:, h, :], imm_value=-1e30), reads=[cand, csA[h]], writes=[c2s[h]])
                    for h in range(8):
                        kb.op("dve", lambda: nc.vector.max(out=csB[h][:], in_=c2s[h][:]), reads=[c2s[h]], writes=[csB[h]])
                    for h in range(8):
                        kb.op("dve", lambda: nc.vector.max_index(out=ciB[h][:], in_max=csB[h][:], in_values=c2s[h][:]), reads=[c2s[h], csB[h]], writes=[ciB[h]])
                    csv = cs[:].rearrange("p a (q k) -> p a q k", q=2)
                    civ = ci[:].rearrange("p a (q k) -> p a q k", q=2)
                    self.cp("dve", csv[:, :, 0, :], csAt[:], csA, [cs])
                    self.cp("dve", csv[:, :, 1, :], csBt[:], csB, [cs])
                    self.cp("dve", civ[:, :, 0, :], ciAt[:], ciA, [ci])
                    self.cp("dve", civ[:, :, 1, :], ciBt[:], ciB, [ci])
                    kb.op("dve", lambda: nc.vector.tensor_single_scalar(out=ciu[:], in_=ci[:], scalar=4, op=ALU.logical_shift_right), reads=[ci], writes=[ciu])
                    self.cp("dve", af[:], ciu[:], [ciu], [af])
                    kb.op("dve", lambda: nc.vector.tensor_single_scalar(out=ciu[:], in_=ci[:], scalar=15, op=ALU.bitwise_and), reads=[ci], writes=[ciu])
                    self.cp("dve", bf[:], ciu[:], [ciu], [bf])
                    io4 = iota16.unsqueeze(1).unsqueeze(1).to_broadcast([128, 8, 16, 16])
                    for (sel, q, dst) in ((af, 0, i0), (bf, 1, i1)):
                        self.tt("dve", eq[:], sel[:].unsqueeze(3).to_broadcast([128, 8, 16, 16]), io4, ALU.is_equal, [sel, self.misc], [eq])
                        self.tt("dve", eq[:], eq[:], sif4[:, :, q, :].unsqueeze(2).to_broadcast([128, 8, 16, 16]), ALU.mult, [eq, sif], [eq])
                        kb.op("dve", lambda: nc.vector.tensor_reduce(out=dst[:], in_=eq[:], axis=AX.X, op=ALU.add), reads=[eq], writes=[dst])
                    self.tt("dve", gex[:], cs[:], cs[:, :, 0:1].to_broadcast([128, 8, 16]), ALU.subtract, [cs], [gex])
                    self.act(gex[:], gex[:], ACT.Exp, [gex], [gex])
                    kb.op("dve", lambda: nc.vector.tensor_reduce(out=gsum[:], in_=gex[:], axis=AX.X, op=ALU.add), reads=[gex], writes=[gsum])
                    kb.op("dve", lambda: nc.vector.reciprocal(out=gsum[:], in_=gsum[:]), reads=[gsum], writes=[gsum])
                    self.tt("dve", gg[:].rearrange("p (h k) -> p h k", h=8), gex[:], gsum[:].unsqueeze(2).to_broadcast([128, 8, 16]), ALU.mult, [gex, gsum], [gg])
                    bt = self.bank()
                    for k_, (src, sbuf_) in enumerate(((i0, i0), (i1, i1), (gg, gg))):
                        src_ap = src[:].rearrange("p h k -> p (h k)") if k_ < 2 else src[:]
                        kb.op("pe", lambda: nc.tensor.transpose(bt[:, k_ * 128:(k_ + 1) * 128], src_ap, self.ident), reads=[sbuf_, self.misc], writes=[bt])
                    self.cp("act", idxT[:, n * 128:(n + 1) * 128, 0], bt[:, 0:128], [bt], [idxT])
                    self.cp("act", idxT[:, n * 128:(n + 1) * 128, 1], bt[:, 128:256], [bt], [idxT])
                    self.cp("act", gT[:, n * 128:(n + 1) * 128], bt[:, 256:384], [bt], [gT])
                kb.barrier()
            with ExitStack() as es:
                sb = lambda n, s, dt: kb.sb(n, s, dt, es)
                G = sb("Gsb", [128, 256, 128], BF16)
                xbb = sb("xbb", [128, 8, 256], BF16)
                NR = 4
                ubl = [sb("ubl%d" % i, [128, 8, 128], BF16) for i in range(NR)]
                vbl = [sb("vbl%d" % i, [128, 1024], BF16) for i in range(NR)]
                hgs = [sb("hg%d" % i, [128, 256], BF16) for i in range(NR)]
                Ws = [sb("Wd%d" % i, [128, 256], BF16) for i in range(NR)]
                Ag = [sb("Ag%d" % i, [128, 128], BF16) for i in range(4)]
                Bg = [sb("Bg%d" % i, [128, 128], BF16) for i in range(4)]
                ys = [sb("yE%d" % i, [128, 8, 128], F32) for i in range(2)]
                eps5 = sb("eps5E", [128, 1], F32)
                kb.op("dve", lambda: nc.vector.memset(eps5[:], 1e-5), writes=[eps5])
                lnb = [(sb("ybfE%d" % i, [128, 8, 128], BF16), sb("ysqE%d" % i, [128, 8, 128], BF16), sb("meanE%d" % i, [128, 128], F32),
                        sb("m2E%d" % i, [128, 128], F32), sb("varE%d" % i, [128, 128], F32), eps5) for i in range(2)]
                iota_b = sb("iota_b", [128, 128], BF16)
                self.cp("dve", iota_b[:], self.misc[:, 3, :], [self.misc], [iota_b])
                iota128 = iota_b[:]
                accb = self.banks[0:4]
                bhs = self.banks[4:6]
                bgs = self.banks[6:8]
                blocks = [(2 * k, 2) for k in range(NTP // 2)] + [(NTP, 1)]
                ri = 0
                yi = 0
                for (n0, ntl) in blocks:
                    nt = ntl * 128
                    t0 = n0 * 128
                    for k in range(ntl):
                        self.cp("dve", xbb[:, :, k * 128:(k + 1) * 128], self.xT[n0 + k][:], [self.xT[n0 + k]], [xbb])
                    sid_g, _ = nc.enter_named_scope("E2g%d" % l, False)
                    for tq in range(nt // 4):
                        bg = bgs[tq % 2]
                        for k in range(4):
                            tl = tq * 4 + k
                            tok = t0 + tl
                            A_ = Ag[tl % 4]
                            B_ = Bg[tl % 4]
                            self.ts("dve", A_[:], iota128, idxT[:, tok, 0:1], gT[:, tok:tok + 1], ALU.is_equal, ALU.mult, [iota_b, idxT, gT], [A_])
                            self.ts("dve", B_[:], iota128, idxT[:, tok, 1:2], None, ALU.is_equal, None, [iota_b, idxT], [B_])
                            self.mm(bg[:, k * 128:(k + 1) * 128], A_[:], B_[:], True, True, [A_, B_], bg)
                        self.cp("act", G[:, tq * 4:(tq + 1) * 4, :], bg[:].rearrange("p (a b) -> p a b", a=4), [bg], [G])
                    nc.leave_named_scope("E2g%d" % l, sid_g, False)
                    sid_m, _ = nc.enter_named_scope("E2m%d" % l, False)
                    def stage1(i1_):
                        r = ri + i1_
                        ub, vb, hg, W, bh = ubl[r % NR], vbl[r % NR], hgs[r % NR], Ws[r % NR], bhs[r % 2]
                        self.ld("sp", ub[:].rearrange("p c n -> p (c n)"), ub16[i1_], [ub16], [ub])
                        self.ld("sp", vb[:], vb16[i1_], [vb16], [vb])
                        for dc in range(8):
                            self.mm(bh[:, 0:nt], ub[:, dc, :], xbb[:, dc, 0:nt], dc == 0, dc == 7, [ub, xbb], bh)
                        self.act(hg[:, 0:nt], bh[:, 0:nt], ACT.Gelu, [bh], [hg])
                        self.tt("dve", W[:, 0:nt], hg[:, 0:nt], G[:, 0:nt, i1_], ALU.mult, [hg, G], [W])

                    def stage2(i1_):
                        r = ri + i1_
                        vb, W = vbl[r % NR], Ws[r % NR]
                        for c in range(8):
                            ab = accb[c // 2]
                            o = ab[:, (c % 2) * 256:(c % 2) * 256 + nt]
                            self.mm(o, vb[:, c * 128:(c + 1) * 128], W[:, 0:nt], (i1_ == 0 and c % 2 == 0), i1_ == 127, [vb, W], ab)

                    for i1_ in range(128):
                        stage1(i1_)
                        if i1_ >= 1:
                            stage2(i1_ - 1)
                    stage2(127)
                    ri += 128
                    nc.leave_named_scope("E2m%d" % l, sid_m, False)
                    for k in range(ntl):
                        n = n0 + k
                        y = ys[yi % 2]
                        lb = lnb[yi % 2]
                        yi += 1
                        for c in range(8):
                            ab = accb[c // 2]
                            o = ab[:, (c % 2) * 256 + k * 128:(c % 2) * 256 + (k + 1) * 128]
                            self.stt(y[:, c, :], self.xT[n][:, c, :], ALPHA, o, ALU.mult, ALU.add, [self.xT[n], ab], [y])
                        self.ln_fm(y, n, VF_L2G, VF_L2B, lb, bank=bhs[yi % 2])
                kb.barrier()


_CACHE = {}


def host_weights(inp, DEPTH, peer_mode="dense"):
    w = {}
    f = lambda a: np.ascontiguousarray(np.asarray(a, dtype=np.float32))
    w["w_in"] = f(inp["w_in"][:DEPTH])
    b_in = f(inp["b_in"][:DEPTH])
    w["b_in"] = b_in
    vec = np.zeros((DEPTH, 128, VF_N), np.float32)
    for l in range(DEPTH):
        b = b_in[l]
        vec[l, :64, VF_BQ:VF_BQ + 16] = b[C_QA:C_KA].reshape(16, 64).T
        vec[l, :64, VF_BQ + 16:VF_BQ + 18] = b[C_KA:C_VA].reshape(2, 64).T
        vec[l, :, VF_BQB:VF_BQB + 4] = b[C_QB:C_KB].reshape(4, 128).T
        vec[l, :, VF_BKB:VF_BKB + 4] = b[C_KB:C_VB].reshape(4, 128).T
        vec[l, :, VF_BGB:VF_BGB + 8] = b[C_GB:C_UC].reshape(8, 128).T
        vec[l, :, VF_BGT:VF_BGT + 24] = b[C_GT:INW].reshape(24, 128).T
        vec[l, :16, VF_BLR] = b[C_LR:C_GB]
        vec[l, :, VF_GNG:VF_GNG + 8] = np.asarray(inp["gla_norm_g"][l], np.float32).reshape(8, 128).T
        vec[l, :, VF_PSC:VF_PSC + 8] = np.asarray(inp["pool_scale"][l], np.float32).reshape(8, 128).T
        vec[l, :, VF_L1G:VF_L1G + 8] = np.asarray(inp["ln1_g"][l], np.float32).reshape(8, 128).T
        vec[l, :, VF_L1B:VF_L1B + 8] = np.asarray(inp["ln1_b"][l], np.float32).reshape(8, 128).T
        vec[l, :, VF_L2G:VF_L2G + 8] = np.asarray(inp["ln2_g"][l], np.float32).reshape(8, 128).T
        vec[l, :, VF_L2B:VF_L2B + 8] = np.asarray(inp["ln2_b"][l], np.float32).reshape(8, 128).T
    w["vecF"] = vec
    w["sinks"] = f(inp["attn_sinks"][:DEPTH])
    w["w_alpha"] = f(inp["w_alpha"][:DEPTH])
    w["b_alpha"] = f(inp["b_alpha"][:DEPTH])
    w["w_pool"] = f(inp["w_pool"][:DEPTH])
    w["w_ba"] = f(inp["w_branch_a"][:DEPTH])
    w["w_bb"] = f(inp["w_branch_b"][:DEPTH])
    w["w_bc"] = f(inp["w_branch_c"][:DEPTH])
    w["w_out"] = f(inp["w_out"][:DEPTH])
    w["pq"] = f(np.asarray(inp["peer_query"][:DEPTH]).reshape(DEPTH, 1024, 2048))
    sk = np.asarray(inp["peer_subkeys"][:DEPTH], np.float32).reshape(DEPTH, 16, 128, 128)
    w["skT"] = f(sk.transpose(0, 3, 1, 2))
    for l in range(DEPTH):
        if peer_mode == "dense":
            u = np.asarray(inp["peer_u"][l], np.float32).reshape(128, 128, 8, 128)
            w["puT%d" % l] = f(u.transpose(1, 3, 2, 0).reshape(128, 128, 1024))
            v = np.asarray(inp["peer_v"][l], np.float32).reshape(128, 128, 1024)
            w["pvp%d" % l] = f(v.transpose(1, 0, 2))
        else:
            w["pu%d" % l] = f(inp["peer_u"][l])
            w["pv%d" % l] = f(inp["peer_v"][l])
    return w


def core_inputs(inp, i, NTP, DEPTH):
    f = lambda a: np.ascontiguousarray(np.asarray(a, dtype=np.float32))
    T = NTP * 128 + 128
    xp = np.asarray(inp["x_prompt"][i, :NTP * 128], np.float32)
    xs = np.asarray(inp["x_sample"][16 * i:16 * i + 16], np.float32).reshape(128, D)
    x = np.concatenate([xp, xs], 0)
    m = {}
    m["xT"] = f(x.reshape(T, 8, 128).transpose(2, 1, 0))
    sl = slice(16 * i, 16 * i + 16)
    swk = np.asarray(inp["state_win_k"][:DEPTH, sl], np.float32)
    m["swkT"] = f(swk.transpose(0, 4, 1, 3, 2))
    swv = np.asarray(inp["state_win_v"][:DEPTH, sl], np.float32)
    m["swv"] = f(swv.transpose(0, 2, 1, 3, 4))
    sg = np.asarray(inp["state_gla"][:DEPTH, sl], np.float32)
    m["sgla"] = f(sg.transpose(0, 2, 3, 1, 4))
    sp = np.asarray(inp["state_pool"][:DEPTH, sl], np.float32)
    m["spool"] = f(sp.reshape(DEPTH, 240, 1024))
    return m


PEER_MODE = "dense"


def get_prog(NTP, DEPTH, dbg=(), phases="ABCDE"):
    key = (NTP, DEPTH, tuple(dbg), phases, PEER_MODE)
    if key not in _CACHE:
        p = Prog(NTP, DEPTH, dbg)
        p.phases = phases
        p.peer_mode = PEER_MODE
        p.build()
        _CACHE[key] = p
    return _CACHE[key]


def run(inp, NTP=16, DEPTH=2, n_cores=8, dbg=(), phases="ABCDE"):
    p = get_prog(NTP, DEPTH, dbg, phases)
    consts = make_consts(NTP)
    w = host_weights(inp, DEPTH, PEER_MODE)
    in_maps = []
    for i in range(n_cores):
        m = dict(w)
        m.update(consts)
        m.update(core_inputs(inp, i, NTP, DEPTH))
        in_maps.append(m)
    res = run_bass_kernel_spmd(p.nc, in_maps, core_ids=list(range(n_cores)))
    return p, res.results


def assemble(results, NTP, DEPTH, n_cores):
    Tp = NTP * 128
    yp = np.zeros((n_cores, Tp, D), np.float32)
    ys = np.zeros((n_cores * 16, 8, D), np.float32)
    pk = np.zeros((DEPTH, n_cores, 128, 2, 64), np.float32)
    pv = np.zeros((DEPTH, n_cores, 128, 2, 64), np.float32)
    pg = np.zeros((DEPTH, n_cores, 4, 128, 256), np.float32)
    pp = np.zeros((DEPTH, n_cores, 15, 1024), np.float32)
    sk = np.zeros((DEPTH, n_cores * 16, 128, 2, 64), np.float32)
    sv = np.zeros((DEPTH, n_cores * 16, 128, 2, 64), np.float32)
    sg = np.zeros((DEPTH, n_cores * 16, 4, 128, 256), np.float32)
    sp = np.zeros((DEPTH, n_cores * 16, 15, 1024), np.float32)
    for i, r in enumerate(results):
        y = np.asarray(r["yT"]).transpose(2, 1, 0).reshape(-1, D)
        yp[i] = y[:Tp]
        ys[16 * i:16 * i + 16] = y[Tp:].reshape(16, 8, D)
        sl = slice(16 * i, 16 * i + 16)
        pk[:, i] = np.asarray(r["o_pkT"]).transpose(0, 3, 2, 1)
        pv[:, i] = np.asarray(r["o_pv"]).reshape(DEPTH, 128, 2, 64)
        pg[:, i] = np.asarray(r["o_pg"])
        pp[:, i] = np.asarray(r["o_pp"])
        sk[:, sl] = np.asarray(r["o_skT"]).transpose(0, 2, 4, 3, 1)
        sv[:, sl] = np.asarray(r["o_sv"]).reshape(DEPTH, 16, 128, 2, 64)
        sg[:, sl] = np.asarray(r["o_sg"]).transpose(0, 3, 1, 2, 4)
        sp[:, sl] = np.asarray(r["o_sp"])
    return (yp, ys, pk, pv, pg, pp, sk, sv, sg, sp)


def kernel(**inputs):
    NTP, DEPTH, NC = 16, 2, 8
    _, results = run(inputs, NTP, DEPTH, NC)
    return assemble(results, NTP, DEPTH, NC)
```

```python
import numpy as np
from contextlib import ExitStack
import concourse.bass as bass
import concourse.mybir as mybir
from concourse.bass_utils import run_bass_kernel_spmd

F32 = mybir.dt.float32
BF16 = mybir.dt.bfloat16
I32 = mybir.dt.int32
U32 = mybir.dt.uint32
ACT = mybir.ActivationFunctionType
ALU = mybir.AluOpType
AX = mybir.AxisListType

D = 1024
INW = 8464
NEG = -30000.0
ALPHA = 4.0 ** 0.25
C_QA, C_KA, C_VA, C_QB, C_KB, C_VB, C_LR, C_GB, C_UC, C_GT = 0, 1024, 1152, 1280, 1792, 2304, 3328, 3344, 4368, 5392
VF_BQ, VF_BQB, VF_BKB, VF_BGB, VF_BGT, VF_BLR, VF_GNG, VF_PSC, VF_L1G, VF_L1B, VF_L2G, VF_L2B, VF_N = 0, 18, 22, 26, 34, 58, 59, 67, 75, 83, 91, 99, 107


class Buf:
    def __init__(self, t, name):
        self.t = t
        self.name = name
        self.w = None
        self.r = []

    def __getitem__(self, k):
        return self.t[k]


class KB:
    EPOCH = 3000

    def __init__(self, n_dma_sems=32):
        self.nc = bass.Bass("TRN2", target_bir_lowering=False)
        nc = self.nc
        self.es = None
        self.eng = {"pe": nc.tensor, "act": nc.scalar, "dve": nc.vector, "pool": nc.gpsimd, "sp": nc.sync}
        self.sems = {}
        self.cur = {}
        self.known = {e: {} for e in self.eng}
        self.n_dma_sems = n_dma_sems
        self.dma_rr = 0
        self.dma_tot = {}
        self.nsem = 0
        self.ninstr = 0

    def start(self, es):
        self.es = es
        for e in ("pe", "act", "dve", "pool"):
            self._new_epoch(e)
        for i in range(self.n_dma_sems):
            k = ("dma", i)
            self.sems[k] = es.enter_context(self.nc.semaphore("dsem%d" % i))
            self.dma_tot[k] = 0

    def _new_epoch(self, e):
        self.nsem += 1
        k = (e, self.nsem)
        self.sems[k] = self.es.enter_context(self.nc.semaphore("s_%s_%d" % (e, self.nsem)))
        self.cur[e] = [k, 0]

    def sb(self, name, shape, dtype, es=None):
        self.nalloc = getattr(self, "nalloc", 0) + 1
        name = "sb%d_%s" % (self.nalloc, name)
        t = (es or self.es).enter_context(self.nc.sbuf_tensor(name, list(shape), dtype))
        return Buf(t, name)

    def ps(self, name, shape, dtype=F32, es=None):
        t = (es or self.es).enter_context(self.nc.psum_tensor(name, list(shape), dtype))
        return Buf(t, name)

    def dram(self, name, shape, dtype, kind):
        t = self.nc.dram_tensor(name, list(shape), dtype, kind=kind)
        return Buf(t.ap(), name)

    def _wait(self, e, dep):
        k, v = dep
        kn = self.known[e]
        if kn.get(k, 0) >= v:
            return
        self.eng[e].wait_ge(self.sems[k], v)
        kn[k] = v

    FUSE_WAIT = True

    def _deps(self, e, reads, writes, fuse=False):
        m = {}
        for b in reads:
            if b.w is not None:
                k, v = b.w
                if m.get(k, 0) < v:
                    m[k] = v
        for b in writes:
            if b.w is not None:
                k, v = b.w
                if m.get(k, 0) < v:
                    m[k] = v
            for k, v in b.r:
                if m.get(k, 0) < v:
                    m[k] = v
        pend = []
        kn = self.known[e]
        for k, v in m.items():
            if e == "pe" and k[0] == "pe":
                continue
            if kn.get(k, 0) >= v:
                continue
            pend.append((k, v))
        if fuse and pend:
            for dep in pend[:-1]:
                self._wait(e, dep)
            return pend[-1]
        for dep in pend:
            self._wait(e, dep)
        return None

    def op(self, e, fn, reads=(), writes=()):
        last = self._deps(e, reads, writes, fuse=self.FUSE_WAIT)
        ins = fn()
        if last is not None:
            ins._wait_ge(self.sems[last[0]], last[1])
            self.known[e][last[0]] = last[1]
        k, c = self.cur[e]
        c += 1
        ins.then_inc(self.sems[k], 1)
        self.cur[e][1] = c
        tag = (k, c)
        for b in reads:
            b.r.append(tag)
            if len(b.r) > 64:
                b.r = self._compact(b.r)
        for b in writes:
            b.w = tag
            b.r = []
        self.ninstr += 1
        if c >= self.EPOCH:
            self._new_epoch(e)
        return ins

    @staticmethod
    def _compact(r):
        m = {}
        for k, v in r:
            if m.get(k, 0) < v:
                m[k] = v
        return list(m.items())

    def dma(self, q, fn, reads=(), writes=()):
        self._deps(q, reads, writes)
        k = ("dma", self.dma_rr)
        self.dma_rr = (self.dma_rr + 1) % self.n_dma_sems
        if self.dma_tot[k] > 0:
            self._wait(q, (k, self.dma_tot[k]))
        ins = fn()
        self.dma_tot[k] += 16
        ins.then_inc(self.sems[k], 16)
        tag = (k, self.dma_tot[k])
        for b in reads:
            b.r.append(tag)
        for b in writes:
            b.w = tag
            b.r = []
        self.ninstr += 1
        return tag

    def barrier(self):
        tags = []
        for e in ("pe", "act", "dve", "pool"):
            k, c = self.cur[e]
            if c > 0:
                tags.append((k, c))
        for k, v in self.dma_tot.items():
            if v > 0:
                tags.append((k, v))
        for e in self.eng:
            for t in tags:
                self._wait(e, t)


def make_consts(NTP):
    import ml_dtypes
    c = {}
    j = np.arange(128)[:, None]
    i = np.arange(128)[None, :]
    bj, tj = j // 8, j % 8
    bi, ti = i // 8, i % 8
    same = (bj == bi)

    def neg(ok):
        return np.where(ok, 0.0, NEG).astype(np.float32)

    att = np.zeros((4, 128, 512), np.float32)
    att[0] = np.tile(neg(j <= i), (1, 4))
    att[1] = np.tile(neg(j > i), (1, 4))
    att[2] = np.tile(neg(same & (tj <= ti)), (1, 4))
    t64 = (np.arange(64) % 8)[None, :]
    att[3] = np.tile(neg(j > t64), (1, 8))
    c["c_att"] = att.transpose(1, 0, 2).copy()
    gm = np.zeros((128, 2, 128), np.float32)
    gm[:, 0] = (j <= i)
    gm[:, 1] = same & (tj <= ti)
    c["c_gm"] = gm
    gu = np.zeros((128, 4, 128), np.float32)
    gu[:, 0] = np.where(j <= i, -1.0 / 16, 0.0)
    gu[:, 1] = np.where(j > i, -1.0 / 16, 0.0)
    gu[:, 2] = np.where(same & (tj <= ti), -1.0 / 16, 0.0)
    gu[:, 3] = np.where(same & (tj > ti), -1.0 / 16, 0.0)
    c["c_gu"] = gu
    ind = np.zeros((128, 16), np.float32)
    ind[np.arange(128), np.arange(128) // 8] = 1.0
    c["c_ind"] = ind
    pm = np.zeros((128, 6, 4, 128), np.float32)
    eye = (j == i).astype(np.float32)
    for g, w in enumerate((2, 4, 8, 16)):
        pm[:, 0, g] = np.where((j <= i) & (j > i - w), 1.0 / w, 0.0) - eye
        pm[:, 1, g] = np.where(j >= 129 + i - w, 1.0 / w, 0.0)
        cnt = np.minimum(i + 1, w).astype(np.float32)
        pm[:, 2, g] = np.where((j <= i) & (j > i - w), 1.0 / cnt, 0.0) - eye
        pm[:, 3, g] = np.where(same & (tj <= ti) & (tj > ti - w), 1.0 / w, 0.0) - eye
        rows = np.arange(240)[:, None]
        rb, rr = rows // 15, rows % 15
        mp = np.where((rb == bi) & (rr >= ti + 16 - w), 1.0 / w, 0.0)
        pm[:, 4, g] = mp[:128]
        pm[:112, 5, g] = mp[128:]
    c["c_pm"] = pm
    T = NTP * 128 + 128
    pos = np.concatenate([np.arange(NTP * 128), 16384 + (np.arange(128) % 8)]).astype(np.float32)
    inv = (np.float32(500000.0) ** (-np.arange(8, dtype=np.float32) / np.float32(8))).astype(np.float32)
    ang = (pos[None, :] * inv[:, None]).astype(np.float32)
    rc = np.ones((64, T), np.float32)
    rs = np.zeros((64, T), np.float32)
    rc[0:8] = np.cos(ang)
    rc[8:16] = np.cos(ang)
    rs[0:8] = -np.sin(ang)
    rs[8:16] = np.sin(ang)
    c["c_rope"] = np.stack([rc, rs], 1).copy()
    perm = np.zeros((64, 64), np.float32)
    for m in range(16):
        perm[(m + 8) % 16, m] = 1.0
    misc = np.zeros((128, 4, 128), np.float32)
    misc[:, 0] = np.eye(128)
    misc[:64, 1, :64] = perm
    misc[:, 2, :16] = np.arange(16)[None, :]
    misc[:, 3, :] = np.arange(128)[None, :]
    c["c_misc"] = misc
    return c


class Prog:
    def __init__(self, NTP=16, DEPTH=2, dbg=()):
        self.NTP = NTP
        self.NT = NTP + 1
        self.T = self.NT * 128
        self.DEPTH = DEPTH
        self.dbg = dbg
        self.kb = KB()
        self.nc = self.kb.nc
        self.dbg_outs = []
        self.phases = "ABCDE"
        self.peer_mode = "dense"

    def mm(self, out, lhsT, rhs, start, stop, reads, bank):
        nc = self.nc
        return self.kb.op("pe", lambda: nc.tensor.matmul(out, lhsT=lhsT, rhs=rhs, start=start, stop=stop), reads=reads, writes=[bank])

    def act(self, out, in_, func, reads, writes, bias=None, scale=1.0):
        nc = self.nc
        if bias is None:
            return self.kb.op("act", lambda: nc.scalar.activation(out=out, in_=in_, func=func, scale=scale), reads=reads, writes=writes)
        return self.kb.op("act", lambda: nc.scalar.activation(out=out, in_=in_, func=func, bias=bias, scale=scale), reads=reads, writes=writes)

    def tt(self, e, out, in0, in1, op, reads, writes):
        eng = self.kb.eng[e]
        return self.kb.op(e, lambda: eng.tensor_tensor(out=out, in0=in0, in1=in1, op=op), reads=reads, writes=writes)

    def ts(self, e, out, in0, s1, s2, op0, op1, reads, writes):
        eng = self.kb.eng[e]
        if op1 is None:
            return self.kb.op(e, lambda: eng.tensor_scalar(out=out, in0=in0, scalar1=s1, scalar2=None, op0=op0), reads=reads, writes=writes)
        return self.kb.op(e, lambda: eng.tensor_scalar(out=out, in0=in0, scalar1=s1, scalar2=s2, op0=op0, op1=op1), reads=reads, writes=writes)

    def stt(self, out, in0, scalar, in1, op0, op1, reads, writes, accum_out=None):
        nc = self.nc
        if accum_out is None:
            return self.kb.op("dve", lambda: nc.vector.scalar_tensor_tensor(out=out, in0=in0, scalar=scalar, in1=in1, op0=op0, op1=op1), reads=reads, writes=writes)
        return self.kb.op("dve", lambda: nc.vector.scalar_tensor_tensor(out=out, in0=in0, scalar=scalar, in1=in1, op0=op0, op1=op1, accum_out=accum_out), reads=reads, writes=writes)

    def cp(self, e, out, in_, reads, writes):
        if e == "act":
            nc = self.nc
            return self.kb.op("act", lambda: nc.scalar.copy(out=out, in_=in_), reads=reads, writes=writes)
        eng = self.kb.eng[e]
        return self.kb.op(e, lambda: eng.tensor_copy(out=out, in_=in_), reads=reads, writes=writes)

    def ld(self, q, out, in_, reads, writes):
        eng = self.kb.eng[q]
        return self.kb.dma(q, lambda: eng.dma_start(out=out, in_=in_), reads=reads, writes=writes)

    def bank(self):
        b = self.banks[self.bank_i]
        self.bank_i = (self.bank_i + 1) % 8
        return b

    def dump(self, name, buf, ap, shape, dtype=F32):
        d = self.kb.dram("dbg_" + name, list(shape), dtype, "ExternalOutput")
        self.ld("sp", d[:], ap, [buf], [d])
        self.dbg_outs.append("dbg_" + name)

    def build(self):
        kb, nc = self.kb, self.nc
        NTP, NT, T, DEPTH = self.NTP, self.NT, self.T, self.DEPTH
        I = lambda n, s, dt=F32: kb.dram(n, s, dt, "ExternalInput")
        O = lambda n, s, dt=F32: kb.dram(n, s, dt, "ExternalOutput")
        self.d = d = {}
        d["xT"] = I("xT", [128, 8, T])
        d["swkT"] = I("swkT", [DEPTH, 64, 16, 2, 128])
        d["swv"] = I("swv", [DEPTH, 128, 16, 2, 64])
        d["sgla"] = I("sgla", [DEPTH, 4, 128, 16, 256])
        d["spool"] = I("spool", [DEPTH, 240, 1024])
        d["w_in"] = I("w_in", [DEPTH, 1024, INW])
        d["vecF"] = I("vecF", [DEPTH, 128, VF_N])
        d["b_in"] = I("b_in", [DEPTH, INW])
        d["sinks"] = I("sinks", [DEPTH, 16])
        d["w_alpha"] = I("w_alpha", [DEPTH, 16, 512])
        d["b_alpha"] = I("b_alpha", [DEPTH, 512])
        d["w_pool"] = I("w_pool", [DEPTH, 4, 256, 256])
        d["w_ba"] = I("w_ba", [DEPTH, 1024, 1024])
        d["w_bb"] = I("w_bb", [DEPTH, 1024, 1024])
        d["w_bc"] = I("w_bc", [DEPTH, 1024, 1024])
        d["w_out"] = I("w_out", [DEPTH, 1024, 1024])
        d["pq"] = I("pq", [DEPTH, 1024, 2048])
        d["skT"] = I("skT", [DEPTH, 128, 16, 128])
        for l_ in range(DEPTH):
            if self.peer_mode == "dense":
                d["puT%d" % l_] = I("puT%d" % l_, [128, 128, 1024])
                d["pvp%d" % l_] = I("pvp%d" % l_, [128, 128, 1024])
                d["ub16_%d" % l_] = kb.dram("ub16_%d" % l_, [128, 128, 1024], BF16, "Internal")
                d["vb16_%d" % l_] = kb.dram("vb16_%d" % l_, [128, 128, 1024], BF16, "Internal")
            else:
                d["pu%d" % l_] = I("pu%d" % l_, [16384, 1024])
                d["pv%d" % l_] = I("pv%d" % l_, [16384, 1024])
        for k, s in (("c_att", [128, 4, 512]), ("c_gm", [128, 2, 128]), ("c_gu", [128, 4, 128]), ("c_ind", [128, 16]),
                     ("c_pm", [128, 6, 4, 128]), ("c_rope", [64, 2, T]), ("c_misc", [128, 4, 128])):
            d[k] = I(k, s)
        d["yT"] = O("yT", [128, 8, T])
        d["o_pkT"] = O("o_pkT", [DEPTH, 64, 2, 128])
        d["o_pv"] = O("o_pv", [DEPTH, 128, 128])
        d["o_pg"] = O("o_pg", [DEPTH, 4, 128, 256])
        d["o_pp"] = O("o_pp", [DEPTH, 15, 1024])
        d["o_skT"] = O("o_skT", [DEPTH, 64, 16, 2, 128])
        d["o_sv"] = O("o_sv", [DEPTH, 16, 128, 128])
        d["o_sg"] = O("o_sg", [DEPTH, 4, 128, 16, 256])
        d["o_sp"] = O("o_sp", [DEPTH, 16, 15, 1024])

        with ExitStack() as es:
            kb.start(es)
            self.banks = [kb.ps("bank%d" % i, [128, 512]) for i in range(8)]
            self.bank_i = 0
            self.xT_t = es.enter_context(nc.sbuf_tensor("xTres", [128, 8, T], F32))
            self.xT = [Buf(self.xT_t[:, :, n * 128:(n + 1) * 128], "xT%d" % n) for n in range(NT)]
            self.misc = kb.sb("misc", [128, 4, 128], F32)
            self.ident = self.misc[:, 0, :]
            self.cbf = kb.sb("cbf", [128, 4, 128], BF16)
            self.vecF = kb.sb("vecF", [128, VF_N], F32)
            self.ld("sp", self.misc[:], d["c_misc"][:], [d["c_misc"]], [self.misc])
            self.cp("dve", self.cbf[:, 0, :], self.misc[:, 0, :], [self.misc], [self.cbf])
            kb.op("dve", lambda: nc.vector.memset(self.cbf[:, 1, :], 1.0), writes=[self.cbf])
            kb.op("dve", lambda: nc.vector.memset(self.cbf[:, 2, :], 1.0 / 1024), writes=[self.cbf])
            kb.op("dve", lambda: nc.vector.memset(self.cbf[:, 3, :], 1.0 / 256), writes=[self.cbf])
            for n in range(NT):
                self.ld("sp", self.xT[n][:], d["xT"][:, :, n * 128:(n + 1) * 128], [d["xT"]], [self.xT[n]])
            for l in range(DEPTH):
                self.layer(l)
            for n in range(NT):
                self.ld("sp", d["yT"][:, :, n * 128:(n + 1) * 128], self.xT[n][:], [self.xT[n]], [d["yT"]])
            kb.barrier()
        return self

    def cast_x(self, n, pool):
        xb = pool[self.xb_i % len(pool)]
        self.xb_i += 1
        self.cp("dve", xb[:], self.xT[n][:], [self.xT[n]], [xb])
        return xb

    def load_w(self, dst, src_ap, src):
        self.ld("pool", dst[:], src_ap, [src], [dst])

    def gate_and_merge(self, l, n, xb, Wg, gcol, Wbr, orows, oT, first, es_sig):
        sig, tmp = es_sig
        for half in range(2):
            bb = self.bank()
            bg = self.bank()
            for c4 in range(4):
                c = half * 4 + c4
                nk = len(orows)
                for ki, k in enumerate(orows):
                    self.mm(bb[:, c4 * 128:(c4 + 1) * 128], Wbr[:, k, c * 128:(c + 1) * 128], oT[:, ki, :], ki == 0, ki == nk - 1, [Wbr, oT], bb)
                for dc in range(8):
                    self.mm(bg[:, c4 * 128:(c4 + 1) * 128], Wg[:, dc, c * 128:(c + 1) * 128], xb[:, dc, :], dc == 0, dc == 7, [Wg, xb], bg)
            for c4 in range(4):
                c = half * 4 + c4
                self.act(sig[:, c4, :], bg[:, c4 * 128:(c4 + 1) * 128], ACT.Sigmoid, [bg, self.vecF], [sig],
                         bias=self.vecF[:, VF_BGT + gcol * 8 + c:VF_BGT + gcol * 8 + c + 1])
            mslice = self.merged[n][:, half * 4:(half + 1) * 4, :]
            bv = bb[:].rearrange("p (a b) -> p a b", a=4)
            if first:
                self.tt("dve", mslice, bv, sig[:], ALU.mult, [bb, sig], [self.merged[n]])
            else:
                self.tt("dve", tmp[:], bv, sig[:], ALU.mult, [bb, sig], [tmp])
                self.tt("pool", mslice, mslice, tmp[:], ALU.add, [tmp, self.merged[n]], [self.merged[n]])

    def layer(self, l):
        kb, nc, d = self.kb, self.nc, self.d
        NTP, NT = self.NTP, self.NT
        kb.barrier()
        self.ld("sp", self.vecF[:], d["vecF"][l], [d["vecF"]], [self.vecF])
        with ExitStack() as les:
            mt = les.enter_context(nc.sbuf_tensor("merged%d" % l, [128, 8, self.T], BF16))
            self.merged = [Buf(mt[:, :, n * 128:(n + 1) * 128], "mg%d" % n) for n in range(NT)]
            if "A" in self.phases:
                with nc.named_scope("A%d" % l):
                    self.phase_attn(l)
            if "B" in self.phases:
                for hh in range(4):
                    with nc.named_scope("B%d_%d" % (l, hh)):
                        self.phase_gla(l, hh)
            if "C" in self.phases:
                with nc.named_scope("C%d" % l):
                    self.phase_pool(l)
            if "D" in self.phases:
                with nc.named_scope("D%d" % l):
                    self.phase_out(l)
            kb.barrier()
        if "E" in self.phases:
            with nc.named_scope("E%d" % l):
                if self.peer_mode == "dense":
                    self.phase_peer_dense(l)
                else:
                    self.phase_peer(l)

    def phase_attn(self, l):
        kb, nc, d = self.kb, self.nc, self.d
        NTP, NT = self.NTP, self.NT
        kb.barrier()
        with ExitStack() as es:
            sb = lambda n, s, dt: kb.sb(n, s, dt, es)
            Wqk = sb("Wqk", [128, 8, 1152], BF16)
            Wv = sb("Wv", [128, 8, 128], BF16)
            Wg = sb("WgA", [128, 8, 1024], BF16)
            Wa = sb("Wa", [128, 8, 1024], BF16)
            win = d["w_in"]
            wv = lambda c0, c1: win[l, :, c0:c1].rearrange("(c p) n -> p c n", p=128)
            self.load_w(Wqk, wv(C_QA, C_VA), win)
            self.load_w(Wv, wv(C_VA, C_QB), win)
            self.load_w(Wg, wv(C_GT, C_GT + 1024), win)
            self.load_w(Wa, d["w_ba"][l].rearrange("(c p) n -> p c n", p=128), d["w_ba"])
            amask = sb("amask", [128, 4, 512], BF16)
            self.load_w(amask, d["c_att"][:], d["c_att"])
            bva = sb("bva", [128, 128], F32)
            self.ld("sp", bva[:], d["b_in"][l, C_VA:C_QB].partition_broadcast(128), [d["b_in"]], [bva])
            esk = sb("esk", [128, 16], F32)
            self.ld("sp", esk[:], d["sinks"][l, :].partition_broadcast(128), [d["sinks"]], [esk])
            self.act(esk[:], esk[:], ACT.Exp, [esk], [esk])
            xbp = [sb("xbA%d" % i, [128, 8, 128], BF16) for i in range(2)]
            self.xb_i = 0
            rope = [sb("rope%d" % i, [64, 2, 128], F32) for i in range(2)]
            qf = [sb("qf%d" % i, [64, 4, 128], F32) for i in range(2)]
            t2 = [sb("t2%d" % i, [64, 4, 128], F32) for i in range(2)]
            QTs = sb("QTs", [64, 16, 128], BF16)
            KTs = [sb("KT%d" % i, [64, 2, 128], BF16) for i in range(2)]
            krf = sb("krf", [64, 2, 128], F32)
            Vd = [sb("Vd%d" % i, [128, 2, 2, 64], BF16) for i in range(2)]
            vf = sb("vf", [128, 128], F32)
            Pown = [sb("Pown%d" % i, [128, 512], BF16) for i in range(2)]
            Pprev = [sb("Pprev%d" % i, [128, 512], BF16) for i in range(2)]
            rden = [sb("rden%d" % i, [128, 512], F32) for i in range(1)]
            oaT = [sb("oaT%d" % i, [128, 8, 128], BF16) for i in range(2)]
            sig = [sb("sigA%d" % i, [128, 4, 128], F32) for i in range(1)]
            kvb = sb("kvb", [128, 4096], BF16)
            kbT = kvb[0:64, :].rearrange("p (b g t) -> p b g t", b=16, g=2)
            vbd = kvb[:, :].rearrange("p (b g a e) -> p b g a e", b=16, g=2, a=2)
            Psp = sb("Psp", [128, 32, 64], BF16)
            perm = self.misc[0:64, 1, 0:64]
            pi = 0
            for n in range(NT):
                samp = (n == NTP)
                first = (n == 0)
                xb = self.cast_x(n, xbp)
                rp = rope[n % 2]
                self.ld("sp", rp[:], d["c_rope"][:, :, n * 128:(n + 1) * 128], [d["c_rope"]], [rp])
                Q = QTs
                KT = KTs[n % 2]
                KTp = KTs[(n + 1) % 2]
                for hg in range(5):
                    nh = 4 if hg < 4 else 2
                    bq = self.bank()
                    for hh in range(nh):
                        h = hg * 4 + hh
                        for dc in range(8):
                            self.mm(bq[0:64, hh * 128:(hh + 1) * 128], Wqk[:, dc, h * 64:(h + 1) * 64], xb[:, dc, :], dc == 0, dc == 7, [Wqk, xb], bq)
                    q_ = qf[pi % 2]
                    t_ = t2[pi % 2]
                    pi += 1
                    for hh in range(nh):
                        h = hg * 4 + hh
                        self.act(q_[:, hh, :], bq[0:64, hh * 128:(hh + 1) * 128], ACT.Identity, [bq, self.vecF], [q_],
                                 bias=self.vecF[0:64, VF_BQ + h:VF_BQ + h + 1])
                    bp = self.bank()
                    self.mm(bp[0:64, 0:nh * 128], perm, q_[:, 0:nh, :], True, True, [self.misc, q_], bp)
                    cb = rp[:, 0, :].unsqueeze(1).to_broadcast([64, nh, 128])
                    sbb = rp[:, 1, :].unsqueeze(1).to_broadcast([64, nh, 128])
                    self.tt("dve", t_[:, 0:nh, :], bp[0:64, 0:nh * 128].rearrange("p (a b) -> p a b", a=nh), sbb, ALU.mult, [bp, rp], [t_])
                    self.tt("pool", q_[:, 0:nh, :], q_[:, 0:nh, :], cb, ALU.mult, [q_, rp], [q_])
                    if hg < 4:
                        self.tt("dve", Q[:, hg * 4:hg * 4 + nh, :], q_[:, 0:nh, :], t_[:, 0:nh, :], ALU.add, [q_, t_], [Q])
                    else:
                        self.tt("dve", KT[:], q_[:, 0:nh, :], t_[:, 0:nh, :], ALU.add, [q_, t_], [KT])
                    if hg == 4 and (n == NTP - 1 or samp):
                        self.tt("pool", krf[:], q_[:, 0:2, :], t_[:, 0:2, :], ALU.add, [q_, t_], [krf])
                bv = self.bank()
                for dc in range(8):
                    self.mm(bv[:, 0:128], xb[:, dc, :], Wv[:, dc, :], dc == 0, dc == 7, [xb, Wv], bv)
                V = Vd[n % 2]
                Vp = Vd[(n + 1) % 2]
                bvv = bv[:, 0:128].rearrange("p (g e) -> p g e", g=2).unsqueeze(2).to_broadcast([128, 2, 2, 64])
                bia = bva[:].rearrange("p (g e) -> p g e", g=2).unsqueeze(2).to_broadcast([128, 2, 2, 64])
                self.tt("dve", V[:], bvv, bia, ALU.add, [bv, bva], [V])
                if n == NTP - 1 or samp:
                    self.tt("dve", vf[:], bv[:, 0:128], bva[:], ALU.add, [bv, bva], [vf])
                if n == NTP - 1:
                    self.ld("sp", d["o_pkT"][l], krf[:], [krf], [d["o_pkT"]])
                    self.ld("sp", d["o_pv"][l], vf[:], [vf], [d["o_pv"]])
                if samp:
                    self.ld("sp", d["o_skT"][l, :, :, :, 0:120], d["swkT"][l, :, :, :, 8:128], [d["swkT"]], [d["o_skT"]])
                    for g_ in range(2):
                        self.ld("sp", d["o_skT"][l, :, :, g_, 120:128], krf[:, g_, :].rearrange("e (b t) -> e b t", t=8), [krf], [d["o_skT"]])
                    self.ld("sp", d["o_sv"][l, :, 0:120, :], d["swv"][l, 8:128].rearrange("p b g e -> b p (g e)"), [d["swv"]], [d["o_sv"]])
                    for b in range(16):
                        self.ld("sp", d["o_sv"][l, b, 120:128, :], vf[8 * b:8 * b + 8, :], [vf], [d["o_sv"]])
                    kb.dma("pool", lambda: nc.gpsimd.dma_start(out=kbT, in_=d["swkT"][l]), reads=[d["swkT"]], writes=[kvb])
                    for q4 in range(4):
                        bs = self.bank()
                        self.mm(bs[:, :], self.cbf[:, 0, :], amask[:, 3, :], True, False, [self.cbf, amask], bs)
                        for bl in range(8):
                            blk = q4 * 8 + bl
                            b, g = blk // 2, blk % 2
                            rhs = Q[:, 8 * g:8 * g + 8, 8 * b:8 * b + 8]
                            self.mm(bs[:, bl * 64:(bl + 1) * 64], kbT[:, b, g, :], rhs, False, bl == 7, [kvb, Q], bs)
                        self.act(Psp[:, q4 * 8:(q4 + 1) * 8, :], bs[:].rearrange("p (a b) -> p a b", a=8), ACT.Exp, [bs], [Psp], scale=0.125)
                    for a_ in range(2):
                        kb.dma("pool", lambda: nc.gpsimd.dma_start(out=vbd[:, :, :, a_, :], in_=d["swv"][l]), reads=[d["swv"]], writes=[kvb])
                for hg in range(4):
                    g = hg // 2
                    Po = Pown[hg % 2]
                    Pp = Pprev[hg % 2]
                    rd = rden[0]
                    bo = self.bank()
                    rq = Q[:, hg * 4:(hg + 1) * 4, :]
                    self.mm(bo[:], KT[:, g, :], rq, True, False, [KT, Q], bo)
                    self.mm(bo[:], self.cbf[:, 0, :], amask[:, 2 if samp else 0, :], False, True, [self.cbf, amask], bo)
                    self.act(Po[:], bo[:], ACT.Exp, [bo], [Po], scale=0.125)
                    use_prev = (not first) and (not samp)
                    if use_prev:
                        bpv = self.bank()
                        self.mm(bpv[:], KTp[:, g, :], rq, True, False, [KTp, Q], bpv)
                        self.mm(bpv[:], self.cbf[:, 0, :], amask[:, 1, :], False, True, [self.cbf, amask], bpv)
                        self.act(Pp[:], bpv[:], ACT.Exp, [bpv], [Pp], scale=0.125)
                    bO = self.bank()
                    bD = self.bank()
                    self.mm(bO[:], V[:, g].rearrange("p a e -> p (a e)"), Po[:], True, (not use_prev) and (not samp), [V, Po], bO)
                    if use_prev:
                        self.mm(bO[:], Vp[:, g].rearrange("p a e -> p (a e)"), Pp[:], False, True, [Vp, Pp], bO)
                    self.mm(bD[:], self.cbf[:, 1, :], Po[:], True, (not use_prev) and (not samp), [self.cbf, Po], bD)
                    if use_prev:
                        self.mm(bD[:], self.cbf[:, 1, :], Pp[:], False, True, [self.cbf, Pp], bD)
                    if samp:
                        hl = (hg % 2) * 4
                        for b in range(16):
                            rhs = Psp[:, b * 2 + g, hl * 8:(hl + 4) * 8]
                            oO = bO[:].rearrange("p (a t) -> p a t", a=4)[:, :, 8 * b:8 * b + 8]
                            oD = bD[:].rearrange("p (a t) -> p a t", a=4)[:, :, 8 * b:8 * b + 8]
                            self.mm(oO, vbd[:, b, g].rearrange("p a e -> p (a e)"), rhs, False, b == 15, [kvb, Psp], bO)
                            self.mm(oD, self.cbf[:, 1, :], rhs, False, b == 15, [self.cbf, Psp], bD)
                    else:
                        pass
                    for hh_ in range(4):
                        self.ts("dve", rd[:, hh_ * 128:(hh_ + 1) * 128], bD[:, hh_ * 128:(hh_ + 1) * 128], esk[:, hg * 4 + hh_:hg * 4 + hh_ + 1], None, ALU.add, None, [bD, esk], [rd])
                    kb.op("dve", lambda: nc.vector.reciprocal(out=rd[:], in_=rd[:]), reads=[rd], writes=[rd])
                    oa = oaT[n % 2]
                    bO3 = bO[:].rearrange("p (a t) -> p a t", a=4)
                    rd3 = rd[:].rearrange("p (a t) -> p a t", a=4)
                    self.tt("dve", oa[0:64, hg * 2:hg * 2 + 2, :], bO3[0:64, 0:4:2, :], rd3[0:64, 0:4:2, :], ALU.mult, [bO, rd], [oa])
                    self.tt("dve", oa[64:128, hg * 2:hg * 2 + 2, :], bO3[64:128, 1:4:2, :], rd3[64:128, 1:4:2, :], ALU.mult, [bO, rd], [oa])
                if "oa" in self.dbg:
                    self.dump("oa_%d_%d" % (l, n), oaT[n % 2], oaT[n % 2][:], [128, 8, 128], BF16)
                self.gate_and_merge(l, n, xb, Wg, 0, Wa, list(range(8)), oaT[n % 2], True, (sig[0], None))
            kb.barrier()

    def phase_gla(self, l, hh):
        kb, nc, d = self.kb, self.nc, self.d
        NTP, NT = self.NTP, self.NT
        kb.barrier()
        with ExitStack() as es:
            sb = lambda n, s, dt: kb.sb(n, s, dt, es)
            win = d["w_in"]
            wv = lambda c0, c1: win[l, :, c0:c1].rearrange("(c p) n -> p c n", p=128)
            Wq = sb("Wq", [128, 8, 128], BF16)
            Wk = sb("Wk", [128, 8, 128], BF16)
            Wv = sb("WvB", [128, 8, 256], BF16)
            Wgb = sb("Wgb", [128, 8, 256], BF16)
            Wlr = sb("Wlr", [128, 8, 16], BF16)
            Wal = sb("Wal", [16, 128], F32)
            Wg = sb("WgB", [128, 8, 1024], BF16)
            Wb = sb("Wb", [128, 2, 1024], BF16)
            self.load_w(Wq, wv(C_QB + hh * 128, C_QB + (hh + 1) * 128), win)
            self.load_w(Wk, wv(C_KB + hh * 128, C_KB + (hh + 1) * 128), win)
            self.load_w(Wv, wv(C_VB + hh * 256, C_VB + (hh + 1) * 256), win)
            self.load_w(Wgb, wv(C_GB + hh * 256, C_GB + (hh + 1) * 256), win)
            self.load_w(Wlr, wv(C_LR, C_LR + 16), win)
            self.load_w(Wg, wv(C_GT + 1024, C_GT + 2048), win)
            self.load_w(Wb, d["w_bb"][l, hh * 256:(hh + 1) * 256, :].rearrange("(c p) n -> p c n", p=128), d["w_bb"])
            self.ld("sp", Wal[:], d["w_alpha"][l, :, hh * 128:(hh + 1) * 128], [d["w_alpha"]], [Wal])
            bkb = sb("bkb", [128, 128], F32)
            bvb = sb("bvb", [128, 256], F32)
            bal = sb("bal", [128, 128], F32)
            self.ld("sp", bkb[:], d["b_in"][l, C_KB + hh * 128:C_KB + (hh + 1) * 128].partition_broadcast(128), [d["b_in"]], [bkb])
            self.ld("sp", bvb[:], d["b_in"][l, C_VB + hh * 256:C_VB + (hh + 1) * 256].partition_broadcast(128), [d["b_in"]], [bvb])
            self.ld("sp", bal[:], d["b_alpha"][l, hh * 128:(hh + 1) * 128].partition_broadcast(128), [d["b_alpha"]], [bal])
            gm = sb("gm", [128, 2, 128], BF16)
            gu = sb("gu", [128, 4, 128], F32)
            ind = sb("ind", [128, 16], F32)
            self.load_w(gm, d["c_gm"][:], d["c_gm"])
            self.ld("sp", gu[:], d["c_gu"][:], [d["c_gu"]], [gu])
            self.ld("sp", ind[:], d["c_ind"][:], [d["c_ind"]], [ind])
            S = sb("S", [128, 256], F32)
            Sbs = [sb("Sb%d" % i, [128, 256], BF16) for i in range(2)]
            kb.op("dve", lambda: nc.vector.memset(S[:], 0.0), writes=[S])
            kb.op("dve", lambda: nc.vector.memset(Sbs[1][:], 0.0), writes=[Sbs[1]])
            S0 = sb("S0", [128, 16, 256], F32)
            S0b = sb("S0b", [128, 16, 256], BF16)
            QM = sb("QM", [128, 16, 128], BF16)
            kb.op("pool", lambda: nc.gpsimd.memset(QM[:], 0.0), writes=[QM])
            xbp = [sb("xbB%d" % i, [128, 8, 128], BF16) for i in range(2)]
            self.xb_i = 0
            gla_specs = (("lrT", [16, 128], F32), ("zb", [128, 128], F32), ("lsp", [128, 128], F32), ("ebs", [128, 128], F32),
                         ("enb", [128, 128], F32), ("eb", [128, 128], F32), ("erb", [128, 128], F32), ("qd", [128, 128], BF16),
                         ("kd", [128, 128], BF16), ("ktm", [128, 128], F32), ("kl", [128, 128], BF16), ("vB", [128, 256], BF16),
                         ("attm", [128, 128], BF16), ("sq", [128, 256], BF16), ("sd", [128, 128], F32), ("rstd", [128, 128], F32),
                         ("gsl", [128, 2, 128], F32), ("otmp", [128, 128], F32))
            gla_sets = [[sb("%s_%d" % (nm, i), shp, dt) for (nm, shp, dt) in gla_specs] for i in range(2)]
            klm = [sb("klm%d" % i, [128, 128], BF16) for i in range(2)]
            obT = [sb("obT%d" % i, [128, 2, 128], BF16) for i in range(2)]
            mtmp = [sb("mtmpB%d" % i, [128, 4, 128], F32) for i in range(2)]
            lnscale = float(np.log(128.0 ** -0.5))
            lnsc = sb("lnsc", [128, 1], F32)
            eps6 = sb("eps6", [128, 1], F32)
            kb.op("dve", lambda: nc.vector.memset(lnsc[:], lnscale), writes=[lnsc])
            kb.op("dve", lambda: nc.vector.memset(eps6[:], 1e-6), writes=[eps6])
            sig8 = [sb("sig8_%d" % i, [128, 8, 128], F32) for i in range(2)]

            def S1(n):
                samp = (n == NTP)
                (lrT, zb, lsp, ebs, enb, eb, erb, qd, kd, ktm, kl, v, attm, sq, sd, rstd, gsl, otmp) = gla_sets[n % 2]
                xb = self.cast_x(n, xbp)
                if samp:
                    self.ld("sp", S0[:], d["sgla"][l, hh], [d["sgla"]], [S0])
                    self.load_w(S0b, d["sgla"][l, hh], d["sgla"])
                b1 = self.bank()
                for dc in range(8):
                    self.mm(b1[0:16, 0:128], Wlr[:, dc, :], xb[:, dc, :], dc == 0, dc == 7, [Wlr, xb], b1)
                self.act(lrT[:], b1[0:16, 0:128], ACT.Identity, [b1, self.vecF], [lrT], bias=self.vecF[0:16, VF_BLR:VF_BLR + 1])
                b2 = self.bank()
                self.mm(b2[:, 0:128], lrT[:], Wal[:], True, True, [lrT, Wal], b2)
                self.tt("dve", zb[:], b2[:, 0:128], bal[:], ALU.add, [b2, bal], [zb])
                self.act(zb[:], zb[:], ACT.Exp, [zb], [zb], scale=-1.0)
                self.act(lsp[:], zb[:], ACT.Ln, [zb], [lsp], bias=1.0)
                kU = 2 if samp else 0
                b3 = self.bank()
                self.mm(b3[:, 0:128], lsp[:], gu[:, kU, :], True, True, [lsp, gu], b3)
                self.mm(b3[:, 128:256], gu[:, kU + 1, :], lsp[:], True, True, [lsp, gu], b3)
                self.act(ebs[:], b3[:, 0:128], ACT.Exp, [b3, lnsc], [ebs], bias=lnsc[:, 0:1])
                self.act(enb[:], b3[:, 0:128], ACT.Exp, [b3], [enb], scale=-1.0)
                self.act(eb[:], b3[:, 0:128], ACT.Exp, [b3], [eb])
                self.act(erb[:], b3[:, 128:256], ACT.Exp, [b3], [erb])
                b4 = self.bank()
                for dc in range(8):
                    self.mm(b4[:, 0:128], Wq[:, dc, :], xb[:, dc, :], dc == 0, dc == 7, [Wq, xb], b4)
                for dc in range(8):
                    self.mm(b4[:, 128:256], Wk[:, dc, :], xb[:, dc, :], dc == 0, dc == 7, [Wk, xb], b4)
                self.stt(qd[:], b4[:, 0:128], self.vecF[:, VF_BQB + hh:VF_BQB + hh + 1], ebs[:], ALU.add, ALU.mult, [b4, self.vecF, ebs], [qd])
                self.stt(kd[:], b4[:, 128:256], self.vecF[:, VF_BKB + hh:VF_BKB + hh + 1], enb[:], ALU.add, ALU.mult, [b4, self.vecF, enb], [kd])
                b5 = self.bank()
                for dc in range(8):
                    self.mm(b5[:, 0:128], xb[:, dc, :], Wk[:, dc, :], dc == 0, dc == 7, [Wk, xb], b5)
                for dc in range(8):
                    self.mm(b5[:, 128:384], xb[:, dc, :], Wv[:, dc, :], dc == 0, dc == 7, [Wv, xb], b5)
                self.tt("dve", ktm[:], b5[:, 0:128], bkb[:], ALU.add, [b5, bkb], [ktm])
                self.tt("pool", kl[:], ktm[:], erb[:], ALU.mult, [ktm, erb], [kl])
                self.tt("dve", v[:], b5[:, 128:384], bvb[:], ALU.add, [b5, bvb], [v])
                if samp:
                    qm_diag = bass.AP(tensor=QM.t, offset=0, ap=[[16 * 128, 128], [128 + 8, 16], [1, 8]])
                    self.cp("dve", qm_diag, qd[:].rearrange("p (b t) -> p b t", t=8), [qd], [QM])
                b9 = self.bank()
                for dvc in range(2):
                    for dc in range(8):
                        self.mm(b9[:, dvc * 128:(dvc + 1) * 128], Wgb[:, dc, dvc * 128:(dvc + 1) * 128], xb[:, dc, :], dc == 0, dc == 7, [Wgb, xb], b9)
                for dvc in range(2):
                    cidx = VF_BGB + hh * 2 + dvc
                    self.act(gsl[:, dvc, :], b9[:, dvc * 128:(dvc + 1) * 128], ACT.Silu, [b9, self.vecF], [gsl], bias=self.vecF[:, cidx:cidx + 1])
                sg = sig8[n % 2]
                for half in range(2):
                    bg = self.bank()
                    for c4 in range(4):
                        c = half * 4 + c4
                        for dc in range(8):
                            self.mm(bg[:, c4 * 128:(c4 + 1) * 128], Wg[:, dc, c * 128:(c + 1) * 128], xb[:, dc, :], dc == 0, dc == 7, [Wg, xb], bg)
                    for c4 in range(4):
                        c = half * 4 + c4
                        self.act(sg[:, c, :], bg[:, c4 * 128:(c4 + 1) * 128], ACT.Sigmoid, [bg, self.vecF], [sg],
                                 bias=self.vecF[:, VF_BGT + 8 + c:VF_BGT + 8 + c + 1])

            def S2(n):
                samp = (n == NTP)
                (lrT, zb, lsp, ebs, enb, eb, erb, qd, kd, ktm, kl, v, attm, sq, sd, rstd, gsl, otmp) = gla_sets[n % 2]
                Sb_prev = Sbs[(n + 1) % 2]
                if not samp:
                    b10 = self.bank()
                    self.mm(b10[:, 0:256], kl[:], v[:], True, True, [kl, v], b10)
                    self.stt(S[:], S[:], eb[:, 127:128], b10[:, 0:256], ALU.mult, ALU.add, [S, eb, b10], [S])
                    self.cp("pool", Sbs[n % 2][:], S[:], [S], [Sbs[n % 2]])
                    if n == NTP - 1:
                        self.ld("sp", d["o_pg"][l, hh], S[:], [S], [d["o_pg"]])
                b6 = self.bank()
                self.mm(b6[:, 0:128], kd[:], qd[:], True, True, [kd, qd], b6)
                self.tt("dve", attm[:], b6[:, 0:128], gm[:, 1 if samp else 0, :], ALU.mult, [b6, gm], [attm])
                b7 = self.bank()
                for dvc in range(2):
                    o7 = b7[:, dvc * 128:(dvc + 1) * 128]
                    self.mm(o7, v[:, dvc * 128:(dvc + 1) * 128], attm[:], True, False, [v, attm], b7)
                    if not samp:
                        self.mm(o7, Sb_prev[:, dvc * 128:(dvc + 1) * 128], qd[:], False, True, [Sb_prev, qd], b7)
                    else:
                        for b in range(16):
                            self.mm(o7, S0b[:, b, dvc * 128:(dvc + 1) * 128], QM[:, b, :], False, b == 15, [S0b, QM], b7)
                self.act(sq[:], b7[:, 0:256], ACT.Square, [b7], [sq])
                b8 = self.bank()
                self.mm(b8[:, 0:128], self.cbf[:, 3, :], sq[:, 0:128], True, False, [self.cbf, sq], b8)
                self.mm(b8[:, 0:128], self.cbf[:, 3, :], sq[:, 128:256], False, True, [self.cbf, sq], b8)
                self.act(sd[:], b8[:, 0:128], ACT.Sqrt, [b8, eps6], [sd], bias=eps6[:, 0:1])
                kb.op("dve", lambda: nc.vector.reciprocal(out=rstd[:], in_=sd[:]), reads=[sd], writes=[rstd])
                ob = obT[n % 2]
                for dvc in range(2):
                    cidx = VF_GNG + hh * 2 + dvc
                    self.tt("dve", otmp[:], b7[:, dvc * 128:(dvc + 1) * 128], rstd[:], ALU.mult, [b7, rstd], [otmp])
                    self.stt(ob[:, dvc, :], otmp[:], self.vecF[:, cidx:cidx + 1], gsl[:, dvc, :], ALU.mult, ALU.mult, [otmp, self.vecF, gsl], [ob])
                if "ob" in self.dbg:
                    self.dump("ob_%d_%d_%d" % (l, hh, n), ob, ob[:], [128, 2, 128], BF16)
                sg = sig8[n % 2]
                tmp = mtmp[n % 2]
                for half in range(2):
                    bb = self.bank()
                    for c4 in range(4):
                        c = half * 4 + c4
                        for ki in range(2):
                            self.mm(bb[:, c4 * 128:(c4 + 1) * 128], Wb[:, ki, c * 128:(c + 1) * 128], ob[:, ki, :], ki == 0, ki == 1, [Wb, ob], bb)
                    mslice = self.merged[n][:, half * 4:(half + 1) * 4, :]
                    self.tt("dve", tmp[:], bb[:].rearrange("p (a b) -> p a b", a=4), sg[:, half * 4:(half + 1) * 4, :], ALU.mult, [bb, sg], [tmp])
                    self.tt("pool", mslice, mslice, tmp[:], ALU.add, [tmp, self.merged[n]], [self.merged[n]])
                if samp:
                    for b in range(16):
                        km = klm[b % 2]
                        self.ts("pool", km[:], kl[:], ind[:, b:b + 1], None, ALU.mult, None, [kl, ind], [km])
                        bb = self.bank()
                        self.mm(bb[:, 0:256], km[:], v[:], True, True, [km, v], bb)
                        self.stt(S0[:, b, :], S0[:, b, :], eb[:, 8 * b + 7:8 * b + 8], bb[:, 0:256], ALU.mult, ALU.add, [S0, eb, bb], [S0])
                    self.ld("sp", d["o_sg"][l, hh], S0[:], [S0], [d["o_sg"]])

            S1(0)
            for n in range(NT):
                if n + 1 < NT:
                    S1(n + 1)
                S2(n)
            kb.barrier()

    def phase_pool(self, l):
        kb, nc, d = self.kb, self.nc, self.d
        NTP, NT = self.NTP, self.NT
        kb.barrier()
        with ExitStack() as es:
            sb = lambda n, s, dt: kb.sb(n, s, dt, es)
            win = d["w_in"]
            wv = lambda c0, c1: win[l, :, c0:c1].rearrange("(c p) n -> p c n", p=128)
            Wu = sb("Wu", [128, 8, 1024], BF16)
            Wp = sb("Wp", [128, 8, 256], BF16)
            Wc = sb("Wc", [128, 8, 1024], BF16)
            Wg = sb("WgC", [128, 8, 1024], BF16)
            self.load_w(Wu, wv(C_UC, C_UC + 1024), win)
            self.load_w(Wp, d["w_pool"][l].rearrange("g (c p) e -> p (g c) e", p=128), d["w_pool"])
            self.load_w(Wc, d["w_bc"][l].rearrange("(c p) n -> p c n", p=128), d["w_bc"])
            self.load_w(Wg, wv(C_GT + 2048, C_GT + 3072), win)
            buc = sb("buc", [128, 1024], F32)
            self.ld("sp", buc[:], d["b_in"][l, C_UC:C_UC + 1024].partition_broadcast(128), [d["b_in"]], [buc])
            pm = sb("pm", [128, 6, 4, 128], BF16)
            self.load_w(pm, d["c_pm"][:], d["c_pm"])
            ub = [sb("ub%d" % i, [128, 1024], BF16) for i in range(2)]
            uf = sb("uf", [128, 1024], F32)
            sprev = sb("sprev", [128, 2, 1024], BF16)
            kb.dma("pool", lambda: nc.gpsimd.dma_start(out=sprev[:, 0, :], in_=d["spool"][l, 0:128, :]), reads=[d["spool"]], writes=[sprev])
            kb.dma("pool", lambda: nc.gpsimd.dma_start(out=sprev[0:112, 1, :], in_=d["spool"][l, 128:240, :]), reads=[d["spool"]], writes=[sprev])
            xbp = [sb("xbC%d" % i, [128, 8, 128], BF16) for i in range(2)]
            self.xb_i = 0
            dbf = sb("dbf", [128, 8, 128], BF16)
            ocT = [sb("ocT%d" % i, [128, 8, 128], BF16) for i in range(2)]
            sig = [sb("sigC%d" % i, [128, 4, 128], F32) for i in range(2)]
            mtmp = [sb("mtmpC%d" % i, [128, 4, 128], F32) for i in range(2)]
            for n in range(NT):
                samp = (n == NTP)
                xb = self.cast_x(n, xbp)
                u = ub[n % 2]
                up = ub[(n + 1) % 2]
                outt = (n == NTP - 1) or samp
                for half in range(2):
                    bu = self.bank()
                    for dc in range(8):
                        self.mm(bu[:], xb[:, dc, :], Wu[:, dc, half * 512:(half + 1) * 512], dc == 0, dc == 7, [xb, Wu], bu)
                    self.tt("dve", u[:, half * 512:(half + 1) * 512], bu[:], buc[:, half * 512:(half + 1) * 512], ALU.add, [bu, buc], [u])
                    if outt:
                        self.tt("dve", uf[:, half * 512:(half + 1) * 512], bu[:], buc[:, half * 512:(half + 1) * 512], ALU.add, [bu, buc], [uf])
                for half in range(2):
                    bd = self.bank()
                    for c4 in range(4):
                        c = half * 4 + c4
                        g = c // 2
                        o = bd[:, c4 * 128:(c4 + 1) * 128]
                        lhs = u[:, c * 128:(c + 1) * 128]
                        if samp:
                            self.mm(o, lhs, pm[:, 3, g, :], True, False, [u, pm], bd)
                            self.mm(o, sprev[:, 0, c * 128:(c + 1) * 128], pm[:, 4, g, :], False, False, [sprev, pm], bd)
                            self.mm(o, sprev[0:112, 1, c * 128:(c + 1) * 128], pm[0:112, 5, g, :], False, True, [sprev, pm], bd)
                        elif n == 0:
                            self.mm(o, lhs, pm[:, 2, g, :], True, True, [u, pm], bd)
                        else:
                            self.mm(o, lhs, pm[:, 0, g, :], True, False, [u, pm], bd)
                            self.mm(o, up[:, c * 128:(c + 1) * 128], pm[:, 1, g, :], False, True, [up, pm], bd)
                    self.cp("act", dbf[:, half * 4:(half + 1) * 4, :], bd[:].rearrange("p (a b) -> p a b", a=4), [bd], [dbf])
                oc = ocT[n % 2]
                for half in range(2):
                    by = self.bank()
                    for c4 in range(4):
                        e = half * 4 + c4
                        g, ec = e // 2, e % 2
                        for cc in range(2):
                            self.mm(by[:, c4 * 128:(c4 + 1) * 128], Wp[:, g * 2 + cc, ec * 128:(ec + 1) * 128], dbf[:, g * 2 + cc, :], cc == 0, cc == 1, [Wp, dbf], by)
                    for c4 in range(4):
                        e = half * 4 + c4
                        self.ts("dve", oc[:, e, :], by[:, c4 * 128:(c4 + 1) * 128], self.vecF[:, VF_PSC + e:VF_PSC + e + 1], None, ALU.mult, None, [by, self.vecF], [oc])
                if "oc" in self.dbg:
                    self.dump("oc_%d_%d" % (l, n), oc, oc[:], [128, 8, 128], BF16)
                self.gate_and_merge(l, n, xb, Wg, 2, Wc, list(range(8)), oc, False, (sig[n % 2], mtmp[n % 2]))
                if n == NTP - 1:
                    self.ld("sp", d["o_pp"][l], uf[113:128, :], [uf], [d["o_pp"]])
                if samp:
                    self.ld("sp", d["o_sp"][l, :, 0:7, :], d["spool"][l].rearrange("(b r) f -> b r f", r=15)[:, 8:15, :], [d["spool"]], [d["o_sp"]])
                    for b in range(16):
                        self.ld("sp", d["o_sp"][l, b, 7:15, :], uf[8 * b:8 * b + 8, :], [uf], [d["o_sp"]])
            kb.barrier()

    def ln_fm(self, y, n, gcol, bcol, bufs, bank=None):
        kb, nc = self.kb, self.nc
        ybf, ysq, mean, m2, var, eps5 = bufs
        self.cp("act", ybf[:], y[:], [y], [ybf])
        self.act(ysq[:], y[:], ACT.Square, [y], [ysq])
        bm = bank if bank is not None else self.bank()
        for c in range(8):
            self.mm(bm[:, 0:128], self.cbf[:, 2, :], ybf[:, c, :], c == 0, c == 7, [self.cbf, ybf], bm)
        for c in range(8):
            self.mm(bm[:, 128:256], self.cbf[:, 2, :], ysq[:, c, :], c == 0, c == 7, [self.cbf, ysq], bm)
        self.cp("act", mean[:], bm[:, 0:128], [bm], [mean])
        self.tt("pool", m2[:], mean[:], mean[:], ALU.mult, [mean], [m2])
        self.tt("dve", var[:], bm[:, 128:256], m2[:], ALU.subtract, [bm, m2], [var])
        self.act(var[:], var[:], ACT.Sqrt, [var, eps5], [var], bias=eps5[:, 0:1])
        kb.op("dve", lambda: nc.vector.reciprocal(out=var[:], in_=var[:]), reads=[var], writes=[var])
        mb = mean[:].unsqueeze(1).to_broadcast([128, 8, 128])
        rb = var[:].unsqueeze(1).to_broadcast([128, 8, 128])
        self.tt("dve", y[:], y[:], mb, ALU.subtract, [y, mean], [y])
        self.tt("pool", y[:], y[:], rb, ALU.mult, [y, var], [y])
        for c in range(8):
            self.ts("dve", self.xT[n][:, c, :], y[:, c, :], self.vecF[:, gcol + c:gcol + c + 1], self.vecF[:, bcol + c:bcol + c + 1],
                    ALU.mult, ALU.add, [y, self.vecF], [self.xT[n]])

    def phase_out(self, l):
        kb, nc, d = self.kb, self.nc, self.d
        NTP, NT = self.NTP, self.NT
        kb.barrier()
        with ExitStack() as es:
            sb = lambda n, s, dt: kb.sb(n, s, dt, es)
            Wo = sb("Wo", [128, 8, 1024], BF16)
            self.load_w(Wo, d["w_out"][l].rearrange("(c p) n -> p c n", p=128), d["w_out"])
            ys = [sb("yD%d" % i, [128, 8, 128], F32) for i in range(2)]
            eps5 = sb("eps5", [128, 1], F32)
            kb.op("dve", lambda: nc.vector.memset(eps5[:], 1e-5), writes=[eps5])
            lnb = [(sb("ybf%d" % i, [128, 8, 128], BF16), sb("ysq%d" % i, [128, 8, 128], BF16), sb("mean%d" % i, [128, 128], F32),
                    sb("m2%d" % i, [128, 128], F32), sb("var%d" % i, [128, 128], F32), eps5) for i in range(2)]
            for n in range(NT):
                y = ys[n % 2]
                for half in range(2):
                    bo = self.bank()
                    for c4 in range(4):
                        c = half * 4 + c4
                        for k in range(8):
                            self.mm(bo[:, c4 * 128:(c4 + 1) * 128], Wo[:, k, c * 128:(c + 1) * 128], self.merged[n][:, k, :], k == 0, k == 7, [Wo, self.merged[n]], bo)
                    self.stt(y[:, half * 4:(half + 1) * 4, :], self.xT[n][:, half * 4:(half + 1) * 4, :], ALPHA,
                             bo[:].rearrange("p (a b) -> p a b", a=4), ALU.mult, ALU.add, [self.xT[n], bo], [y])
                self.ln_fm(y, n, VF_L1G, VF_L1B, lnb[n % 2])
                if "x1" in self.dbg:
                    self.dump("x1_%d_%d" % (l, n), self.xT[n], self.xT[n][:], [128, 8, 128], F32)
            kb.barrier()

    def phase_peer(self, l):
        kb, nc, d = self.kb, self.nc, self.d
        NTP, NT = self.NTP, self.NT
        kb.barrier()
        with ExitStack() as es:
            sb = lambda n, s, dt: kb.sb(n, s, dt, es)
            Wpq = sb("Wpq", [128, 8, 2048], BF16)
            skT = sb("skT", [128, 16, 128], BF16)
            self.load_w(Wpq, d["pq"][l].rearrange("(c p) n -> p c n", p=128), d["pq"])
            self.load_w(skT, d["skT"][l], d["skT"])
            pu, pv = d["pu%d" % l], d["pv%d" % l]
            xbp = [sb("xbE%d" % i, [128, 8, 128], BF16) for i in range(2)]
            self.xb_i = 0
            xtok = sb("xtok", [128, 1024], F32)
            qT = sb("qTE", [128, 16, 128], BF16)
            ssb = sb("ssb", [128, 16, 128], F32)
            s2 = sb("s2", [128, 128], F32)
            sv = sb("sv", [128, 16, 16], F32)
            si = sb("si", [128, 16, 16], U32)
            sif = sb("sif", [128, 16, 16], F32)
            cand = sb("cand", [128, 8, 256], F32)
            cand2 = sb("cand2", [128, 256], F32)
            cs = sb("cs", [128, 8, 16], F32)
            ci = sb("ci", [128, 8, 16], U32)
            ciu = sb("ciu", [128, 8, 16], U32)
            af = sb("af", [128, 8, 16], F32)
            bf = sb("bf", [128, 8, 16], F32)
            eq = sb("eq", [128, 8, 16, 16], F32)
            i0 = sb("i0", [128, 8, 16], F32)
            i1 = sb("i1", [128, 8, 16], F32)
            eidf = sb("eidf", [128, 128], F32)
            eidx = sb("eidx", [128, 128], I32)
            gex = sb("gex", [128, 8, 16], F32)
            gsum = sb("gsum", [128, 8], F32)
            gg = sb("gg", [128, 128], F32)
            hraw = sb("hraw", [128, 128], F32)
            hw = sb("hw", [128, 128], F32)
            NG = 10
            gb = [sb("gbuf%d" % i, [128, 1024], BF16) for i in range(NG)]
            NPB = 4
            prods = [sb("prod%d" % i, [128, 1024], BF16) for i in range(NPB)]
            dgs = [sb("dg%d" % i, [128, 128], BF16) for i in range(4)]
            xtokb = sb("xtokb", [128, 1024], BF16)
            junkb = sb("junkb", [128, 1024], BF16)
            junk = sb("junk", [128, 1024], F32)
            acc = sb("acc", [128, 1024], F32)
            st = sb("st", [128, 8], F32)
            iota16 = self.misc[:, 2, 0:16]
            gi = 0
            for n in range(NT):
                xb = self.cast_x(n, xbp)
                for half in range(2):
                    bt = self.bank()
                    for c4 in range(4):
                        c = half * 4 + c4
                        kb.op("pe", lambda: nc.tensor.transpose(bt[:, c4 * 128:(c4 + 1) * 128], self.xT[n][:, c, :], self.ident), reads=[self.xT[n], self.misc], writes=[bt])
                    self.cp("act", xtok[:, half * 512:(half + 1) * 512], bt[:], [bt], [xtok])
                for q4 in range(4):
                    bq = self.bank()
                    for hh in range(4):
                        hp = q4 * 4 + hh
                        for dc in range(8):
                            self.mm(bq[:, hh * 128:(hh + 1) * 128], Wpq[:, dc, hp * 128:(hp + 1) * 128], xb[:, dc, :], dc == 0, dc == 7, [Wpq, xb], bq)
                    self.cp("act", qT[:, q4 * 4:(q4 + 1) * 4, :], bq[:].rearrange("p (a b) -> p a b", a=4), [bq], [qT])
                for q4 in range(4):
                    bs = self.bank()
                    for hh in range(4):
                        hp = q4 * 4 + hh
                        self.mm(bs[:, hh * 128:(hh + 1) * 128], qT[:, hp, :], skT[:, hp, :], True, True, [qT, skT], bs)
                    self.cp("act", ssb[:, q4 * 4:(q4 + 1) * 4, :], bs[:].rearrange("p (a b) -> p a b", a=4), [bs], [ssb])
                for hp in range(16):
                    kb.op("dve", lambda: nc.vector.max(out=sv[:, hp, 0:8], in_=ssb[:, hp, :]), reads=[ssb], writes=[sv])
                    kb.op("dve", lambda: nc.vector.max_index(out=si[:, hp, 0:8], in_max=sv[:, hp, 0:8], in_values=ssb[:, hp, :]), reads=[ssb, sv], writes=[si])
                    kb.op("dve", lambda: nc.vector.match_replace(out=s2[:], in_to_replace=sv[:, hp, 0:8], in_values=ssb[:, hp, :], imm_value=-1e30), reads=[ssb, sv], writes=[s2])
                    kb.op("dve", lambda: nc.vector.max(out=sv[:, hp, 8:16], in_=s2[:]), reads=[s2], writes=[sv])
                    kb.op("dve", lambda: nc.vector.max_index(out=si[:, hp, 8:16], in_max=sv[:, hp, 8:16], in_values=s2[:]), reads=[s2, sv], writes=[si])
                self.cp("dve", sif[:], si[:], [si], [sif])
                sv4 = sv[:].rearrange("p (h q) k -> p h q k", q=2)
                sif4 = sif[:].rearrange("p (h q) k -> p h q k", q=2)
                cand4 = cand[:].rearrange("p h (a b) -> p h a b", a=16)
                self.tt("dve", cand4, sv4[:, :, 0, :].unsqueeze(3).to_broadcast([128, 8, 16, 16]),
                        sv4[:, :, 1, :].unsqueeze(2).to_broadcast([128, 8, 16, 16]), ALU.add, [sv], [cand])
                for h in range(8):
                    kb.op("dve", lambda: nc.vector.max(out=cs[:, h, 0:8], in_=cand[:, h, :]), reads=[cand], writes=[cs])
                    kb.op("dve", lambda: nc.vector.max_index(out=ci[:, h, 0:8], in_max=cs[:, h, 0:8], in_values=cand[:, h, :]), reads=[cand, cs], writes=[ci])
                    kb.op("dve", lambda: nc.vector.match_replace(out=cand2[:], in_to_replace=cs[:, h, 0:8], in_values=cand[:, h, :], imm_value=-1e30), reads=[cand, cs], writes=[cand2])
                    kb.op("dve", lambda: nc.vector.max(out=cs[:, h, 8:16], in_=cand2[:]), reads=[cand2], writes=[cs])
                    kb.op("dve", lambda: nc.vector.max_index(out=ci[:, h, 8:16], in_max=cs[:, h, 8:16], in_values=cand2[:]), reads=[cand2, cs], writes=[ci])
                kb.op("dve", lambda: nc.vector.tensor_single_scalar(out=ciu[:], in_=ci[:], scalar=4, op=ALU.logical_shift_right), reads=[ci], writes=[ciu])
                self.cp("dve", af[:], ciu[:], [ciu], [af])
                kb.op("dve", lambda: nc.vector.tensor_single_scalar(out=ciu[:], in_=ci[:], scalar=15, op=ALU.bitwise_and), reads=[ci], writes=[ciu])
                self.cp("dve", bf[:], ciu[:], [ciu], [bf])
                io4 = iota16.unsqueeze(1).unsqueeze(1).to_broadcast([128, 8, 16, 16])
                for (sel, q, dst) in ((af, 0, i0), (bf, 1, i1)):
                    self.tt("dve", eq[:], sel[:].unsqueeze(3).to_broadcast([128, 8, 16, 16]), io4, ALU.is_equal, [sel, self.misc], [eq])
                    self.tt("dve", eq[:], eq[:], sif4[:, :, q, :].unsqueeze(2).to_broadcast([128, 8, 16, 16]), ALU.mult, [eq, sif], [eq])
                    kb.op("dve", lambda: nc.vector.tensor_reduce(out=dst[:], in_=eq[:], axis=AX.X, op=ALU.add), reads=[eq], writes=[dst])
                self.stt(eidf[:].rearrange("p (h k) -> p h k", h=8), i0[:], 128.0, i1[:], ALU.mult, ALU.add, [i0, i1], [eidf])
                self.ts("dve", eidf[:], eidf[:], 0.0, 16383.0, ALU.max, ALU.min, [eidf], [eidf])
                self.cp("dve", eidx[:], eidf[:], [eidf], [eidx])
                self.tt("dve", gex[:], cs[:], cs[:, :, 0:1].to_broadcast([128, 8, 16]), ALU.subtract, [cs], [gex])
                self.act(gex[:], gex[:], ACT.Exp, [gex], [gex])
                kb.op("dve", lambda: nc.vector.tensor_reduce(out=gsum[:], in_=gex[:], axis=AX.X, op=ALU.add), reads=[gex], writes=[gsum])
                kb.op("dve", lambda: nc.vector.reciprocal(out=gsum[:], in_=gsum[:]), reads=[gsum], writes=[gsum])
                self.tt("dve", gg[:].rearrange("p (h k) -> p h k", h=8), gex[:], gsum[:].unsqueeze(2).to_broadcast([128, 8, 16]), ALU.mult, [gex, gsum], [gg])
                if "eidx" in self.dbg:
                    self.dump("eidx_%d_%d" % (l, n), eidx, eidx[:], [128, 128], I32)
                    self.dump("gg_%d_%d" % (l, n), gg, gg[:], [128, 128], F32)
                self.cp("act", xtokb[:], xtok[:], [xtok], [xtokb])
                for s in range(128):
                    g_ = gb[gi % NG]
                    pr = prods[gi % NPB]
                    gi += 1
                    kb.dma("pool", lambda: nc.gpsimd.indirect_dma_start(out=g_[:], out_offset=None, in_=pu[:], in_offset=bass.IndirectOffsetOnAxis(ap=eidx[:, s:s + 1], axis=0)),
                           reads=[eidx, pu], writes=[g_])
                    self.tt("dve", pr[:], g_[:], xtokb[:], ALU.mult, [g_, xtokb], [pr])
                    kb.op("act", lambda: nc.scalar.activation(out=junkb[:], in_=pr[:], func=ACT.Copy, accum_out=hraw[:, s:s + 1]), reads=[pr], writes=[junkb, hraw])
                self.act(hw[:], hraw[:], ACT.Gelu, [hraw], [hw])
                self.tt("dve", hw[:], hw[:], gg[:], ALU.mult, [hw, gg], [hw])
                bacc = [self.bank(), self.bank()]
                for s in range(128):
                    g_ = gb[gi % NG]
                    dg = dgs[gi % 4]
                    gi += 1
                    kb.dma("pool", lambda: nc.gpsimd.indirect_dma_start(out=g_[:], out_offset=None, in_=pv[:], in_offset=bass.IndirectOffsetOnAxis(ap=eidx[:, s:s + 1], axis=0)),
                           reads=[eidx, pv], writes=[g_])
                    kb.op("act", lambda: nc.scalar.activation(out=dg[:], in_=self.cbf[:, 0, :], func=ACT.Copy, scale=hw[:, s:s + 1]), reads=[self.cbf, hw], writes=[dg])
                    for half in range(2):
                        self.mm(bacc[half][:], dg[:], g_[:, half * 512:(half + 1) * 512], s == 0, s == 127, [dg, g_], bacc[half])
                for half in range(2):
                    self.stt(acc[:, half * 512:(half + 1) * 512], xtok[:, half * 512:(half + 1) * 512], ALPHA, bacc[half][:], ALU.mult, ALU.add, [xtok, bacc[half]], [acc])
                kb.op("dve", lambda: nc.vector.tensor_reduce(out=st[:, 0:1], in_=acc[:], axis=AX.X, op=ALU.add), reads=[acc], writes=[st])
                self.stt(junk[:], acc[:], 1.0, acc[:], ALU.mult, ALU.mult, [acc], [junk, st], accum_out=st[:, 1:2])
                self.ts("dve", st[:, 2:3], st[:, 0:1], 1.0 / 1024, None, ALU.mult, None, [st], [st])
                self.tt("dve", st[:, 3:4], st[:, 2:3], st[:, 2:3], ALU.mult, [st], [st])
                self.stt(st[:, 4:5], st[:, 1:2], 1.0 / 1024, st[:, 3:4], ALU.mult, ALU.subtract, [st], [st])
                self.ts("dve", st[:, 4:5], st[:, 4:5], 1e-5, None, ALU.add, None, [st], [st])
                self.act(st[:, 5:6], st[:, 4:5], ACT.Sqrt, [st], [st])
                kb.op("dve", lambda: nc.vector.reciprocal(out=st[:, 6:7], in_=st[:, 5:6]), reads=[st], writes=[st])
                self.ts("dve", acc[:], acc[:], st[:, 2:3], st[:, 6:7], ALU.subtract, ALU.mult, [acc, st], [acc])
                for half in range(2):
                    bt = self.bank()
                    for c4 in range(4):
                        c = half * 4 + c4
                        kb.op("pe", lambda: nc.tensor.transpose(bt[:, c4 * 128:(c4 + 1) * 128], acc[:, c * 128:(c + 1) * 128], self.ident), reads=[acc, self.misc], writes=[bt])
                    for c4 in range(4):
                        c = half * 4 + c4
                        self.ts("dve", self.xT[n][:, c, :], bt[:, c4 * 128:(c4 + 1) * 128], self.vecF[:, VF_L2G + c:VF_L2G + c + 1],
                                self.vecF[:, VF_L2B + c:VF_L2B + c + 1], ALU.mult, ALU.add, [bt, self.vecF], [self.xT[n]])
            kb.barrier()


    def phase_peer_dense(self, l):
        kb, nc, d = self.kb, self.nc, self.d
        NTP, NT, T = self.NTP, self.NT, self.T
        kb.barrier()
        puT, pvp, ub16, vb16 = d["puT%d" % l], d["pvp%d" % l], d["ub16_%d" % l], d["vb16_%d" % l]
        with ExitStack() as oes:
            idxT = kb.sb("idxT", [128, T, 2], F32, oes)
            gT = kb.sb("gT", [128, T], F32, oes)
            with ExitStack() as es:
                sb = lambda n, s, dt: kb.sb(n, s, dt, es)
                Wpq = sb("Wpq", [128, 8, 2048], BF16)
                skT = sb("skT", [128, 16, 128], BF16)
                self.load_w(Wpq, d["pq"][l].rearrange("(c p) n -> p c n", p=128), d["pq"])
                self.load_w(skT, d["skT"][l], d["skT"])
                for c in range(16):
                    kb.dma("pool", lambda: nc.gpsimd.dma_start(out=ub16[c * 8:(c + 1) * 8], in_=puT[c * 8:(c + 1) * 8]), reads=[puT], writes=[ub16])
                    kb.dma("pool", lambda: nc.gpsimd.dma_start(out=vb16[c * 8:(c + 1) * 8], in_=pvp[c * 8:(c + 1) * 8]), reads=[pvp], writes=[vb16])
                xbp = [sb("xbE%d" % i, [128, 8, 128], BF16) for i in range(2)]
                self.xb_i = 0
                qT = sb("qTE", [128, 16, 128], BF16)
                ssb = sb("ssb", [128, 16, 128], F32)
                s2 = sb("s2", [128, 128], F32)
                sv = sb("sv", [128, 16, 16], F32)
                si = sb("si", [128, 16, 16], U32)
                sif = sb("sif", [128, 16, 16], F32)
                cand = sb("cand", [128, 8, 256], F32)
                cand2 = sb("cand2", [128, 256], F32)
                cs = sb("cs", [128, 8, 16], F32)
                ci = sb("ci", [128, 8, 16], U32)
                ciu = sb("ciu", [128, 8, 16], U32)
                af = sb("af", [128, 8, 16], F32)
                bf = sb("bf", [128, 8, 16], F32)
                eq = sb("eq", [128, 8, 16, 16], F32)
                i0 = sb("i0", [128, 8, 16], F32)
                i1 = sb("i1", [128, 8, 16], F32)
                gex = sb("gex", [128, 8, 16], F32)
                gsum = sb("gsum", [128, 8], F32)
                gg = sb("gg", [128, 128], F32)
                iota16 = self.misc[:, 2, 0:16]

                def split(name, ngrp, width, dt):
                    t = es.enter_context(nc.sbuf_tensor("%s_%d" % (name, l), [128, ngrp, width], dt))
                    return t, [Buf(t[:, g_, :], "%s%d" % (name, g_)) for g_ in range(ngrp)]
                svAt, svA = split("svA", 16, 8, F32)
                svBt, svB = split("svB", 16, 8, F32)
                siAt, siA = split("siA", 16, 8, U32)
                siBt, siB = split("siB", 16, 8, U32)
                s2t, s2s = split("s2s", 16, 128, F32)
                csAt, csA = split("csA", 8, 8, F32)
                csBt, csB = split("csB", 8, 8, F32)
                ciAt, ciA = split("ciA", 8, 8, U32)
                ciBt, ciB = split("ciB", 8, 8, U32)
                c2t, c2s = split("c2s", 8, 256, F32)
                ssbs = [ssb, sb("ssb2", [128, 16, 128], F32)]

                def E1a(n):
                    ssb = ssbs[n % 2]
                    xb = self.cast_x(n, xbp)
                    for q4 in range(4):
                        bq = self.bank()
                        for hh in range(4):
                            hp = q4 * 4 + hh
                            for dc in range(8):
                                self.mm(bq[:, hh * 128:(hh + 1) * 128], Wpq[:, dc, hp * 128:(hp + 1) * 128], xb[:, dc, :], dc == 0, dc == 7, [Wpq, xb], bq)
                        self.cp("act", qT[:, q4 * 4:(q4 + 1) * 4, :], bq[:].rearrange("p (a b) -> p a b", a=4), [bq], [qT])
                    for q4 in range(4):
                        bs = self.bank()
                        for hh in range(4):
                            hp = q4 * 4 + hh
                            self.mm(bs[:, hh * 128:(hh + 1) * 128], qT[:, hp, :], skT[:, hp, :], True, True, [qT, skT], bs)
                        self.cp("act", ssb[:, q4 * 4:(q4 + 1) * 4, :], bs[:].rearrange("p (a b) -> p a b", a=4), [bs], [ssb])

                E1a(0)
                for n in range(NT):
                    if n + 1 < NT:
                        E1a(n + 1)
                    ssb = ssbs[n % 2]
                    for hp in range(16):
                        kb.op("dve", lambda: nc.vector.max(out=svA[hp][:], in_=ssb[:, hp, :]), reads=[ssb], writes=[svA[hp]])
                    for hp in range(16):
                        kb.op("dve", lambda: nc.vector.max_index(out=siA[hp][:], in_max=svA[hp][:], in_values=ssb[:, hp, :]), reads=[ssb, svA[hp]], writes=[siA[hp]])
                    for hp in range(16):
                        kb.op("dve", lambda: nc.vector.match_replace(out=s2s[hp][:], in_to_replace=svA[hp][:], in_values=ssb[:, hp, :], imm_value=-1e30), reads=[ssb, svA[hp]], writes=[s2s[hp]])
                    for hp in range(16):
                        kb.op("dve", lambda: nc.vector.max(out=svB[hp][:], in_=s2s[hp][:]), reads=[s2s[hp]], writes=[svB[hp]])
                    for hp in range(16):
                        kb.op("dve", lambda: nc.vector.max_index(out=siB[hp][:], in_max=svB[hp][:], in_values=s2s[hp][:]), reads=[s2s[hp], svB[hp]], writes=[siB[hp]])
                    svv = sv[:].rearrange("p a (q k) -> p a q k", q=2)
                    siv = si[:].rearrange("p a (q k) -> p a q k", q=2)
                    self.cp("dve", svv[:, :, 0, :], svAt[:], svA, [sv])
                    self.cp("dve", svv[:, :, 1, :], svBt[:], svB, [sv])
                    self.cp("dve", siv[:, :, 0, :], siAt[:], siA, [si])
                    self.cp("dve", siv[:, :, 1, :], siBt[:], siB, [si])
                    self.cp("dve", sif[:], si[:], [si], [sif])
                    sv4 = sv[:].rearrange("p (h q) k -> p h q k", q=2)
                    sif4 = sif[:].rearrange("p (h q) k -> p h q k", q=2)
                    cand4 = cand[:].rearrange("p h (a b) -> p h a b", a=16)
                    self.tt("dve", cand4, sv4[:, :, 0, :].unsqueeze(3).to_broadcast([128, 8, 16, 16]),
                            sv4[:, :, 1, :].unsqueeze(2).to_broadcast([128, 8, 16, 16]), ALU.add, [sv], [cand])
                    for h in range(8):
                        kb.op("dve", lambda: nc.vector.max(out=csA[h][:], in_=cand[:, h, :]), reads=[cand], writes=[csA[h]])
                    for h in range(8):
                        kb.op("dve", lambda: nc.vector.max_index(out=ciA[h][:], in_max=csA[h][:], in_values=cand[:, h, :]), reads=[cand, csA[h]], writes=[ciA[h]])
                    for h in range(8):
                        kb.op("dve", lambda: nc.vector.match_replace(out=c2s[h][:], in_to_replace=csA[h][:], in_values=cand[:, h, :], imm_value=-1e30), reads=[cand, csA[h]], writes=[c2s[h]])
                    for h in range(8):
                        kb.op("dve", lambda: nc.vector.max(out=csB[h][:], in_=c2s[h][:]), reads=[c2s[h]], writes=[csB[h]])
                    for h in range(8):
                        kb.op("dve", lambda: nc.vector.max_index(out=ciB[h][:], in_max=csB[h][:], in_values=c2s[h][:]), reads=[c2s[h], csB[h]], writes=[ciB[h]])
                    csv = cs[:].rearrange("p a (q k) -> p a q k", q=2)
                    civ = ci[:].rearrange("p a (q k) -> p a q k", q=2)
                    self.cp("dve", csv[:, :, 0, :], csAt[:], csA, [cs])
                    self.cp("dve", csv[:, :, 1, :], csBt[:], csB, [cs])
                    self.cp("dve", civ[:, :, 0, :], ciAt[:], ciA, [ci])
                    self.cp("dve", civ[:, :, 1, :], ciBt[:], ciB, [ci])
                    kb.op("dve", lambda: nc.vector.tensor_single_scalar(out=ciu[:], in_=ci[:], scalar=4, op=ALU.logical_shift_right), reads=[ci], writes=[ciu])
                    self.cp("dve", af[:], ciu[:], [ciu], [af])
                    kb.op("dve", lambda: nc.vector.tensor_single_scalar(out=ciu[:], in_=ci[:], scalar=15, op=ALU.bitwise_and), reads=[ci], writes=[ciu])
                    self.cp("dve", bf[:], ciu[:], [ciu], [bf])
                    io4 = iota16.unsqueeze(1).unsqueeze(1).to_broadcast([128, 8, 16, 16])
                    for (sel, q, dst) in ((af, 0, i0), (bf, 1, i1)):
                        self.tt("dve", eq[:], sel[:].unsqueeze(3).to_broadcast([128, 8, 16, 16]), io4, ALU.is_equal, [sel, self.misc], [eq])
                        self.tt("dve", eq[:], eq[:], sif4[:, :, q, :].unsqueeze(2).to_broadcast([128, 8, 16, 16]), ALU.mult, [eq, sif], [eq])
                        kb.op("dve", lambda: nc.vector.tensor_reduce(out=dst[:], in_=eq[:], axis=AX.X, op=ALU.add), reads=[eq], writes=[dst])
                    self.tt("dve", gex[:], cs[:], cs[:, :, 0:1].to_broadcast([128, 8, 16]), ALU.subtract, [cs], [gex])
                    self.act(gex[:], gex[:], ACT.Exp, [gex], [gex])
                    kb.op("dve", lambda: nc.vector.tensor_reduce(out=gsum[:], in_=gex[:], axis=AX.X, op=ALU.add), reads=[gex], writes=[gsum])
                    kb.op("dve", lambda: nc.vector.reciprocal(out=gsum[:], in_=gsum[:]), reads=[gsum], writes=[gsum])
                    self.tt("dve", gg[:].rearrange("p (h k) -> p h k", h=8), gex[:], gsum[:].unsqueeze(2).to_broadcast([128, 8, 16]), ALU.mult, [gex, gsum], [gg])
                    bt = self.bank()
                    for k_, (src, sbuf_) in enumerate(((i0, i0), (i1, i1), (gg, gg))):
                        src_ap = src[:].rearrange("p h k -> p (h k)") if k_ < 2 else src[:]
                        kb.op("pe", lambda: nc.tensor.transpose(bt[:, k_ * 128:(k_ + 1) * 128], src_ap, self.ident), reads=[sbuf_, self.misc], writes=[bt])
                    self.cp("act", idxT[:, n * 128:(n + 1) * 128, 0], bt[:, 0:128], [bt], [idxT])
                    self.cp("act", idxT[:, n * 128:(n + 1) * 128, 1], bt[:, 128:256], [bt], [idxT])
                    self.cp("act", gT[:, n * 128:(n + 1) * 128], bt[:, 256:384], [bt], [gT])
                kb.barrier()
            with ExitStack() as es:
                sb = lambda n, s, dt: kb.sb(n, s, dt, es)
                G = sb("Gsb", [128, 256, 128], BF16)
                xbb = sb("xbb", [128, 8, 256], BF16)
                NR = 4
                ubl = [sb("ubl%d" % i, [128, 8, 128], BF16) for i in range(NR)]
                vbl = [sb("vbl%d" % i, [128, 1024], BF16) for i in range(NR)]
                hgs = [sb("hg%d" % i, [128, 256], BF16) for i in range(NR)]
                Ws = [sb("Wd%d" % i, [128, 256], BF16) for i in range(NR)]
                Ag = [sb("Ag%d" % i, [128, 128], BF16) for i in range(4)]
                Bg = [sb("Bg%d" % i, [128, 128], BF16) for i in range(4)]
                ys = [sb("yE%d" % i, [128, 8, 128], F32) for i in range(2)]
                eps5 = sb("eps5E", [128, 1], F32)
                kb.op("dve", lambda: nc.vector.memset(eps5[:], 1e-5), writes=[eps5])
                lnb = [(sb("ybfE%d" % i, [128, 8, 128], BF16), sb("ysqE%d" % i, [128, 8, 128], BF16), sb("meanE%d" % i, [128, 128], F32),
                        sb("m2E%d" % i, [128, 128], F32), sb("varE%d" % i, [128, 128], F32), eps5) for i in range(2)]
                iota_b = sb("iota_b", [128, 128], BF16)
                self.cp("dve", iota_b[:], self.misc[:, 3, :], [self.misc], [iota_b])
                iota128 = iota_b[:]
                accb = self.banks[0:4]
                bhs = self.banks[4:6]
                bgs = self.banks[6:8]
                blocks = [(2 * k, 2) for k in range(NTP // 2)] + [(NTP, 1)]
                ri = 0
                yi = 0
                for (n0, ntl) in blocks:
                    nt = ntl * 128
                    t0 = n0 * 128
                    for k in range(ntl):
                        self.cp("dve", xbb[:, :, k * 128:(k + 1) * 128], self.xT[n0 + k][:], [self.xT[n0 + k]], [xbb])
                    sid_g, _ = nc.enter_named_scope("E2g%d" % l, False)
                    for tq in range(nt // 4):
                        bg = bgs[tq % 2]
                        for k in range(4):
                            tl = tq * 4 + k
                            tok = t0 + tl
                            A_ = Ag[tl % 4]
                            B_ = Bg[tl % 4]
                            self.ts("dve", A_[:], iota128, idxT[:, tok, 0:1], gT[:, tok:tok + 1], ALU.is_equal, ALU.mult, [iota_b, idxT, gT], [A_])
                            self.ts("dve", B_[:], iota128, idxT[:, tok, 1:2], None, ALU.is_equal, None, [iota_b, idxT], [B_])
                            self.mm(bg[:, k * 128:(k + 1) * 128], A_[:], B_[:], True, True, [A_, B_], bg)
                        self.cp("act", G[:, tq * 4:(tq + 1) * 4, :], bg[:].rearrange("p (a b) -> p a b", a=4), [bg], [G])
                    nc.leave_named_scope("E2g%d" % l, sid_g, False)
                    sid_m, _ = nc.enter_named_scope("E2m%d" % l, False)
                    def stage1(i1_):
                        r = ri + i1_
                        ub, vb, hg, W, bh = ubl[r % NR], vbl[r % NR], hgs[r % NR], Ws[r % NR], bhs[r % 2]
                        self.ld("sp", ub[:].rearrange("p c n -> p (c n)"), ub16[i1_], [ub16], [ub])
                        self.ld("sp", vb[:], vb16[i1_], [vb16], [vb])
                        for dc in range(8):
                            self.mm(bh[:, 0:nt], ub[:, dc, :], xbb[:, dc, 0:nt], dc == 0, dc == 7, [ub, xbb], bh)
                        self.act(hg[:, 0:nt], bh[:, 0:nt], ACT.Gelu, [bh], [hg])
                        self.tt("dve", W[:, 0:nt], hg[:, 0:nt], G[:, 0:nt, i1_], ALU.mult, [hg, G], [W])

                    def stage2(i1_):
                        r = ri + i1_
                        vb, W = vbl[r % NR], Ws[r % NR]
                        for c in range(8):
                            ab = accb[c // 2]
                            o = ab[:, (c % 2) * 256:(c % 2) * 256 + nt]
                            self.mm(o, vb[:, c * 128:(c + 1) * 128], W[:, 0:nt], (i1_ == 0 and c % 2 == 0), i1_ == 127, [vb, W], ab)

                    for i1_ in range(128):
                        stage1(i1_)
                        if i1_ >= 1:
                            stage2(i1_ - 1)
                    stage2(127)
                    ri += 128
                    nc.leave_named_scope("E2m%d" % l, sid_m, False)
                    for k in range(ntl):
                        n = n0 + k
                        y = ys[yi % 2]
                        lb = lnb[yi % 2]
                        yi += 1
                        for c in range(8):
                            ab = accb[c // 2]
                            o = ab[:, (c % 2) * 256 + k * 128:(c % 2) * 256 + (k + 1) * 128]
                            self.stt(y[:, c, :], self.xT[n][:, c, :], ALPHA, o, ALU.mult, ALU.add, [self.xT[n], ab], [y])
                        self.ln_fm(y, n, VF_L2G, VF_L2B, lb, bank=bhs[yi % 2])
                kb.barrier()


_CACHE = {}


def host_weights(inp, DEPTH, peer_mode="dense"):
    w = {}
    f = lambda a: np.ascontiguousarray(np.asarray(a, dtype=np.float32))
    w["w_in"] = f(inp["w_in"][:DEPTH])
    b_in = f(inp["b_in"][:DEPTH])
    w["b_in"] = b_in
    vec = np.zeros((DEPTH, 128, VF_N), np.float32)
    for l in range(DEPTH):
        b = b_in[l]
        vec[l, :64, VF_BQ:VF_BQ + 16] = b[C_QA:C_KA].reshape(16, 64).T
        vec[l, :64, VF_BQ + 16:VF_BQ + 18] = b[C_KA:C_VA].reshape(2, 64).T
        vec[l, :, VF_BQB:VF_BQB + 4] = b[C_QB:C_KB].reshape(4, 128).T
        vec[l, :, VF_BKB:VF_BKB + 4] = b[C_KB:C_VB].reshape(4, 128).T
        vec[l, :, VF_BGB:VF_BGB + 8] = b[C_GB:C_UC].reshape(8, 128).T
        vec[l, :, VF_BGT:VF_BGT + 24] = b[C_GT:INW].reshape(24, 128).T
        vec[l, :16, VF_BLR] = b[C_LR:C_GB]
        vec[l, :, VF_GNG:VF_GNG + 8] = np.asarray(inp["gla_norm_g"][l], np.float32).reshape(8, 128).T
        vec[l, :, VF_PSC:VF_PSC + 8] = np.asarray(inp["pool_scale"][l], np.float32).reshape(8, 128).T
        vec[l, :, VF_L1G:VF_L1G + 8] = np.asarray(inp["ln1_g"][l], np.float32).reshape(8, 128).T
        vec[l, :, VF_L1B:VF_L1B + 8] = np.asarray(inp["ln1_b"][l], np.float32).reshape(8, 128).T
        vec[l, :, VF_L2G:VF_L2G + 8] = np.asarray(inp["ln2_g"][l], np.float32).reshape(8, 128).T
        vec[l, :, VF_L2B:VF_L2B + 8] = np.asarray(inp["ln2_b"][l], np.float32).reshape(8, 128).T
    w["vecF"] = vec
    w["sinks"] = f(inp["attn_sinks"][:DEPTH])
    w["w_alpha"] = f(inp["w_alpha"][:DEPTH])
    w["b_alpha"] = f(inp["b_alpha"][:DEPTH])
    w["w_pool"] = f(inp["w_pool"][:DEPTH])
    w["w_ba"] = f(inp["w_branch_a"][:DEPTH])
    w["w_bb"] = f(inp["w_branch_b"][:DEPTH])
    w["w_bc"] = f(inp["w_branch_c"][:DEPTH])
    w["w_out"] = f(inp["w_out"][:DEPTH])
    w["pq"] = f(np.asarray(inp["peer_query"][:DEPTH]).reshape(DEPTH, 1024, 2048))
    sk = np.asarray(inp["peer_subkeys"][:DEPTH], np.float32).reshape(DEPTH, 16, 128, 128)
    w["skT"] = f(sk.transpose(0, 3, 1, 2))
    for l in range(DEPTH):
        if peer_mode == "dense":
            u = np.asarray(inp["peer_u"][l], np.float32).reshape(128, 128, 8, 128)
            w["puT%d" % l] = f(u.transpose(1, 3, 2, 0).reshape(128, 128, 1024))
            v = np.asarray(inp["peer_v"][l], np.float32).reshape(128, 128, 1024)
            w["pvp%d" % l] = f(v.transpose(1, 0, 2))
        else:
            w["pu%d" % l] = f(inp["peer_u"][l])
            w["pv%d" % l] = f(inp["peer_v"][l])
    return w


def core_inputs(inp, i, NTP, DEPTH):
    f = lambda a: np.ascontiguousarray(np.asarray(a, dtype=np.float32))
    T = NTP * 128 + 128
    xp = np.asarray(inp["x_prompt"][i, :NTP * 128], np.float32)
    xs = np.asarray(inp["x_sample"][16 * i:16 * i + 16], np.float32).reshape(128, D)
    x = np.concatenate([xp, xs], 0)
    m = {}
    m["xT"] = f(x.reshape(T, 8, 128).transpose(2, 1, 0))
    sl = slice(16 * i, 16 * i + 16)
    swk = np.asarray(inp["state_win_k"][:DEPTH, sl], np.float32)
    m["swkT"] = f(swk.transpose(0, 4, 1, 3, 2))
    swv = np.asarray(inp["state_win_v"][:DEPTH, sl], np.float32)
    m["swv"] = f(swv.transpose(0, 2, 1, 3, 4))
    sg = np.asarray(inp["state_gla"][:DEPTH, sl], np.float32)
    m["sgla"] = f(sg.transpose(0, 2, 3, 1, 4))
    sp = np.asarray(inp["state_pool"][:DEPTH, sl], np.float32)
    m["spool"] = f(sp.reshape(DEPTH, 240, 1024))
    return m


PEER_MODE = "dense"


def get_prog(NTP, DEPTH, dbg=(), phases="ABCDE"):
    key = (NTP, DEPTH, tuple(dbg), phases, PEER_MODE)
    if key not in _CACHE:
        p = Prog(NTP, DEPTH, dbg)
        p.phases = phases
        p.peer_mode = PEER_MODE
        p.build()
        _CACHE[key] = p
    return _CACHE[key]


def run(inp, NTP=16, DEPTH=2, n_cores=8, dbg=(), phases="ABCDE"):
    p = get_prog(NTP, DEPTH, dbg, phases)
    consts = make_consts(NTP)
    w = host_weights(inp, DEPTH, PEER_MODE)
    in_maps = []
    for i in range(n_cores):
        m = dict(w)
        m.update(consts)
        m.update(core_inputs(inp, i, NTP, DEPTH))
        in_maps.append(m)
    res = run_bass_kernel_spmd(p.nc, in_maps, core_ids=list(range(n_cores)))
    return p, res.results


def assemble(results, NTP, DEPTH, n_cores):
    Tp = NTP * 128
    yp = np.zeros((n_cores, Tp, D), np.float32)
    ys = np.zeros((n_cores * 16, 8, D), np.float32)
    pk = np.zeros((DEPTH, n_cores, 128, 2, 64), np.float32)
    pv = np.zeros((DEPTH, n_cores, 128, 2, 64), np.float32)
    pg = np.zeros((DEPTH, n_cores, 4, 128, 256), np.float32)
    pp = np.zeros((DEPTH, n_cores, 15, 1024), np.float32)
    sk = np.zeros((DEPTH, n_cores * 16, 128, 2, 64), np.float32)
    sv = np.zeros((DEPTH, n_cores * 16, 128, 2, 64), np.float32)
    sg = np.zeros((DEPTH, n_cores * 16, 4, 128, 256), np.float32)
    sp = np.zeros((DEPTH, n_cores * 16, 15, 1024), np.float32)
    for i, r in enumerate(results):
        y = np.asarray(r["yT"]).transpose(2, 1, 0).reshape(-1, D)
        yp[i] = y[:Tp]
        ys[16 * i:16 * i + 16] = y[Tp:].reshape(16, 8, D)
        sl = slice(16 * i, 16 * i + 16)
        pk[:, i] = np.asarray(r["o_pkT"]).transpose(0, 3, 2, 1)
        pv[:, i] = np.asarray(r["o_pv"]).reshape(DEPTH, 128, 2, 64)
        pg[:, i] = np.asarray(r["o_pg"])
        pp[:, i] = np.asarray(r["o_pp"])
        sk[:, sl] = np.asarray(r["o_skT"]).transpose(0, 2, 4, 3, 1)
        sv[:, sl] = np.asarray(r["o_sv"]).reshape(DEPTH, 16, 128, 2, 64)
        sg[:, sl] = np.asarray(r["o_sg"]).transpose(0, 3, 1, 2, 4)
        sp[:, sl] = np.asarray(r["o_sp"])
    return (yp, ys, pk, pv, pg, pp, sk, sv, sg, sp)


def kernel(**inputs):
    NTP, DEPTH, NC = 16, 2, 8
    _, results = run(inputs, NTP, DEPTH, NC)
    return assemble(results, NTP, DEPTH, NC)
```
